# Optimizing a Trainium2 kernel written in Bass

```python
import math
import jax
import jax.numpy as jnp
from jax import lax
import numpy as np

D_MODEL = 1024
BATCH = 8
SEQ = 2048
DEPTH = 4
DEC_BATCH = 128
DEC_SEQ = 8
PAST_LEN = 8192
PAGE_SIZE = 128

N_A_LAYERS = DEPTH // 2
N_B_LAYERS = DEPTH - N_A_LAYERS
EXPAND = 2
D_INNER = EXPAND * D_MODEL
A_HEAD_K = 128
A_HEADS = D_INNER // A_HEAD_K
A_HEAD_V = D_INNER // A_HEADS
A_CHUNK = 32
B_HEAD_DIM = 64
B_HEADS = D_INNER // B_HEAD_DIM
B_KV_HEADS = B_HEADS // 8
B_GROUP = B_HEADS // B_KV_HEADS
WINDOW = 128
KV_WIDTH = 2 * B_KV_HEADS * B_HEAD_DIM
EPS = 1e-6
F32 = jnp.float32

kernel_name = 'yoco_hgrn2_swa_sink_step'


def rms_norm(x, g):
    xf = x.astype(F32)
    y = xf * lax.rsqrt(jnp.mean(xf * xf, axis=-1, keepdims=True) + EPS)
    return (y * g.astype(F32)).astype(x.dtype)


def hgrn_lower_bounds(p):
    lb = jnp.cumsum(jax.nn.softmax(p.astype(F32), axis=0), axis=0)
    return lb - lb[:1]


def hgrn2_recurrence(q, k, v, logf, s0):
    b, l, h, _ = q.shape
    dv = v.shape[-1]
    c = math.gcd(l, A_CHUNK)
    n = l // c

    def to_chunks(t):
        return t.reshape(b, n, c, h, t.shape[-1]).transpose(1, 0, 3, 2, 4)

    causal = jnp.tril(jnp.ones((c, c), dtype=bool))[None, None, :, :, None]

    def step(s, inp):
        qc, kc, vc, fc = inp
        bc = jnp.cumsum(fc, axis=2)
        o = jnp.einsum('bhtd,bhde->bhte', qc * jnp.exp(bc), s)
        rel = jnp.where(causal, bc[:, :, :, None, :] - bc[:, :, None, :, :], -jnp.inf)
        att = jnp.einsum('bhtsd,bhsd->bhts', qc[:, :, :, None, :] * jnp.exp(rel), kc)
        o = o + jnp.einsum('bhts,bhse->bhte', att, vc)
        b_last = bc[:, :, -1, :]
        s = jnp.exp(b_last)[..., None] * s + jnp.einsum(
            'bhsd,bhse->bhde', kc * jnp.exp(b_last[:, :, None, :] - bc), vc)
        return s, o

    s_fin, o = lax.scan(step, s0, (to_chunks(q), to_chunks(k), to_chunks(v), to_chunks(logf)))
    return o.transpose(1, 0, 3, 2, 4).reshape(b, l, h, dv), s_fin


def hgrn2_layer(x, s0, w_in, w_out, g_norm, g_onorm, lb):
    b, l, _ = x.shape
    q, fz, i, gate = jnp.split(rms_norm(x, g_norm) @ w_in, 4, axis=-1)
    fz = fz.astype(F32)
    lb = lb.astype(F32)
    logf = jnp.logaddexp(jnp.log(lb), jnp.log1p(-lb) + jax.nn.log_sigmoid(fz))
    k = (1.0 - lb) * jax.nn.sigmoid(-fz)
    q = jax.nn.silu(q.astype(F32)) * A_HEAD_K ** -0.5
    heads = lambda t: t.reshape(b, l, A_HEADS, -1)
    o, s_new = hgrn2_recurrence(heads(q), heads(k), heads(i.astype(F32)), heads(logf), s0.astype(F32))
    o = rms_norm(o, g_onorm).reshape(b, l, D_INNER) * jax.nn.silu(gate.astype(F32))
    return x + o.astype(x.dtype) @ w_out, s_new


def shared_kv(x, g_kv, w_kv):
    b, l, _ = x.shape
    return (rms_norm(x, g_kv) @ w_kv).reshape(b, l, 2, B_KV_HEADS, B_HEAD_DIM)


def sink_attend(q, k, v, mask, sinks):
    s = jnp.einsum('...tkgd,...skd->...kgts', q, k).astype(F32) * B_HEAD_DIM ** -0.5
    s = jnp.where(mask[..., None, None, :, :], s, -jnp.inf)
    sink = jnp.broadcast_to(sinks.astype(F32).reshape(B_KV_HEADS, B_GROUP, 1, 1), s.shape[:-1] + (1,))
    p = jax.nn.softmax(jnp.concatenate([s, sink], axis=-1), axis=-1)[..., :-1]
    return jnp.einsum('...kgts,...skd->...tkgd', p.astype(v.dtype), v)


def swa_prompt(q, kv, sinks):
    b, l = q.shape[:2]
    n = l // WINDOW
    qb = q.reshape(b, n, WINDOW, B_KV_HEADS, B_GROUP, B_HEAD_DIM)
    kvb = kv.reshape(b, n, WINDOW, 2, B_KV_HEADS, B_HEAD_DIM)
    prev = jnp.concatenate([jnp.zeros_like(kvb[:, :1]), kvb[:, :-1]], axis=1)
    kv2 = jnp.concatenate([prev, kvb], axis=2)
    t = jnp.arange(WINDOW)[:, None] + WINDOW
    s = jnp.arange(2 * WINDOW)[None, :]
    band = (t - s >= 0) & (t - s <= WINDOW)
    valid = (jnp.arange(n)[:, None] > 0) | (jnp.arange(2 * WINDOW)[None, :] >= WINDOW)
    mask = band[None] & valid[:, None, :]
    o = sink_attend(qb, kv2[:, :, :, 0], kv2[:, :, :, 1], mask, sinks)
    return o.reshape(b, l, D_INNER)


def swa_sample(q, kv_new, kv_buf, sinks):
    bd, l = q.shape[:2]
    wb = kv_buf.shape[1]
    kv_all = jnp.concatenate([kv_buf.astype(kv_new.dtype), kv_new], axis=1)
    q_pos = PAST_LEN + jnp.arange(l)
    k_pos = PAST_LEN - wb + jnp.arange(wb + l)
    d = q_pos[:, None] - k_pos[None, :]
    mask = (d >= 0) & (d <= WINDOW)
    o = sink_attend(q.reshape(bd, l, B_KV_HEADS, B_GROUP, B_HEAD_DIM), kv_all[:, :, 0], kv_all[:, :, 1], mask, sinks)
    return o.reshape(bd, l, D_INNER)


def swa_in(x, w_in, g_norm):
    b, l, _ = x.shape
    q, gate = jnp.split(rms_norm(x, g_norm) @ w_in, 2, axis=-1)
    return q.reshape(b, l, B_HEADS, B_HEAD_DIM), gate


def swa_out(x, o, gate, w_out):
    return x + (o.astype(F32) * jax.nn.silu(gate.astype(F32))).astype(x.dtype) @ w_out


def setup_inputs(seed: int = 0) -> dict:
    key = jax.random.key(seed)
    ks = jax.random.split(key, 17)
    nrm = lambda k, shape, s=1.0: s * jax.random.normal(k, shape, F32)
    wb = min(WINDOW, PAST_LEN)
    return {
        'x_prompt': nrm(ks[0], (BATCH, SEQ, D_MODEL)),
        'x_sample': nrm(ks[1], (DEC_BATCH, DEC_SEQ, D_MODEL)),
        'state_hgrn': nrm(ks[2], (N_A_LAYERS, DEC_BATCH, A_HEADS, A_HEAD_K, A_HEAD_V), 0.3),
        'cache_kv_window': nrm(ks[3], (DEC_BATCH, wb, 2, B_KV_HEADS, B_HEAD_DIM)),
        'w_in_a': nrm(ks[4], (N_A_LAYERS, D_MODEL, 4 * D_INNER), D_MODEL ** -0.5),
        'w_out_a': nrm(ks[5], (N_A_LAYERS, D_INNER, D_MODEL), D_INNER ** -0.5),
        'norm_a': 1.0 + nrm(ks[6], (N_A_LAYERS, D_MODEL), 0.02),
        'onorm_a': 1.0 + nrm(ks[7], (N_A_LAYERS, A_HEAD_V), 0.02),
        'lower_bounds_a': 1.0 + nrm(ks[8], (N_A_LAYERS, D_INNER), 0.5),
        'norm_kv': 1.0 + nrm(ks[9], (D_MODEL,), 0.02),
        'w_kv': nrm(ks[10], (D_MODEL, KV_WIDTH), D_MODEL ** -0.5),
        'w_in_b': nrm(ks[11], (N_B_LAYERS, D_MODEL, 2 * D_INNER), D_MODEL ** -0.5),
        'w_out_b': nrm(ks[12], (N_B_LAYERS, D_INNER, D_MODEL), D_INNER ** -0.5),
        'norm_b': 1.0 + nrm(ks[13], (N_B_LAYERS, D_MODEL), 0.02),
        'sinks_b': nrm(ks[14], (N_B_LAYERS, B_HEADS)),
        'norm_f': 1.0 + nrm(ks[15], (D_MODEL,), 0.02),
    }


def reference(x_prompt, x_sample, state_hgrn, cache_kv_window, w_in_a, w_out_a, norm_a, onorm_a,
              lower_bounds_a, norm_kv, w_kv, w_in_b, w_out_b, norm_b, sinks_b, norm_f):
    lb = hgrn_lower_bounds(lower_bounds_a)
    hp, hs = x_prompt, x_sample
    s_zero = jnp.zeros((x_prompt.shape[0], A_HEADS, A_HEAD_K, A_HEAD_V), F32)
    sp_list, ss_list = [], []
    kv_p = kv_s = None
    for layer in range(DEPTH):
        if layer < N_A_LAYERS:
            hp, sp = hgrn2_layer(hp, s_zero, w_in_a[layer], w_out_a[layer], norm_a[layer], onorm_a[layer], lb[layer])
            hs, ss = hgrn2_layer(hs, state_hgrn[layer], w_in_a[layer], w_out_a[layer], norm_a[layer], onorm_a[layer], lb[layer])
            sp_list.append(sp)
            ss_list.append(ss)
            if layer == N_A_LAYERS - 1:
                kv_p = shared_kv(hp, norm_kv, w_kv)
                kv_s = shared_kv(hs, norm_kv, w_kv)
        else:
            j = layer - N_A_LAYERS
            q, gate = swa_in(hp, w_in_b[j], norm_b[j])
            hp = swa_out(hp, swa_prompt(q, kv_p, sinks_b[j]), gate, w_out_b[j])
            q, gate = swa_in(hs, w_in_b[j], norm_b[j])
            hs = swa_out(hs, swa_sample(q, kv_s, cache_kv_window, sinks_b[j]), gate, w_out_b[j])
    y_prompt = rms_norm(hp, norm_f)
    y_sample = rms_norm(hs, norm_f)
    state_hgrn_prompt = jnp.stack(sp_list).astype(x_prompt.dtype)
    state_hgrn_sample = jnp.stack(ss_list).astype(state_hgrn.dtype)
    kv_window_prompt = kv_p[:, -min(WINDOW, kv_p.shape[1]):]
    kv_window_sample = jnp.concatenate([cache_kv_window.astype(kv_s.dtype), kv_s], axis=1)[:, -cache_kv_window.shape[1]:]
    return (y_prompt, y_sample, state_hgrn_prompt, state_hgrn_sample, kv_window_prompt, kv_window_sample)
```

```python
import bisect
from contextlib import ExitStack

import numpy as np
import concourse.bass as bass
import concourse.mybir as mybir
from concourse.bass_utils import run_bass_kernel_spmd

F32 = mybir.dt.float32
BF16 = mybir.dt.bfloat16
AF = mybir.ActivationFunctionType
ALU = mybir.AluOpType
AX = mybir.AxisListType

NCORES = 8
NT = 17
TOK = NT * 128
EPS = 1e-6
QSCALE = 128 ** -0.5
SEM_LIMIT = 30000
NDS = 24


class Res:
    __slots__ = ("name", "w", "rc", "rd", "excl", "also")

    def __init__(self, name, excl=False):
        self.name = name
        self.excl = excl
        self.also = ()
        self.w = None
        self.rc = {}
        self.rd = []


def dfr(f, *a, **k):
    fn = lambda: f(*a, **k)
    o = k.get("out", a[0] if a else None)
    try:
        n = 1
        for d in o.shape[1:]:
            n *= int(d)
        fn.n = n
    except Exception:
        fn.n = None
    fn.nm = getattr(f, "__name__", "")
    return fn


class Eng:
    def __init__(self, name, h):
        self.name = name
        self.h = h
        self.sem = None
        self.cnt = 0
        self.pos = 0
        self.last = None
        self.last_marked = True
        self.mpos = []
        self.mtk = []
        self.waited = {}


class OpRec:
    __slots__ = ("id", "en", "fn", "deps", "odeps", "dma", "mark", "dur", "lat", "key", "pm")


SCHED = True


class Trk:
    def __init__(self, nc, es):
        self.nc = nc
        self.es = es
        self.E = {n: Eng(n, h) for n, h in [("pe", nc.tensor), ("act", nc.scalar), ("dve", nc.vector),
                                            ("pool", nc.gpsimd), ("sp", nc.sync)]}
        self.sems = []
        self.dsem = [self._newsem("dma%d" % i) for i in range(NDS)]
        self.dval = [0] * NDS
        self.dnext = 0
        self.swn = 0
        self.swlast = {}
        self.nid = 0
        self.pending = []
        self.filler = None
        self._engof = {}
        self.done = {}

    def _newsem(self, name):
        s = self.es.enter_context(self.nc.semaphore(name))
        self.sems.append(s)
        return len(self.sems) - 1

    def _record(self, en, fn, R, W, is_dma, mark, dur, lat, key=None, pm=0):
        deps, odeps = set(), set()
        W = list(W)
        for w in list(W):
            for x in w.also:
                if x not in W:
                    W.append(x)
        same_skip = (not is_dma) and en == "pe"
        for r in R:
            if r.w is not None:
                deps.add(r.w)
            if r.excl:
                for e2, oid in r.rc.items():
                    if e2 != en:
                        deps.add(oid)
        for w in W:
            if w.w is not None:
                (odeps if (same_skip and self._eng_of(w.w) == en) else deps).add(w.w)
            for e2, oid in w.rc.items():
                (odeps if (same_skip and e2 == en) else deps).add(oid)
            deps.update(w.rd)
        o = OpRec()
        o.id, o.en, o.fn, o.dma, o.mark, o.dur, o.lat, o.key = self.nid, en, fn, is_dma, mark, dur, lat, key
        o.pm = pm
        self.nid += 1
        for r in R:
            if is_dma:
                r.rd.append(o.id)
                if len(r.rd) > 12:
                    deps.add(r.rd[0])
                    r.rd = r.rd[1:]
            else:
                prev = r.rc.get(en)
                if prev is not None:
                    odeps.add(prev)
                r.rc[en] = o.id
        for w in W:
            w.w = o.id
            w.rc = {}
            w.rd = []
        o.deps = [d for d in deps if d != o.id]
        o.odeps = [d for d in odeps if d != o.id and d not in deps]
        self._engof[o.id] = en
        self.pending.append(o)
        return o

    def _eng_of(self, oid):
        return self._engof.get(oid)

    def op(self, en, fn, R=(), W=(), mark=None, n=256, pm=0):
        if mark is None:
            mark = en != "pe"
        n = getattr(fn, "n", None) or n
        nm = getattr(fn, "nm", "")
        if en == "pe":
            dur = max(64, n) / 1.95 + 10
        elif en == "act":
            dur = 230 + 0.75 * n
        elif en == "dve":
            f_ = 2.0 if "scan" in nm else (6.5 if nm == "reciprocal" else 1.0)
            dur = (150 + f_ * n) / 0.96
        else:
            dur = 1000.0 if n <= 32 else 450 + 1.3 * n
        self._record(en, fn, R, W, False, mark, dur, 0.0, pm=pm)

    def dma(self, qn, out, in_, R=(), W=(), key=None, nbytes=262144, **kw):
        issue = 1000.0 if qn == "pool" else 60.0
        fn = lambda: self.E[qn].h.dma_start(out=out, in_=in_, **kw)
        self._record(qn, fn, R, W, True, False, issue, 2000.0 + nbytes / 150.0, key)

    def _schedule(self, ops):
        if not SCHED:
            return ops
        import heapq
        idx = {o.id: i for i, o in enumerate(ops)}
        nd = [0] * len(ops)
        succ = [[] for _ in ops]
        ready_t = [0.0] * len(ops)
        for i, o in enumerate(ops):
            for d in o.deps + o.odeps:
                j = idx.get(d)
                if j is not None:
                    nd[i] += 1
                    succ[j].append(i)
        def hk(o):
            return ("pe", o.pm) if o.en == "pe" else o.en
        heaps = {e: [] for e in self.E if e != "pe"}
        for m in range(3):
            heaps[("pe", m)] = []
        for i, o in enumerate(ops):
            if nd[i] == 0:
                heapq.heappush(heaps[hk(o)], (0.0, i))
        free = {e: 0.0 for e in self.E}
        fin = [0.0] * len(ops)
        order = []
        nleft = len(ops)
        import os
        WIN = int(os.environ.get('K_WIN', '6000'))
        XLAT = float(os.environ.get('K_XLAT', '120'))
        lo = 0
        sched = [False] * len(ops)
        pe_mode = 0
        SWITCH = float(os.environ.get('K_SWITCH', '300'))
        rnow = {e: [] for e in heaps}
        POL = int(os.environ.get('K_POL', '1'))
        FILL = float(os.environ.get('K_FILL', '500'))
        if POL == 2:
            rank = [0.0] * len(ops)
            for i in range(len(ops) - 1, -1, -1):
                m = 0.0
                for j in succ[i]:
                    if rank[j] > m:
                        m = rank[j]
                rank[i] = ops[i].dur + ops[i].lat + m
            prio = [-(rank[i]) for i in range(len(ops))]
        else:
            prio = list(range(len(ops)))

        def engname(e):
            return "pe" if isinstance(e, tuple) else e

        while nleft:
            best = None
            for e, h in heaps.items():
                en = engname(e)
                rn = rnow[e]
                while h and h[0][0] <= free[en]:
                    rt, i = heapq.heappop(h)
                    heapq.heappush(rn, (prio[i], i, rt))
                if rn:
                    _, i, rt = rn[0]
                    st = free[en]
                elif h:
                    rt, i = h[0]
                    st = max(free[en], rt)
                else:
                    continue
                if en == "pe" and e[1] != pe_mode:
                    st += SWITCH
                if best is None or (st, i) < (best[0], best[1]):
                    best = (st, i, e)
            st, i, e = best
            en = engname(e)
            if rnow[e] and rnow[e][0][1] == i:
                heapq.heappop(rnow[e])
            else:
                heapq.heappop(heaps[e])
            if en == "pe":
                pe_mode = e[1]
            o = ops[i]
            st = max(st, free[en])
            if en == "pe" and self.filler is not None and FILL > 0 and st - free[en] >= FILL:
                kfill = min(int((st - free[en]) / float(os.environ.get("K_FD", "60"))), int(os.environ.get("K_FMAX", "40")))
                for _ in range(kfill):
                    fo = OpRec()
                    fo.id, fo.en, fo.fn, fo.deps, fo.odeps, fo.dma, fo.mark = -1, "pe", self.filler(), [], [], False, False
                    fo.dur, fo.lat, fo.key, fo.pm = 60.0, 0.0, None, 0
                    order.append(fo)
            free[en] = st + o.dur
            fin[i] = st + o.dur + o.lat
            order.append(o)
            nleft -= 1
            for j in succ[i]:
                nd[j] -= 1
                lat = XLAT if ops[j].en != en else 40.0
                ready_t[j] = max(ready_t[j], fin[i] + lat)
                if nd[j] == 0:
                    heapq.heappush(heaps[hk(ops[j])], (ready_t[j], j))
        return order

    def _mark_last(self, F):
        if F.last_marked:
            return F.mtk[-1]
        if F.sem is None or F.cnt >= SEM_LIMIT:
            F.sem = self._newsem("%s_%d" % (F.name, len(self.sems)))
            F.cnt = 0
        F.cnt += 1
        F.last.then_inc(self.sems[F.sem], 1)
        t = (F.sem, F.cnt)
        F.mpos.append(F.pos)
        F.mtk.append(t)
        F.last_marked = True
        return t

    def _resolve(self, oid):
        d = self.done[oid]
        if d[0] == "t":
            return (d[1], d[2])
        F = self.E[d[1]]
        i = bisect.bisect_left(F.mpos, d[2])
        if i < len(F.mpos):
            return F.mtk[i]
        assert F.pos >= d[2]
        return self._mark_last(F)

    def _wait(self, F, tk):
        s, v = tk
        if F.waited.get(s, 0) >= v:
            return
        F.h.wait_ge(self.sems[s], v)
        F.waited[s] = v

    def flush(self):
        ops = self.pending
        self.pending = []
        for o in self._schedule(ops):
            F = self.E[o.en]
            if o.id == -1:
                F.last = o.fn()
                F.pos += 1
                F.last_marked = False
                continue
            for d in o.deps:
                self._wait(F, self._resolve(d))
            if o.dma:
                if o.en == "pool":
                    self.swn += 1
                    si = self._newsem("sw%d" % self.swn)
                    self.swlast[o.key] = si
                    ins = o.fn()
                    ins.then_inc(self.sems[si], 16)
                    self.done[o.id] = ("t", si, 16)
                else:
                    i = self.dnext
                    self.dnext = (i + 1) % NDS
                    if self.dval[i] > 0:
                        self._wait(F, (self.dsem[i], self.dval[i]))
                    ins = o.fn()
                    ins.then_inc(self.sems[self.dsem[i]], 16)
                    self.dval[i] += 16
                    self.done[o.id] = ("t", self.dsem[i], self.dval[i])
            else:
                ins = o.fn()
                F.pos += 1
                F.last = ins
                F.last_marked = False
                if o.mark:
                    self._mark_last(F)
                self.done[o.id] = ("c", o.en, F.pos)

    def barrier(self):
        self.flush()
        tickets = []
        for n in ("pe", "act", "dve", "pool"):
            F = self.E[n]
            if F.last is not None:
                tickets.append(self._mark_last(F))
        for i in range(NDS):
            if self.dval[i] > 0:
                tickets.append((self.dsem[i], self.dval[i]))
        for si in self.swlast.values():
            tickets.append((si, 16))
        for n in ("pe", "act", "dve", "pool", "sp"):
            for t in tickets:
                self._wait(self.E[n], t)

    def finish(self):
        self.barrier()


def build(n_a=2, n_b=2, dbg=False):
    nc = bass.Bass("TRN2", target_bir_lowering=False)

    def din(name, shape):
        return nc.dram_tensor(name, shape, F32, kind="ExternalInput").ap()

    def dout(name, shape):
        return nc.dram_tensor(name, shape, F32, kind="ExternalOutput").ap()

    x_p = din("x_p", [2048, 1024])
    x_s = din("x_s", [128, 1024])
    st_in = din("st_in", [2, 16, 16, 128, 128])
    cache = din("cache", [16, 128, 512])
    w_in_a = din("w_in_a", [2, 1024, 8192])
    w_out_a = din("w_out_a", [2, 2048, 1024])
    norm_a = din("norm_a", [2, 1024])
    onorm_a = din("onorm_a", [2, 128])
    lb_a = din("lower_bounds_a", [2, 2048])
    norm_kv = din("norm_kv", [1024])
    w_kv = din("w_kv", [1024, 512])
    w_in_b = din("w_in_b", [2, 1024, 4096])
    w_out_b = din("w_out_b", [2, 2048, 1024])
    norm_b = din("norm_b", [2, 1024])
    sinks_b = din("sinks_b", [2, 32])
    norm_f = din("norm_f", [1024])

    y_p = dout("y_p", [2048, 1024])
    y_s = dout("y_s", [128, 1024])
    sp_out = dout("sp_out", [2, 16, 128, 128])
    ss_out = dout("ss_out", [2, 16, 16, 128, 128])
    kvp_out = dout("kvp_out", [128, 512])
    kvs_out = dout("kvs_out", [16, 128, 512])
    xdbg = dout("xdbg", [TOK, 1024]) if dbg else None

    with ExitStack() as es:
        tk = Trk(nc, es)
        op, dma = tk.op, tk.dma
        PE, ACT, DVE, POOL = nc.tensor, nc.scalar, nc.vector, nc.gpsimd

        def sb(name, shape, dt, stack=es):
            return stack.enter_context(nc.sbuf_tensor(name, shape, dt))

        banks = [es.enter_context(nc.psum_tensor("bank%d" % i, [128, 512], F32)) for i in range(8)]
        bankr = [Res("bank%d" % i, excl=True) for i in range(8)]

        X = sb("X", [128, NT, 1024], F32)
        Xr = [Res("X%d" % i) for i in range(NT)]
        xnT = sb("xnT", [128, 8, TOK], BF16)
        xnTr = [Res("xnT%d" % i) for i in range(NT)]
        ident = sb("ident", [128, 128], BF16)
        identf = sb("identf", [128, 128], F32)
        gB = sb("gB", [128, 1024], F32)
        gBr = Res("gB")
        xtmp = [sb("xtmp0", [128, 1024], BF16)] * 2
        xtmpr = [Res("xtmp0")] * 2
        ssq = sb("ssq", [128, NT], F32)
        ssqr = Res("ssq")
        rstd = sb("rstd", [128, NT], F32)
        rstdr = Res("rstd")
        mhalf = sb("mhalf", [128, 32], F32)
        mask_p = sb("mask_p", [128, 128], BF16)
        mask_s = sb("mask_s", [128, 128], BF16)
        Eind = sb("Eind", [16, 2, 128], BF16)
        cr = Res("consts")

        def setup_consts():
            op("pool", dfr(POOL.memset, ident[:], 1.0), W=[cr])
            op("pool", dfr(POOL.affine_select, out=ident[:], in_=ident[:], pattern=[[-1, 128]],
                                                  compare_op=ALU.is_equal, fill=0.0, base=0,
                                                  channel_multiplier=1), R=[cr], W=[cr])
            op("pool", dfr(POOL.memset, identf[:], 1.0), W=[cr])
            op("pool", dfr(POOL.affine_select, out=identf[:], in_=identf[:], pattern=[[-1, 128]],
                                                  compare_op=ALU.is_equal, fill=0.0, base=0,
                                                  channel_multiplier=1), R=[cr], W=[cr])
            op("pool", dfr(POOL.memset, mhalf[:], -0.5), W=[cr])
        op("pool", dfr(POOL.memset, Eind[:], 1.0), W=[cr])
        for slot, cs in ((0, 64), (1, 8)):
            op("pool", dfr(POOL.affine_select, out=Eind[:, slot, :], in_=Eind[:, slot, :],
                                                  pattern=[[1, 128]], compare_op=ALU.is_ge, fill=0.0,
                                                  base=0, channel_multiplier=-cs), R=[cr], W=[cr])
            op("pool", dfr(POOL.affine_select, out=Eind[:, slot, :], in_=Eind[:, slot, :],
                                                  pattern=[[-1, 128]], compare_op=ALU.is_ge, fill=0.0,
                                                  base=cs - 1, channel_multiplier=cs), R=[cr], W=[cr])
        for slot, m in ((0, mask_p), (1, mask_s)):
            op("pe", dfr(PE.matmul, banks[3][:, 0:128], lhsT=Eind[:, slot, :], rhs=Eind[:, slot, :],
                                       start=True, stop=True), R=[cr], W=[bankr[3]], mark=True)
            op("dve", dfr(DVE.tensor_copy, out=m[:], in_=banks[3][:, 0:128]), R=[bankr[3]], W=[cr])
            op("pool", dfr(POOL.affine_select, out=m[:], in_=m[:], pattern=[[1, 128]],
                                                  compare_op=ALU.is_ge, fill=0.0, base=0,
                                                  channel_multiplier=-1), R=[cr], W=[cr])

        setup_consts()

        xpv = x_p.rearrange("(n p) d -> p n d", p=128)
        for ti in range(16):
            dma("sp", X[:, ti, :], xpv[:, ti, :], W=[Xr[ti]])
        dma("sp", X[:, 16, :], x_s[:, :], W=[Xr[16]])

        def norm_phase(gvec):
            dma("sp", gB[:], gvec.partition_broadcast(128), W=[gBr])
            for ti in range(NT):
                b = ti % 2
                op("act", dfr(ACT.activation, out=xtmp[b][:], in_=X[:, ti, :], func=AF.Square,
                                                 accum_out=ssq[:, ti:ti + 1]),
                   R=[Xr[ti]], W=[xtmpr[b], ssqr])
            op("dve", dfr(DVE.tensor_scalar, out=rstd[:], in0=ssq[:], scalar1=1.0 / 1024, scalar2=EPS,
                                                op0=ALU.mult, op1=ALU.add), R=[ssqr], W=[rstdr])
            op("pool", dfr(POOL.tensor_tensor, out=rstd[:], in0=rstd[:], in1=mhalf[:, 0:NT], op=ALU.pow),
               R=[rstdr, cr], W=[rstdr])
            trv = banks[7][:].bitcast(BF16).rearrange("p (k t) -> p k t", k=8)
            for ti in range(NT):
                b = ti % 2
                op("dve", dfr(DVE.scalar_tensor_tensor, out=xtmp[b][:], in0=X[:, ti, :],
                                                           scalar=rstd[:, ti:ti + 1], in1=gB[:],
                                                           op0=ALU.mult, op1=ALU.mult),
                   R=[Xr[ti], rstdr, gBr], W=[xtmpr[b]])
                for k in range(8):
                    op("pe", dfr(PE.transpose, out=trv[:, k, :], in_=xtmp[b][:, k * 128:(k + 1) * 128],
                                                  identity=ident[:]),
                       R=[xtmpr[b], cr], W=[bankr[7]], mark=(k == 7))
                op("act", dfr(ACT.copy, out=xnT[:, :, ti * 128:(ti + 1) * 128], in_=trv),
                   R=[bankr[7]], W=[xnTr[ti]])

        TCH = 256

        def hgrn_phase():
            with ExitStack() as hs:
                def hb(name, shape, dt):
                    return sb(name, shape, dt, hs)

                Wq = hb("Wq", [128, 8, 512], BF16)
                Wf = hb("Wf", [128, 8, 512], BF16)
                Wi = hb("Wi", [128, 8, 512], BF16)
                Wg = hb("Wg", [128, 8, 512], BF16)
                Wo = hb("Wo", [128, 4, 1024], BF16)
                Wqr, Wfr, Wir, Wgr, Wor = (Res(n) for n in ("Wq", "Wf", "Wi", "Wg", "Wo"))
                praw = hb("praw", [34, 128], F32)
                prawr = Res("praw")
                pT = hb("pT", [128, 34], F32)
                pTr = Res("pT")
                lb = hb("lb", [128, 2, 16], F32)
                oml = hb("oml", [128, 2, 16], F32)
                lbr = Res("lb")
                tT = [[hb("t%s%d" % (n, i), [128, TCH], F32) for n in "ABCDE"] for i in range(2)]
                tTr = [[Res("t%s%d" % (n, i)) for n in "ABCDE"] for i in range(2)]
                hctr = [0]
                NSET = 2
                qh = [hb("qh%d" % s, [128, 4, TCH], BF16) for s in range(NSET)]
                qt = [hb("qt%d" % s, [128, 4, TCH], BF16) for s in range(NSET)]
                kh = [hb("kh%d" % s, [128, 4, TCH], BF16) for s in range(NSET)]
                dec = [hb("dec%d" % s, [128, 4, 16], F32) for s in range(NSET)]
                rl = [hb("rl%d" % s, [128, 4, 16], F32) for s in range(NSET)]
                rlr = [[Res("rl") for _ in range(4)] for _ in range(NSET)]
                emask_p = hb("emask_p", [128, TCH], F32)
                emask_s = hb("emask_s", [128, 128], F32)
                qhr = [[Res("qh") for _ in range(4)] for _ in range(NSET)]
                qtr = [[Res("qt") for _ in range(4)] for _ in range(NSET)]
                khr = [[Res("kh") for _ in range(4)] for _ in range(NSET)]
                decr = [[Res("dec") for _ in range(4)] for _ in range(NSET)]
                smask_p = hb("smask_p", [128, TCH], F32)
                smask_s = hb("smask_s", [128, 128], F32)
                sel = hb("sel", [128, 16, 16], BF16)
                selT = hb("selT", [128, 16], BF16)
                v_sb = hb("v_sb", [128, 512], BF16)
                sg = hb("sg", [128, 512], BF16)
                gs = hb("gs", [128, 512], BF16)
                khtok = hb("khtok", [128, 4, 128], BF16)
                attm = hb("attm", [128, 4, 128], BF16)
                oss = hb("oss", [128, 4], F32)
                orstd = hb("orstd", [128, 4], F32)
                og = hb("og", [128, 512], BF16)
                ogT = hb("ogT", [128, 4, 128], BF16)
                v_sbr, sgr, gsr, khtokr, attmr, ossr, orstdr, ogr, ogTr = (
                    Res(n) for n in ("v_sb", "sg", "gs", "khtok", "attm", "oss", "orstd", "og", "ogT"))
                oss1 = hb("oss1", [128, 4], F32)
                orstd1 = hb("orstd1", [128, 4], F32)
                S = hb("S", [128, 4, 128], F32)
                Sr = [Res("S%d" % h) for h in range(4)]
                snap = hb("snap", [128, 4, 4, 128], BF16)
                snapr = [[Res("snap") for _ in range(4)] for _ in range(4)]
                S0 = [hb("S0_%d" % i, [128, 8, 128], F32) for i in range(2)]
                S0r = [Res("S0_%d" % i) for i in range(2)]
                S0bf = hb("S0bf", [128, 8, 128], BF16)
                S0bfr = Res("S0bf")
                Qexp = hb("Qexp", [128, 8, 128], BF16)
                Qexpr = Res("Qexp")
                Vexp = hb("Vexp", [128, 8, 128], BF16)
                Vexpr = Res("Vexp")

                f0 = S0[0][:].rearrange("p j e -> p (j e)").bitcast(BF16)
                f1 = S0[1][:].rearrange("p j e -> p (j e)").bitcast(BF16)
                TB = [(v_sb, sg, gs, khtok, attm, oss, orstd, og, ogT),
                      (f0[:, 0:512], f0[:, 512:1024], f0[:, 1024:1536],
                       f0[:, 1536:2048].rearrange("p (h t) -> p h t", h=4),
                       f1[:, 0:512].rearrange("p (h t) -> p h t", h=4), oss1, orstd1,
                       f1[:, 512:1024], f1[:, 1024:1536].rearrange("p (h t) -> p h t", h=4))]
                TBR = [(v_sbr, sgr, gsr, khtokr, attmr, ossr, orstdr, ogr, ogTr),
                       tuple(Res(n + "1") for n in ("v_sb", "sg", "gs", "khtok", "attm", "oss", "orstd", "og", "ogT"))]
                for ri, r_ in enumerate(TBR[1]):
                    if ri in (0, 1, 2, 3):
                        r_.also = (S0r[0],)
                    elif ri in (4, 7, 8):
                        r_.also = (S0r[1],)
                S0r[0].also = tuple(TBR[1][i] for i in (0, 1, 2, 3))
                S0r[1].also = tuple(TBR[1][i] for i in (4, 7, 8))

                q_ps = banks[0][:, 0:TCH]
                f_ps = banks[0][:, 256:256 + TCH]
                qfr = bankr[0]
                v_ps, v_psr = banks[1], bankr[1]
                g_ps, g_psr = banks[2], bankr[2]
                att_ps = banks[3][:].rearrange("p (h t) -> p h t", h=4)
                att_psr = bankr[3]
                o_ps = banks[4][:].rearrange("p (h t) -> p h t", h=4)
                o_psr = bankr[4]
                b5 = banks[3][:].bitcast(BF16)
                kT_ps = b5[:, 0:512].rearrange("p (h t) -> p h t", h=4)
                ogT_ps = b5[:, 512:1024].rearrange("p (h t) -> p h t", h=4)
                kT_psr, ogT_psr = bankr[3], bankr[3]
                tk.filler = lambda: dfr(PE.matmul, banks[5][:, 0:128], lhsT=ident[:], rhs=ident[:], start=True, stop=True)
                kvb = banks[6][:].rearrange("p (h e) -> p h e", h=4)
                kv_psr = [bankr[6]]
                kv4_ps = banks[6][:].rearrange("p (j e) -> p j e", j=4)
                y_ps, y_psr = banks[7], bankr[7]

                def v3(ap, c):
                    return ap.rearrange("p (c t) -> p c t", t=c)

                op("pool", dfr(POOL.memset, smask_p[:], 0.0), W=[cr])
                op("pool", dfr(POOL.memset, v3(smask_p, 64)[:, :, 0:1], 1.0), W=[cr])
                op("pool", dfr(POOL.memset, emask_p[:], 0.0), W=[cr])
                op("pool", dfr(POOL.memset, v3(emask_p, 64)[:, :, 63:64], 1.0), R=[cr], W=[cr])
                op("pool", dfr(POOL.memset, emask_s[:], 0.0), W=[cr])
                op("pool", dfr(POOL.memset, v3(emask_s, 8)[:, :, 7:8], 1.0), R=[cr], W=[cr])
                op("pool", dfr(POOL.memset, smask_s[:], 0.0), W=[cr])
                op("pool", dfr(POOL.memset, v3(smask_s, 8)[:, :, 0:1], 1.0), W=[cr])
                op("pool", dfr(POOL.memset, sel[:], 1.0), W=[cr])
                op("pool", dfr(POOL.affine_select, out=sel[:], in_=sel[:], pattern=[[-1, 16], [1, 16]],
                                                      compare_op=ALU.is_equal, fill=0.0, base=0,
                                                      channel_multiplier=0), R=[cr], W=[cr])
                op("pool", dfr(POOL.memset, selT[:], 1.0), W=[cr])
                op("pool", dfr(POOL.affine_select, out=selT[:], in_=selT[:], pattern=[[-8, 16]],
                                                      compare_op=ALU.is_ge, fill=0.0, base=0,
                                                      channel_multiplier=1), R=[cr], W=[cr])
                op("pool", dfr(POOL.affine_select, out=selT[:], in_=selT[:], pattern=[[8, 16]],
                                                      compare_op=ALU.is_ge, fill=0.0, base=7,
                                                      channel_multiplier=-1), R=[cr], W=[cr])

                dma("sp", praw[0:32, :], lb_a.rearrange("l (h d) -> (l h) d", d=128), W=[prawr])
                dma("sp", praw[32:34, :], onorm_a[:, :], W=[prawr])
                op("pe", dfr(PE.transpose, out=banks[3][:, 0:34], in_=praw[:, :], identity=identf[0:34, 0:34]),
                   R=[prawr, cr], W=[bankr[3]], mark=True)
                op("dve", dfr(DVE.tensor_copy, out=pT[:], in_=banks[3][:, 0:34]), R=[bankr[3]], W=[pTr])
                op("dve", dfr(DVE.memset, lb[:, 0, :], 0.0), W=[lbr])
                op("dve", dfr(DVE.tensor_tensor, out=lb[:, 1, :], in0=pT[:, 16:32], in1=pT[:, 0:16],
                                                    op=ALU.subtract), R=[pTr], W=[lbr])
                op("act", dfr(ACT.activation, out=lb[:, 1, :], in_=lb[:, 1, :], func=AF.Sigmoid),
                   R=[lbr], W=[lbr])
                op("dve", dfr(DVE.tensor_scalar, out=oml[:], in0=lb[:], scalar1=-1.0, scalar2=1.0,
                                                    op0=ALU.mult, op1=ALU.add), R=[lbr], W=[lbr])

                chunks = [(c * TCH, TCH, False) for c in range(2048 // TCH)] + [(2048, 128, True)]

                def load_weights(l, g):
                    wv = w_in_a[l].rearrange("(k p) n -> p k n", p=128)
                    for typ, (Wt, Wr) in enumerate(((Wq, Wqr), (Wf, Wfr), (Wi, Wir), (Wg, Wgr))):
                        c0 = typ * 2048 + g * 512
                        dma("pool", Wt[:], wv[:, :, c0:c0 + 512], W=[Wr], key="Win%d" % typ)
                    wo = w_out_a[l][g * 512:(g + 1) * 512, :].rearrange("(h p) n -> p h n", p=128)
                    dma("pool", Wo[:], wo, W=[Wor], key="Wo")
                    op("pool", dfr(POOL.tensor_scalar, out=Wo[:], in0=Wo[:], scalar1=pT[:, 32 + l:33 + l],
                                                          scalar2=0.0, op0=ALU.mult, op1=ALU.add),
                       R=[Wor, pTr], W=[Wor])

                def stage_a(l, g, st, t0, T, samp):
                    cs = 8 if samp else 64
                    nch = T // cs
                    mid = cs // 2 - 1
                    smask = smask_s if samp else smask_p
                    tiles = list(range(t0 // 128, (t0 + T) // 128))
                    xr = [xnTr[t] for t in tiles]
                    for hl in range(4):
                        hg = g * 4 + hl
                        qv, fv = q_ps[:, 0:T], f_ps[:, 0:T]
                        for k in range(8):
                            op("pe", dfr(PE.matmul, qv, lhsT=Wq[:, k, hl * 128:(hl + 1) * 128],
                                                       rhs=xnT[:, k, t0:t0 + T], start=(k == 0), stop=(k == 7)),
                               R=[Wqr] + xr, W=[qfr])
                        for k in range(8):
                            op("pe", dfr(PE.matmul, fv, lhsT=Wf[:, k, hl * 128:(hl + 1) * 128],
                                                       rhs=xnT[:, k, t0:t0 + T], start=(k == 0), stop=(k == 7)),
                               R=[Wfr] + xr, W=[qfr], mark=(k == 7))
                        tb = hctr[0] % 2
                        hctr[0] += 1
                        A, B, C, Dd, Ee = (t_[:, 0:T] for t_ in tT[tb])
                        tAr, tBr, tCr, tDr, tEr = tTr[tb]
                        op("act", dfr(ACT.activation, out=A, in_=fv, func=AF.Sigmoid), R=[qfr], W=[tAr])
                        op("act", dfr(ACT.activation, out=B, in_=fv, func=AF.Sigmoid, scale=-1.0),
                           R=[qfr], W=[tBr])
                        op("act", dfr(ACT.activation, out=C, in_=qv, func=AF.Sigmoid), R=[qfr], W=[tCr])
                        op("dve", dfr(DVE.tensor_tensor, out=C, in0=qv, in1=C, op=ALU.mult),
                           R=[qfr, tCr], W=[tCr])
                        if l > 0:
                            op("act", dfr(ACT.activation, out=A, in_=A, func=AF.Identity,
                                          scale=oml[:, l, hg:hg + 1], bias=lb[:, l, hg:hg + 1]),
                               R=[tAr, lbr], W=[tAr])
                        op("pool", dfr(POOL.tensor_tensor, out=Dd, in0=A, in1=smask[:, 0:T], op=ALU.mult),
                           R=[tAr, cr], W=[tDr])
                        op("dve", dfr(DVE.tensor_tensor_scan, out=Ee, data0=A, data1=Dd, initial=1.0,
                                                                 op0=ALU.mult, op1=ALU.max),
                           R=[tAr, tDr], W=[tEr])
                        op("act", dfr(ACT.copy, out=Dd[:, 0:T - 1], in_=A[:, 1:T]), R=[tAr], W=[tDr])
                        op("pool", dfr(POOL.memset, v3(Dd, cs)[:, :, cs - 1:cs], 1.0), R=[tDr], W=[tDr])
                        emask = emask_s if samp else emask_p
                        op("dve", dfr(DVE.tensor_tensor_scan, out=A[:, ::-1], data0=Dd[:, ::-1],
                                                                 data1=emask[:, 0:T][:, ::-1], initial=1.0,
                                                                 op0=ALU.mult, op1=ALU.max),
                           R=[tDr, cr], W=[tAr])
                        E3 = v3(Ee, cs)
                        op("dve", dfr(DVE.scalar_tensor_tensor, out=kh[st][:, hl, 0:T], in0=B,
                                                                   scalar=oml[:, l, hg:hg + 1], in1=A,
                                                                   op0=ALU.mult, op1=ALU.mult),
                           R=[tBr, tAr, lbr], W=[khr[st][hl]])
                        op("dve", dfr(DVE.scalar_tensor_tensor, out=qh[st][:, hl, 0:T], in0=C, scalar=QSCALE,
                                                                   in1=Ee, op0=ALU.mult, op1=ALU.mult),
                           R=[tCr, tEr], W=[qhr[st][hl]])
                        op("dve", dfr(DVE.tensor_copy, out=dec[st][:, hl, 0:nch], in_=E3[:, :, cs - 1]),
                           R=[tEr], W=[decr[st][hl]])
                        op("dve", dfr(DVE.reciprocal, out=rl[st][:, hl, 0:nch], in_=dec[st][:, hl, 0:nch]),
                           R=[decr[st][hl]], W=[rlr[st][hl]])
                        op("pool", dfr(POOL.tensor_tensor,
                            out=v3(qt[st][:, hl, 0:T], cs), in0=v3(qh[st][:, hl, 0:T], cs),
                            in1=rl[st][:, hl, 0:nch].unsqueeze(2).to_broadcast([128, nch, cs]), op=ALU.mult),
                           R=[qhr[st][hl], rlr[st][hl]], W=[qtr[st][hl]])
                        yield

                def stage_b(l, g, st, t0, T, samp, kctr):
                    for ti in range(T // 128):
                        tile = t0 // 128 + ti
                        tsl = slice(ti * 128, (ti + 1) * 128)
                        gsl = slice(tile * 128, (tile + 1) * 128)
                        tp = 0
                        v_sb, sg, gs, khtok, attm, oss, orstd, og, ogT = TB[tp]
                        v_sbr, sgr, gsr, khtokr, attmr, ossr, orstdr, ogr, ogTr = TBR[tp]
                        osq, osqr = sg, sgr
                        for k in range(8):
                            op("pe", dfr(PE.matmul, v_ps[:], lhsT=xnT[:, k, gsl], rhs=Wi[:, k, :],
                                                       start=(k == 0), stop=(k == 7)),
                               R=[Wir, xnTr[tile]], W=[v_psr], mark=(k == 7))
                        for k in range(8):
                            op("pe", dfr(PE.matmul, g_ps[:], lhsT=xnT[:, k, gsl], rhs=Wg[:, k, :],
                                                       start=(k == 0), stop=(k == 7)),
                               R=[Wgr, xnTr[tile]], W=[g_psr], mark=(k == 7))
                        op("act", dfr(ACT.copy, out=v_sb[:], in_=v_ps[:]), R=[v_psr], W=[v_sbr])
                        op("act", dfr(ACT.activation, out=sg[:], in_=g_ps[:], func=AF.Sigmoid),
                           R=[g_psr], W=[sgr])
                        op("dve", dfr(DVE.tensor_tensor, out=gs[:], in0=g_ps[:], in1=sg[:], op=ALU.mult),
                           R=[g_psr, sgr], W=[gsr])
                        yield
                        for hl in range(4):
                            op("pe", dfr(PE.transpose, out=kT_ps[:, hl, :], in_=kh[st][:, hl, tsl],
                                                          identity=ident[:]),
                               R=[khr[st][hl], cr], W=[kT_psr], mark=(hl == 3))
                        op("act", dfr(ACT.copy, out=khtok[:], in_=kT_ps), R=[kT_psr], W=[khtokr])
                        for hl in range(4):
                            op("pe", dfr(PE.matmul, att_ps[:, hl, :], lhsT=kh[st][:, hl, tsl],
                                                       rhs=qt[st][:, hl, tsl], start=True, stop=True),
                               R=[khr[st][hl], qtr[st][hl]], W=[att_psr], mark=(hl == 3))
                        msk = mask_s if samp else mask_p
                        op("dve", dfr(DVE.tensor_tensor, out=attm[:], in0=att_ps,
                                                            in1=msk[:].unsqueeze(1).to_broadcast([128, 4, 128]),
                                                            op=ALU.mult), R=[att_psr, cr], W=[attmr])
                        vh = v_sb[:].rearrange("p (h e) -> p h e", h=4)
                        yield
                        if not samp:
                            k0 = kctr[0]
                            for c in range(2):
                                kk = k0 + c
                                sl64 = slice(64 * c, 64 * c + 64)
                                for hl in range(4):
                                    op("pe", dfr(PE.matmul, kvb[:, hl, :], lhsT=khtok[sl64, hl, :], rhs=vh[sl64, hl, :],
                                                               start=True, stop=True),
                                       R=[khtokr, v_sbr], W=kv_psr, mark=(hl == 3), pm=1, n=128)
                                for hl in range(4):
                                    dcol = dec[st][:, hl, 2 * ti + c:2 * ti + c + 1]
                                    if kk == 0:
                                        op("dve", dfr(DVE.tensor_copy, out=S[:, hl, :], in_=kvb[:, hl, :]),
                                           R=kv_psr, W=[Sr[hl]])
                                    else:
                                        op("dve", dfr(DVE.scalar_tensor_tensor,
                                            out=S[:, hl, :], in0=S[:, hl, :], scalar=dcol, in1=kvb[:, hl, :],
                                            op0=ALU.mult, op1=ALU.add),
                                           R=[Sr[hl], decr[st][hl]] + kv_psr, W=[Sr[hl]])
                                op("act", dfr(ACT.copy, out=snap[:, :, (kk + 1) % 4, :], in_=S[:, :, :]),
                                   R=Sr, W=[snapr[h_][(kk + 1) % 4] for h_ in range(4)])
                                yield
                            if tile == 15:
                                for hl in range(4):
                                    dma("sp", sp_out[l, g * 4 + hl, :, :], S[:, hl, :], R=[Sr[hl]])
                            for hl in range(4):
                                op("pe", dfr(PE.matmul, o_ps[:, hl, :], lhsT=attm[:, hl, :], rhs=vh[:, hl, :],
                                                           start=True, stop=False),
                                   R=[attmr, v_sbr], W=[o_psr])
                                for c in range(2):
                                    kk = k0 + c
                                    sl64 = slice(64 * c, 64 * c + 64)
                                    op("pe", dfr(PE.matmul,
                                        o_ps[sl64, hl, :], lhsT=qh[st][:, hl, ti * 128 + 64 * c:ti * 128 + 64 * c + 64],
                                        rhs=snap[:, hl, kk % 4, :], start=False, stop=True),
                                       R=[qhr[st][hl], snapr[hl][kk % 4]], W=[o_psr], pm=2, n=128)
                            for hl in range(4):
                                kctr[hl] += 2
                        else:
                            items = [(hl, hf) for hl in range(4) for hf in range(2)]

                            def s0_load(n):
                                hl_, hf_ = items[n]
                                dma("sp", S0[n % 2][:],
                                    st_in[l, 8 * hf_:8 * hf_ + 8, g * 4 + hl_, :, :].rearrange("j d e -> d j e"),
                                    W=[S0r[n % 2]])

                            s0_load(0)
                            for n, (hl, hf) in enumerate(items):
                                hg = g * 4 + hl
                                bi = n % 2
                                if n + 1 < len(items):
                                    s0_load(n + 1)
                                j0 = 8 * hf
                                op("act", dfr(ACT.copy, out=S0bf[:], in_=S0[bi][:]), R=[S0r[bi]], W=[S0bfr])
                                op("dve", dfr(DVE.tensor_tensor,
                                    out=Qexp[:].rearrange("p j (a u) -> p j a u", u=8),
                                    in0=qh[st][:, hl, 0:128].rearrange("p (a u) -> p a u", u=8).unsqueeze(1)
                                    .to_broadcast([128, 8, 16, 8]),
                                    in1=sel[:, j0:j0 + 8, :].unsqueeze(3).to_broadcast([128, 8, 16, 8]),
                                    op=ALU.mult), R=[qhr[st][hl], cr], W=[Qexpr])
                                op("dve", dfr(DVE.tensor_tensor,
                                    out=Vexp[:], in0=vh[:, hl, :].unsqueeze(1).to_broadcast([128, 8, 128]),
                                    in1=selT[:, j0:j0 + 8].unsqueeze(2).to_broadcast([128, 8, 128]), op=ALU.mult),
                                   R=[v_sbr, cr], W=[Vexpr])
                                if hf == 0:
                                    op("pe", dfr(PE.matmul, o_ps[:, hl, :], lhsT=attm[:, hl, :], rhs=vh[:, hl, :],
                                                               start=True, stop=False), R=[attmr, v_sbr], W=[o_psr])
                                for jj in range(8):
                                    op("pe", dfr(PE.matmul, o_ps[:, hl, :], lhsT=Qexp[:, jj, :], rhs=S0bf[:, jj, :],
                                                               start=False, stop=(hf == 1 and jj == 7)),
                                       R=[Qexpr, S0bfr], W=[o_psr])
                                for q4 in range(2):
                                    op("pe", dfr(PE.matmul, kv4_ps, lhsT=khtok[:, hl, :],
                                                               rhs=Vexp[:, 4 * q4:4 * q4 + 4, :], start=True, stop=True),
                                       R=[khtokr, Vexpr], W=kv_psr, mark=True)
                                    for jj in range(4):
                                        jl = 4 * q4 + jj
                                        op("dve", dfr(DVE.scalar_tensor_tensor,
                                            out=S0[bi][:, jl, :], in0=S0[bi][:, jl, :],
                                            scalar=dec[st][:, hl, j0 + jl:j0 + jl + 1],
                                            in1=kv4_ps[:, jj, :], op0=ALU.mult, op1=ALU.add),
                                           R=[S0r[bi], decr[st][hl]] + kv_psr, W=[S0r[bi]])
                                dma("sp", ss_out[l, j0:j0 + 8, hg, :, :].rearrange("j d e -> d j e"), S0[bi][:],
                                    R=[S0r[bi]])
                                yield
                        op("act", dfr(ACT.activation, out=osq[:], in_=banks[4][:], func=AF.Square),
                           R=[o_psr], W=[osqr])
                        op("dve", dfr(DVE.tensor_reduce, out=oss[:], in_=osq[:].rearrange("p (h e) -> p h e", h=4),
                                                            axis=AX.X, op=ALU.add), R=[osqr], W=[ossr])
                        op("dve", dfr(DVE.tensor_scalar, out=orstd[:], in0=oss[:], scalar1=1.0 / 128,
                                                            scalar2=EPS, op0=ALU.mult, op1=ALU.add),
                           R=[ossr], W=[orstdr])
                        op("pool", dfr(POOL.tensor_tensor, out=orstd[:], in0=orstd[:], in1=mhalf[:, 0:4],
                                                              op=ALU.pow), R=[orstdr, cr], W=[orstdr])
                        for hl in range(4):
                            op("dve", dfr(DVE.scalar_tensor_tensor,
                                out=og[:, hl * 128:(hl + 1) * 128], in0=o_ps[:, hl, :], scalar=orstd[:, hl:hl + 1],
                                in1=gs[:, hl * 128:(hl + 1) * 128], op0=ALU.mult, op1=ALU.mult),
                               R=[o_psr, orstdr, gsr], W=[ogr])
                        yield
                        for hl in range(4):
                            op("pe", dfr(PE.transpose, out=ogT_ps[:, hl, :], in_=og[:, hl * 128:(hl + 1) * 128],
                                                          identity=ident[:]),
                               R=[ogr, cr], W=[ogT_psr], mark=(hl == 3))
                        op("act", dfr(ACT.copy, out=ogT[:], in_=ogT_ps), R=[ogT_psr], W=[ogTr])
                        for half in range(2):
                            hs_ = slice(half * 512, (half + 1) * 512)
                            for hl in range(4):
                                op("pe", dfr(PE.matmul, y_ps[:], lhsT=ogT[:, hl, :], rhs=Wo[:, hl, hs_],
                                                           start=(hl == 0), stop=(hl == 3)),
                                   R=[ogTr, Wor], W=[y_psr], mark=(hl == 3))
                            op("dve", dfr(DVE.tensor_tensor, out=X[:, tile, hs_], in0=y_ps[:], in1=X[:, tile, hs_],
                                                                op=ALU.add),
                               R=[y_psr, Xr[tile]], W=[Xr[tile]])

                import os
                LV = int(os.environ.get("KDBG_LV", "99"))
                def drive(items):
                    items = [[g_, w_] for g_, w_ in items if g_ is not None]
                    while items:
                        for it in list(items):
                            for _ in range(it[1]):
                                try:
                                    next(it[0])
                                except StopIteration:
                                    items.remove(it)
                                    break

                def load_qf(l, g):
                    wv = w_in_a[l].rearrange("(k p) n -> p k n", p=128)
                    for typ, (Wt, Wr) in ((0, (Wq, Wqr)), (1, (Wf, Wfr))):
                        c0 = typ * 2048 + g * 512
                        dma("pool", Wt[:], wv[:, :, c0:c0 + 512], W=[Wr], key="Win%d" % typ)

                def load_igo(l, g):
                    wv = w_in_a[l].rearrange("(k p) n -> p k n", p=128)
                    for typ, (Wt, Wr) in ((2, (Wi, Wir)), (3, (Wg, Wgr))):
                        c0 = typ * 2048 + g * 512
                        dma("pool", Wt[:], wv[:, :, c0:c0 + 512], W=[Wr], key="Win%d" % typ)
                    wo = w_out_a[l][g * 512:(g + 1) * 512, :].rearrange("(h p) n -> p h n", p=128)
                    dma("pool", Wo[:], wo, W=[Wor], key="Wo")
                    op("pool", dfr(POOL.tensor_scalar, out=Wo[:], in0=Wo[:], scalar1=pT[:, 32 + l:33 + l],
                                                          scalar2=0.0, op0=ALU.mult, op1=ALU.add),
                       R=[Wor, pTr], W=[Wor])

                gidx = 0
                for l in range(n_a):
                    norm_phase(norm_a[l])
                    pending_b = None
                    for g in range(4):
                        kctr = [0, 0, 0, 0]
                        for ci, (t0, T, samp) in enumerate(chunks):
                            st = gidx % NSET
                            gidx += 1
                            if ci == 0:
                                load_qf(l, g)
                            drive([(stage_a(l, g, st, t0, T, samp), 1), (pending_b, 3)])
                            if ci == 0:
                                load_igo(l, g)
                                for hl0 in range(4):
                                    op("pool", dfr(POOL.memset, snap[:, hl0, 0, :], 0.0), W=[snapr[hl0][0]])
                            pending_b = stage_b(l, g, st, t0, T, samp, kctr)
                    drive([(pending_b, 1)])
                tk.barrier()

        hgrn_phase()


        TS = 256

        def swa_phase():
            with ExitStack() as ws:
                def hb(name, shape, dt, stack=ws):
                    return sb(name, shape, dt, stack)

                KT = hb("KT", [128, 4, TOK], BF16)
                KTr = [Res("KT%d" % i) for i in range(NT)]
                V = hb("V", [128, NT, 4, 64], BF16)
                Vr = [Res("V%d" % i) for i in range(NT)]
                KcT = hb("KcT", [128, 16, 4, 128], BF16)
                Vc = hb("Vc", [128, 16, 4, 64], BF16)
                KcTr, Vcr = Res("KcT"), Res("Vc")
                ones64 = hb("ones64", [128, 64], BF16)
                mask2 = hb("mask2", [128, 2, 128], BF16)
                maskc = hb("maskc", [128, 8], BF16)
                WA = hb("WA", [128, 8, 512], BF16)
                WB = hb("WB", [128, 8, 512], BF16)
                WO = hb("WO", [128, 4, 1024], BF16)
                WAr, WBr, WOr = Res("WA"), Res("WB"), Res("WO")
                esraw = hb("esraw", [128, 32], F32)
                esp = hb("esp", [128, 16], F32)
                esr = Res("es")

                op("pool", dfr(POOL.memset, ones64[:], 1.0), W=[cr])
                op("pool", dfr(POOL.memset, mask2[:], 1.0), W=[cr])
                op("pool", dfr(POOL.affine_select, out=mask2[:, 0, :], in_=mask2[:, 0, :], pattern=[[-1, 128]],
                                                      compare_op=ALU.is_ge, fill=0.0, base=0,
                                                      channel_multiplier=1), R=[cr], W=[cr])
                op("pool", dfr(POOL.affine_select, out=mask2[:, 1, :], in_=mask2[:, 1, :], pattern=[[1, 128]],
                                                      compare_op=ALU.is_ge, fill=0.0, base=0,
                                                      channel_multiplier=-1), R=[cr], W=[cr])
                op("pool", dfr(POOL.memset, maskc[:], 1.0), W=[cr])
                op("pool", dfr(POOL.affine_select, out=maskc[:], in_=maskc[:], pattern=[[-1, 8]],
                                                      compare_op=ALU.is_ge, fill=0.0, base=0,
                                                      channel_multiplier=1), R=[cr], W=[cr])

                tk.filler = lambda: dfr(PE.matmul, banks[5][:, 0:128], lhsT=ident[:], rhs=ident[:], start=True, stop=True)
                norm_phase(norm_kv)
                with ExitStack() as ks:
                    kc = hb("kc", [128, 512], F32, ks)
                    kcb = hb("kcb", [128, 4, 2, 64], BF16, ks)
                    kvf = hb("kvf", [128, 512], F32, ks)
                    kcr, kcbr, kvfr = Res("kc"), Res("kcb"), Res("kvf")
                    dma("pool", WA[:], w_kv.rearrange("(k p) n -> p k n", p=128), W=[WAr], key="wkv")
                    WBv = WB[:].rearrange("p k (g d h) -> p k g d h", g=4, d=2)
                    WAv = WA[:, :, 0:256].rearrange("p k (g h) -> p k g h", g=4)
                    for k in range(8):
                        op("pool", dfr(POOL.tensor_copy,
                            out=WBv[:, k], in_=WAv[:, k].unsqueeze(2).to_broadcast([128, 4, 2, 64])),
                           R=[WAr], W=[WBr])
                    kchunks = [(c * 512, 512) for c in range(4)] + [(2048, 128)]
                    import os
                    KV_ = int(os.environ.get("KDBG_KV", "99"))
                    for g in range(4 if KV_ >= 1 else 0):
                        for (t0, T) in kchunks:
                            tl = list(range(t0 // 128, (t0 + T) // 128))
                            for k in range(8):
                                op("pe", dfr(PE.matmul, banks[0][:, 0:T], lhsT=WB[:, k, g * 128:(g + 1) * 128],
                                                           rhs=xnT[:, k, t0:t0 + T], start=(k == 0), stop=(k == 7)),
                                   R=[WBr] + [xnTr[t] for t in tl], W=[bankr[0]], mark=(k == 7))
                            op("act", dfr(ACT.copy, out=KT[:, g, t0:t0 + T], in_=banks[0][:, 0:T]),
                               R=[bankr[0]], W=[KTr[t] for t in tl])
                    for tile in range(NT if KV_ >= 2 else 0):
                        gsl = slice(tile * 128, (tile + 1) * 128)
                        full = tile >= 15
                        c0 = 0 if full else 256
                        for k in range(8):
                            op("pe", dfr(PE.matmul, banks[1][:, c0:512], lhsT=xnT[:, k, gsl], rhs=WA[:, k, c0:512],
                                                       start=(k == 0), stop=(k == 7)),
                               R=[WAr, xnTr[tile]], W=[bankr[1]], mark=(k == 7))
                        op("dve", dfr(DVE.tensor_copy, out=V[:, tile, :, :].rearrange("p g h -> p (g h)"),
                                                          in_=banks[1][:, 256:512]), R=[bankr[1]], W=[Vr[tile]])
                        if full:
                            op("act", dfr(ACT.copy, out=kvf[:], in_=banks[1][:]), R=[bankr[1]], W=[kvfr])
                            if KV_ < 3:
                                pass
                            elif tile == 15:
                                dma("sp", kvp_out[:, :], kvf[:], R=[kvfr])
                            else:
                                for j in range(16):
                                    dma("sp", kvs_out[j, 120:128, :], kvf[8 * j:8 * j + 8, :], R=[kvfr])
                    if KV_ >= 4:
                        dma("sp", kvs_out[:, 0:120, :], cache[:, 8:128, :])
                    trb = banks[2][:].bitcast(BF16)[:, 0:512].rearrange("p (g w) -> p g w", g=4)
                    for j in range(16 if KV_ >= 5 else 0):
                        dma("sp", kc[:], cache[j, :, :], W=[kcr])
                        op("dve", dfr(DVE.tensor_copy,
                            out=kcb[:], in_=kc[:, 0:256].rearrange("p (g h) -> p g h", g=4).unsqueeze(2)
                            .to_broadcast([128, 4, 2, 64])), R=[kcr], W=[kcbr])
                        op("pool", dfr(POOL.tensor_copy, out=Vc[:, j, :, :].rearrange("p g h -> p (g h)"),
                                                            in_=kc[:, 256:512]), R=[kcr], W=[Vcr])
                        for g in range(4):
                            op("pe", dfr(PE.transpose, out=trb[:, g, :],
                                                          in_=kcb[:, g, :, :].rearrange("p d h -> p (d h)"),
                                                          identity=ident[:]),
                               R=[kcbr, cr], W=[bankr[2]], mark=(g == 3))
                        op("act", dfr(ACT.copy, out=KcT[:, j, :, :], in_=trb), R=[bankr[2]], W=[KcTr])
                    tk.barrier()

                if n_b == 0:
                    tk.barrier()
                    return
                QTs = [hb("QT%d" % i, [128, 4, TS], BF16) for i in range(2)]
                gsTs = [hb("gsT%d" % i, [128, 4, TS], BF16) for i in range(2)]
                QTrs = [[Res("QT%d" % i) for i in range(4)] for _ in range(2)]
                gsTrs = [[Res("gsT%d" % i) for i in range(4)] for _ in range(2)]
                pTs = [[hb("pT%d_%d" % (q, i), [128, 2, 2, 128], BF16) for i in range(2)] for q in range(2)]
                pTrs = [[Res("pT%d_%d" % (q, i)) for i in range(2)] for q in range(2)]
                pc = [hb("pc%d" % i, [128, 16, 4, 8], BF16) for i in range(2)]
                pcr = [Res("pc%d" % i) for i in range(2)]
                t1 = hb("t1", [128, 4, 128], F32)
                t2 = hb("t2", [128, 4, 128], F32)
                t1r, t2r = Res("t1"), Res("t2")
                ogT = hb("ogTb", [128, 4, 128], BF16)
                ogTr = Res("ogTb")
                sTbs = [[banks[2], banks[3]], [banks[2], banks[3]]]
                sTrs = [[bankr[2], bankr[3]], [bankr[2], bankr[3]]]
                tk.filler = lambda: dfr(PE.matmul, banks[6][:, 0:128], lhsT=ident[:], rhs=ident[:], start=True, stop=True)
                oT_ps = banks[4][:].rearrange("p (r t) -> p r t", r=4)
                dn_ps = banks[5][:].rearrange("p (r t) -> p r t", r=4)
                oTr, dnr = bankr[4], bankr[5]
                y_ps, y_psr = banks[7], bankr[7]
                pctr = [0]
                swc = [0]

                def load_w(jl, g):
                    wv = w_in_b[jl].rearrange("(k p) n -> p k n", p=128)
                    dma("pool", WA[:], wv[:, :, g * 512:(g + 1) * 512], W=[WAr], key="bq")
                    dma("pool", WB[:], wv[:, :, 2048 + g * 512:2048 + (g + 1) * 512], W=[WBr], key="bg")
                    dma("pool", WO[:], w_out_b[jl][g * 512:(g + 1) * 512, :].rearrange("(r p) n -> p r n", p=128),
                        W=[WOr], key="bo")

                def sw_a(g, t0, T, cs_):
                    QT, gsT, QTr, gsTr = QTs[cs_], gsTs[cs_], QTrs[cs_], gsTrs[cs_]
                    tl = [xnTr[t] for t in range(t0 // 128, (t0 + T) // 128)]
                    for pr in range(4):
                        for k in range(8):
                            op("pe", dfr(PE.matmul, banks[0][:, 0:T], lhsT=WA[:, k, pr * 128:(pr + 1) * 128],
                                                       rhs=xnT[:, k, t0:t0 + T], start=(k == 0), stop=(k == 7)),
                               R=[WAr] + tl, W=[bankr[0]], mark=(k == 7), n=T)
                        op("act", dfr(ACT.activation, out=QT[:, pr, 0:T], in_=banks[0][:, 0:T], func=AF.Copy,
                                                         scale=0.125), R=[bankr[0]], W=[QTr[pr]], n=T)
                        for k in range(8):
                            op("pe", dfr(PE.matmul, banks[0][:, 256:256 + T], lhsT=WB[:, k, pr * 128:(pr + 1) * 128],
                                                       rhs=xnT[:, k, t0:t0 + T], start=(k == 0), stop=(k == 7)),
                               R=[WBr] + tl, W=[bankr[0]], mark=(k == 7), n=T)
                        op("act", dfr(ACT.activation, out=gsT[:, pr, 0:T], in_=banks[0][:, 256:256 + T], func=AF.Tanh,
                                                         scale=0.5), R=[bankr[0]], W=[gsTr[pr]], n=T)
                        op("dve", dfr(DVE.scalar_tensor_tensor, out=gsT[:, pr, 0:T], in0=gsT[:, pr, 0:T], scalar=1.0,
                                                                   in1=banks[0][:, 256:256 + T], op0=ALU.add,
                                                                   op1=ALU.mult),
                           R=[gsTr[pr], bankr[0]], W=[gsTr[pr]], n=T)

                def sw_b(g, t0, T, cs_):
                    QT, gsT, QTr, gsTr = QTs[cs_], gsTs[cs_], QTrs[cs_], gsTrs[cs_]
                    for ti in range(T // 128):
                        tile = t0 // 128 + ti
                        tsl = slice(ti * 128, (ti + 1) * 128)
                        samp = tile == 16
                        kts = [1] if (tile == 0 or samp) else [0, 1]
                        if samp:
                            sTb, sTr = sTbs[0], sTrs[0]
                            for par in range(2):
                                ps = slice(par * 64, par * 64 + 64)
                                scb = sTb[par][:].rearrange("p (j r t) -> p j r t", j=16, r=4)
                                for j in range(16):
                                    for pr in range(4):
                                        op("pe", dfr(PE.matmul, scb[:, j, pr, :], lhsT=KcT[ps, j, g, :],
                                                                   rhs=QT[ps, pr, 8 * j:8 * j + 8], start=True,
                                                                   stop=True),
                                           R=[KcTr, QTr[pr]], W=[sTr[par]], mark=(j == 15 and pr == 3), pm=1, n=64)
                            for par in range(2):
                                op("act", dfr(ACT.activation, out=pc[par][:].rearrange("p j r t -> p (j r t)"),
                                                                 in_=sTb[par][:], func=AF.Exp),
                                   R=[sTr[par]], W=[pcr[par]])
                                op("pool", dfr(POOL.tensor_tensor,
                                    out=pc[par][:].rearrange("p j r t -> p (j r) t"),
                                    in0=pc[par][:].rearrange("p j r t -> p (j r) t"),
                                    in1=maskc[:].unsqueeze(1).to_broadcast([128, 64, 8]), op=ALU.mult),
                                   R=[pcr[par], cr], W=[pcr[par]])
                        for pp in range(2):
                            sTb, sTr, pT, pTr = sTbs[pp], sTrs[pp], pTs[pp], pTrs[pp]
                            sTv = [sTb[par][:].rearrange("p (a k t) -> p a k t", a=2, k=2) for par in range(2)]
                            for a in range(2):
                                pr = 2 * pp + a
                                for par in range(2):
                                    ps = slice(par * 64, par * 64 + 64)
                                    for kt_i in kts:
                                        ktile = tile - 1 + kt_i
                                        op("pe", dfr(PE.matmul, sTv[par][:, a, kt_i, :],
                                                                   lhsT=KT[ps, g, ktile * 128:(ktile + 1) * 128],
                                                                   rhs=QT[ps, pr, tsl], start=True, stop=True),
                                           R=[KTr[ktile], QTr[pr]], W=[sTr[par]],
                                           mark=(a == 1 and kt_i == 1), pm=1, n=128)
                            k0 = kts[0]
                            for par in range(2):
                                op("act", dfr(ACT.activation, out=pT[par][:, :, k0:2, :], in_=sTv[par][:, :, k0:2, :],
                                                                 func=AF.Exp), R=[sTr[par]], W=[pTr[par]])
                                if samp:
                                    mk = mask_s[:].unsqueeze(1).to_broadcast([128, 2, 128])
                                    op("pool", dfr(POOL.tensor_tensor, out=pT[par][:, :, 1, :],
                                                                          in0=pT[par][:, :, 1, :], in1=mk, op=ALU.mult),
                                       R=[pTr[par], cr], W=[pTr[par]])
                                elif len(kts) == 2 and False:
                                    mk = mask2[:, :, :].unsqueeze(1).to_broadcast([128, 2, 2, 128])
                                    op("pool", dfr(POOL.tensor_tensor, out=pT[par][:], in0=pT[par][:], in1=mk,
                                                   op=ALU.mult), R=[pTr[par], cr], W=[pTr[par]])
                                else:
                                    for kt_i in kts:
                                        mk = mask2[:, kt_i, :].unsqueeze(1).to_broadcast([128, 2, 128])
                                        op("pool", dfr(POOL.tensor_tensor, out=pT[par][:, :, kt_i, :],
                                                                              in0=pT[par][:, :, kt_i, :], in1=mk,
                                                                              op=ALU.mult),
                                           R=[pTr[par], cr], W=[pTr[par]])
                            for a in range(2):
                                pr = 2 * pp + a
                                for par in range(2):
                                    ps = slice(par * 64, par * 64 + 64)
                                    for (dst, dres, use_v) in ((oT_ps, oTr, True), (dn_ps, dnr, False)):
                                        n_mm = len(kts) + (16 if samp else 0)
                                        cnt = 0
                                        for kt_i in kts:
                                            ktile = tile - 1 + kt_i
                                            cnt += 1
                                            lh = V[:, ktile, g, :] if use_v else ones64[:]
                                            op("pe", dfr(PE.matmul, dst[ps, pr, :], lhsT=lh,
                                                                       rhs=pT[par][:, a, kt_i, :],
                                                                       start=(cnt == 1), stop=(cnt == n_mm)),
                                               R=[Vr[ktile], pTr[par], cr], W=[dres], pm=2, n=128)
                                        if samp:
                                            for j in range(16):
                                                cnt += 1
                                                lh = Vc[:, j, g, :] if use_v else ones64[:]
                                                op("pe", dfr(PE.matmul, dst[ps, pr, 8 * j:8 * j + 8], lhsT=lh,
                                                                           rhs=pc[par][:, j, pr, :],
                                                                           start=False, stop=(cnt == n_mm)),
                                                   R=[Vcr, pcr[par], cr], W=[dres], pm=2, n=64)
                        esb_ = esp[:, g * 4:(g + 1) * 4].unsqueeze(2).to_broadcast([128, 4, 128])
                        op("dve", dfr(DVE.scalar_tensor_tensor, out=t1[:], in0=dn_ps, scalar=2.0, in1=esb_,
                                                                   op0=ALU.mult, op1=ALU.add),
                           R=[dnr, esr], W=[t1r], n=512)
                        op("dve", dfr(DVE.reciprocal, out=t1[:], in_=t1[:]), R=[t1r], W=[t1r])
                        op("dve", dfr(DVE.tensor_tensor, out=t2[:], in0=oT_ps, in1=t1[:], op=ALU.mult),
                           R=[oTr, t1r], W=[t2r])
                        op("pool", dfr(POOL.tensor_tensor, out=ogT[:], in0=t2[:], in1=gsT[:, :, tsl], op=ALU.mult),
                           R=[t2r] + gsTr, W=[ogTr])
                        for half in range(2):
                            hs_ = slice(half * 512, (half + 1) * 512)
                            for pr in range(4):
                                op("pe", dfr(PE.matmul, y_ps[:], lhsT=ogT[:, pr, :], rhs=WO[:, pr, hs_],
                                                           start=(pr == 0), stop=(pr == 3)),
                                   R=[ogTr, WOr], W=[y_psr], mark=(pr == 3))
                            op("dve", dfr(DVE.tensor_tensor, out=X[:, tile, hs_], in0=y_ps[:], in1=X[:, tile, hs_],
                                                                op=ALU.add),
                               R=[y_psr, Xr[tile]], W=[Xr[tile]])

                chunks = [(c * TS, TS) for c in range(2048 // TS)] + [(2048, 128)]
                for jl in range(n_b):
                    norm_phase(norm_b[jl])
                    dma("sp", esraw[:], sinks_b[jl].partition_broadcast(128), W=[esr])
                    op("act", dfr(ACT.activation, out=esraw[:], in_=esraw[:], func=AF.Exp), R=[esr], W=[esr])
                    op("dve", dfr(DVE.tensor_scalar, out=esraw[:], in0=esraw[:], scalar1=2.0, scalar2=0.0,
                                                        op0=ALU.mult, op1=ALU.add), R=[esr], W=[esr])
                    ev = esraw[:].rearrange("p (r a) -> p r a", a=2)
                    op("dve", dfr(DVE.tensor_copy, out=esp[0:64, :], in_=ev[0:64, :, 0]), R=[esr], W=[esr])
                    op("dve", dfr(DVE.tensor_copy, out=esp[64:128, :], in_=ev[64:128, :, 1]), R=[esr], W=[esr])
                    for g in range(4):
                        load_w(jl, g)
                        for (t0, T) in chunks:
                            sw_a(g, t0, T, swc[0] % 2)
                            sw_b(g, t0, T, swc[0] % 2)
                            swc[0] += 1
                tk.barrier()

        if n_a == 2:
            swa_phase()

        def final_norm():
            tk.filler = None
            with ExitStack() as fs:
                yt = [sb("yt%d" % i, [128, 1024], F32, fs) for i in range(2)]
                ytr = [Res("yt%d" % i) for i in range(2)]
                dma("sp", gB[:], norm_f.partition_broadcast(128), W=[gBr])
                for ti in range(NT):
                    op("act", dfr(ACT.activation, out=xtmp[0][:], in_=X[:, ti, :], func=AF.Square,
                                                     accum_out=ssq[:, ti:ti + 1]),
                       R=[Xr[ti]], W=[xtmpr[0], ssqr])
                op("dve", dfr(DVE.tensor_scalar, out=rstd[:], in0=ssq[:], scalar1=1.0 / 1024, scalar2=EPS,
                                                    op0=ALU.mult, op1=ALU.add), R=[ssqr], W=[rstdr])
                op("pool", dfr(POOL.tensor_tensor, out=rstd[:], in0=rstd[:], in1=mhalf[:, 0:NT], op=ALU.pow),
                   R=[rstdr, cr], W=[rstdr])
                ypv = y_p.rearrange("(n p) d -> p n d", p=128)
                for ti in range(NT):
                    b = ti % 2
                    op("dve", dfr(DVE.scalar_tensor_tensor, out=yt[b][:], in0=X[:, ti, :],
                                                               scalar=rstd[:, ti:ti + 1], in1=gB[:],
                                                               op0=ALU.mult, op1=ALU.mult),
                       R=[Xr[ti], rstdr, gBr], W=[ytr[b]])
                    if ti < 16:
                        dma("sp", ypv[:, ti, :], yt[b][:], R=[ytr[b]])
                    else:
                        dma("sp", y_s[:, :], yt[b][:], R=[ytr[b]])
                tk.barrier()

        if not dbg or n_b == 2:
            final_norm()

        if dbg:
            xd = xdbg.rearrange("(n p) d -> p n d", p=128)
            for ti in range(NT):
                dma("sp", xd[:, ti, :], X[:, ti, :], R=[Xr[ti]])
        tk.finish()
    return nc


def _shard_inputs(inp):
    maps = []
    shared = {k: np.ascontiguousarray(inp[k], dtype=np.float32) for k in
              ("w_in_a", "w_out_a", "norm_a", "onorm_a", "lower_bounds_a", "norm_kv", "w_kv", "w_in_b",
               "w_out_b", "norm_b", "sinks_b", "norm_f")}
    for c in range(NCORES):
        m = dict(shared)
        m["x_p"] = np.ascontiguousarray(inp["x_prompt"][c], dtype=np.float32)
        m["x_s"] = np.ascontiguousarray(inp["x_sample"][16 * c:16 * c + 16].reshape(128, 1024), dtype=np.float32)
        m["st_in"] = np.ascontiguousarray(inp["state_hgrn"][:, 16 * c:16 * c + 16], dtype=np.float32)
        m["cache"] = np.ascontiguousarray(inp["cache_kv_window"][16 * c:16 * c + 16].reshape(16, 128, 512),
                                          dtype=np.float32)
        maps.append(m)
    return maps


def kernel(**inputs):
    nc = build()
    maps = _shard_inputs(inputs)
    res = run_bass_kernel_spmd(nc, maps, core_ids=list(range(NCORES)))
    r = res.results
    y_prompt = np.stack([r[c]["y_p"] for c in range(NCORES)], axis=0)
    y_sample = np.concatenate([r[c]["y_s"].reshape(16, 8, 1024) for c in range(NCORES)], axis=0)
    sp = np.stack([r[c]["sp_out"] for c in range(NCORES)], axis=1)
    ss = np.concatenate([r[c]["ss_out"] for c in range(NCORES)], axis=1)
    kvp = np.stack([r[c]["kvp_out"].reshape(128, 2, 4, 64) for c in range(NCORES)], axis=0)
    kvs = np.concatenate([r[c]["kvs_out"].reshape(16, 128, 2, 4, 64) for c in range(NCORES)], axis=0)
    return (y_prompt, y_sample, sp, ss, kvp, kvs)
```

```python
import bisect
from contextlib import ExitStack

import numpy as np
import concourse.bass as bass
import concourse.mybir as mybir
from concourse.bass_utils import run_bass_kernel_spmd

F32 = mybir.dt.float32
BF16 = mybir.dt.bfloat16
AF = mybir.ActivationFunctionType
ALU = mybir.AluOpType
AX = mybir.AxisListType

NCORES = 8
NT = 17
TOK = NT * 128
EPS = 1e-6
QSCALE = 128 ** -0.5
SEM_LIMIT = 30000
NDS = 24


class Res:
    __slots__ = ("name", "w", "rc", "rd", "excl", "also")

    def __init__(self, name, excl=False):
        self.name = name
        self.excl = excl
        self.also = ()
        self.w = None
        self.rc = {}
        self.rd = []


def dfr(f, *a, **k):
    fn = lambda: f(*a, **k)
    o = k.get("out", a[0] if a else None)
    try:
        n = 1
        for d in o.shape[1:]:
            n *= int(d)
        fn.n = n
    except Exception:
        fn.n = None
    fn.nm = getattr(f, "__name__", "")
    return fn


class Eng:
    def __init__(self, name, h):
        self.name = name
        self.h = h
        self.sem = None
        self.cnt = 0
        self.pos = 0
        self.last = None
        self.last_marked = True
        self.mpos = []
        self.mtk = []
        self.waited = {}


class OpRec:
    __slots__ = ("id", "en", "fn", "deps", "odeps", "dma", "mark", "dur", "lat", "key", "pm")


SCHED = True


class Trk:
    def __init__(self, nc, es):
        self.nc = nc
        self.es = es
        self.E = {n: Eng(n, h) for n, h in [("pe", nc.tensor), ("act", nc.scalar), ("dve", nc.vector),
                                            ("pool", nc.gpsimd), ("sp", nc.sync)]}
        self.sems = []
        self.dsem = [self._newsem("dma%d" % i) for i in range(NDS)]
        self.dval = [0] * NDS
        self.dnext = 0
        self.swn = 0
        self.swlast = {}
        self.nid = 0
        self.pending = []
        self.filler = None
        self._engof = {}
        self.done = {}

    def _newsem(self, name):
        s = self.es.enter_context(self.nc.semaphore(name))
        self.sems.append(s)
        return len(self.sems) - 1

    def _record(self, en, fn, R, W, is_dma, mark, dur, lat, key=None, pm=0):
        deps, odeps = set(), set()
        W = list(W)
        for w in list(W):
            for x in w.also:
                if x not in W:
                    W.append(x)
        same_skip = (not is_dma) and en == "pe"
        for r in R:
            if r.w is not None:
                deps.add(r.w)
            if r.excl:
                for e2, oid in r.rc.items():
                    if e2 != en:
                        deps.add(oid)
        for w in W:
            if w.w is not None:
                (odeps if (same_skip and self._eng_of(w.w) == en) else deps).add(w.w)
            for e2, oid in w.rc.items():
                (odeps if (same_skip and e2 == en) else deps).add(oid)
            deps.update(w.rd)
        o = OpRec()
        o.id, o.en, o.fn, o.dma, o.mark, o.dur, o.lat, o.key = self.nid, en, fn, is_dma, mark, dur, lat, key
        o.pm = pm
        self.nid += 1
        for r in R:
            if is_dma:
                r.rd.append(o.id)
                if len(r.rd) > 12:
                    deps.add(r.rd[0])
                    r.rd = r.rd[1:]
            else:
                prev = r.rc.get(en)
                if prev is not None:
                    odeps.add(prev)
                r.rc[en] = o.id
        for w in W:
            w.w = o.id
            w.rc = {}
            w.rd = []
        o.deps = [d for d in deps if d != o.id]
        o.odeps = [d for d in odeps if d != o.id and d not in deps]
        self._engof[o.id] = en
        self.pending.append(o)
        return o

    def _eng_of(self, oid):
        return self._engof.get(oid)

    def op(self, en, fn, R=(), W=(), mark=None, n=256, pm=0):
        if mark is None:
            mark = en != "pe"
        n = getattr(fn, "n", None) or n
        nm = getattr(fn, "nm", "")
        if en == "pe":
            dur = max(64, n) / 1.95 + 10
        elif en == "act":
            dur = 230 + 0.75 * n
        elif en == "dve":
            f_ = 2.0 if "scan" in nm else (6.5 if nm == "reciprocal" else 1.0)
            dur = (150 + f_ * n) / 0.96
        else:
            dur = 1000.0 if n <= 32 else 450 + 1.3 * n
        self._record(en, fn, R, W, False, mark, dur, 0.0, pm=pm)

    def dma(self, qn, out, in_, R=(), W=(), key=None, nbytes=262144, **kw):
        issue = 1000.0 if qn == "pool" else 60.0
        fn = lambda: self.E[qn].h.dma_start(out=out, in_=in_, **kw)
        self._record(qn, fn, R, W, True, False, issue, 2000.0 + nbytes / 150.0, key)

    def _schedule(self, ops):
        if not SCHED:
            return ops
        import heapq
        idx = {o.id: i for i, o in enumerate(ops)}
        nd = [0] * len(ops)
        succ = [[] for _ in ops]
        ready_t = [0.0] * len(ops)
        for i, o in enumerate(ops):
            for d in o.deps + o.odeps:
                j = idx.get(d)
                if j is not None:
                    nd[i] += 1
                    succ[j].append(i)
        def hk(o):
            return ("pe", o.pm) if o.en == "pe" else o.en
        heaps = {e: [] for e in self.E if e != "pe"}
        for m in range(3):
            heaps[("pe", m)] = []
        for i, o in enumerate(ops):
            if nd[i] == 0:
                heapq.heappush(heaps[hk(o)], (0.0, i))
        free = {e: 0.0 for e in self.E}
        fin = [0.0] * len(ops)
        order = []
        nleft = len(ops)
        import os
        WIN = int(os.environ.get('K_WIN', '6000'))
        XLAT = float(os.environ.get('K_XLAT', '120'))
        lo = 0
        sched = [False] * len(ops)
        pe_mode = 0
        SWITCH = float(os.environ.get('K_SWITCH', '300'))
        rnow = {e: [] for e in heaps}
        POL = int(os.environ.get('K_POL', '1'))
        FILL = float(os.environ.get('K_FILL', '500'))
        if POL == 2:
            rank = [0.0] * len(ops)
            for i in range(len(ops) - 1, -1, -1):
                m = 0.0
                for j in succ[i]:
                    if rank[j] > m:
                        m = rank[j]
                rank[i] = ops[i].dur + ops[i].lat + m
            prio = [-(rank[i]) for i in range(len(ops))]
        else:
            prio = list(range(len(ops)))

        def engname(e):
            return "pe" if isinstance(e, tuple) else e

        while nleft:
            best = None
            for e, h in heaps.items():
                en = engname(e)
                rn = rnow[e]
                while h and h[0][0] <= free[en]:
                    rt, i = heapq.heappop(h)
                    heapq.heappush(rn, (prio[i], i, rt))
                if rn:
                    _, i, rt = rn[0]
                    st = free[en]
                elif h:
                    rt, i = h[0]
                    st = max(free[en], rt)
                else:
                    continue
                if en == "pe" and e[1] != pe_mode:
                    st += SWITCH
                if best is None or (st, i) < (best[0], best[1]):
                    best = (st, i, e)
            st, i, e = best
            en = engname(e)
            if rnow[e] and rnow[e][0][1] == i:
                heapq.heappop(rnow[e])
            else:
                heapq.heappop(heaps[e])
            if en == "pe":
                pe_mode = e[1]
            o = ops[i]
            st = max(st, free[en])
            if en == "pe" and self.filler is not None and FILL > 0 and st - free[en] >= FILL:
                kfill = min(int((st - free[en]) / float(os.environ.get("K_FD", "60"))), int(os.environ.get("K_FMAX", "40")))
                for _ in range(kfill):
                    fo = OpRec()
                    fo.id, fo.en, fo.fn, fo.deps, fo.odeps, fo.dma, fo.mark = -1, "pe", self.filler(), [], [], False, False
                    fo.dur, fo.lat, fo.key, fo.pm = 60.0, 0.0, None, 0
                    order.append(fo)
            free[en] = st + o.dur
            fin[i] = st + o.dur + o.lat
            order.append(o)
            nleft -= 1
            for j in succ[i]:
                nd[j] -= 1
                lat = XLAT if ops[j].en != en else 40.0
                ready_t[j] = max(ready_t[j], fin[i] + lat)
                if nd[j] == 0:
                    heapq.heappush(heaps[hk(ops[j])], (ready_t[j], j))
        return order

    def _mark_last(self, F):
        if F.last_marked:
            return F.mtk[-1]
        if F.sem is None or F.cnt >= SEM_LIMIT:
            F.sem = self._newsem("%s_%d" % (F.name, len(self.sems)))
            F.cnt = 0
        F.cnt += 1
        F.last.then_inc(self.sems[F.sem], 1)
        t = (F.sem, F.cnt)
        F.mpos.append(F.pos)
        F.mtk.append(t)
        F.last_marked = True
        return t

    def _resolve(self, oid):
        d = self.done[oid]
        if d[0] == "t":
            return (d[1], d[2])
        F = self.E[d[1]]
        i = bisect.bisect_left(F.mpos, d[2])
        if i < len(F.mpos):
            return F.mtk[i]
        assert F.pos >= d[2]
        return self._mark_last(F)

    def _wait(self, F, tk):
        s, v = tk
        if F.waited.get(s, 0) >= v:
            return
        F.h.wait_ge(self.sems[s], v)
        F.waited[s] = v

    def flush(self):
        ops = self.pending
        self.pending = []
        for o in self._schedule(ops):
            F = self.E[o.en]
            if o.id == -1:
                F.last = o.fn()
                F.pos += 1
                F.last_marked = False
                continue
            for d in o.deps:
                self._wait(F, self._resolve(d))
            if o.dma:
                if o.en == "pool":
                    self.swn += 1
                    si = self._newsem("sw%d" % self.swn)
                    self.swlast[o.key] = si
                    ins = o.fn()
                    ins.then_inc(self.sems[si], 16)
                    self.done[o.id] = ("t", si, 16)
                else:
                    i = self.dnext
                    self.dnext = (i + 1) % NDS
                    if self.dval[i] > 0:
                        self._wait(F, (self.dsem[i], self.dval[i]))
                    ins = o.fn()
                    ins.then_inc(self.sems[self.dsem[i]], 16)
                    self.dval[i] += 16
                    self.done[o.id] = ("t", self.dsem[i], self.dval[i])
            else:
                ins = o.fn()
                F.pos += 1
                F.last = ins
                F.last_marked = False
                if o.mark:
                    self._mark_last(F)
                self.done[o.id] = ("c", o.en, F.pos)

    def barrier(self):
        self.flush()
        tickets = []
        for n in ("pe", "act", "dve", "pool"):
            F = self.E[n]
            if F.last is not None:
                tickets.append(self._mark_last(F))
        for i in range(NDS):
            if self.dval[i] > 0:
                tickets.append((self.dsem[i], self.dval[i]))
        for si in self.swlast.values():
            tickets.append((si, 16))
        for n in ("pe", "act", "dve", "pool", "sp"):
            for t in tickets:
                self._wait(self.E[n], t)

    def finish(self):
        self.barrier()


def build(n_a=2, n_b=2, dbg=False):
    nc = bass.Bass("TRN2", target_bir_lowering=False)

    def din(name, shape):
        return nc.dram_tensor(name, shape, F32, kind="ExternalInput").ap()

    def dout(name, shape):
        return nc.dram_tensor(name, shape, F32, kind="ExternalOutput").ap()

    x_p = din("x_p", [2048, 1024])
    x_s = din("x_s", [128, 1024])
    st_in = din("st_in", [2, 16, 16, 128, 128])
    cache = din("cache", [16, 128, 512])
    w_in_a = din("w_in_a", [2, 1024, 8192])
    w_out_a = din("w_out_a", [2, 2048, 1024])
    norm_a = din("norm_a", [2, 1024])
    onorm_a = din("onorm_a", [2, 128])
    lb_a = din("lower_bounds_a", [2, 2048])
    norm_kv = din("norm_kv", [1024])
    w_kv = din("w_kv", [1024, 512])
    w_in_b = din("w_in_b", [2, 1024, 4096])
    w_out_b = din("w_out_b", [2, 2048, 1024])
    norm_b = din("norm_b", [2, 1024])
    sinks_b = din("sinks_b", [2, 32])
    norm_f = din("norm_f", [1024])

    y_p = dout("y_p", [2048, 1024])
    y_s = dout("y_s", [128, 1024])
    sp_out = dout("sp_out", [2, 16, 128, 128])
    ss_out = dout("ss_out", [2, 16, 16, 128, 128])
    kvp_out = dout("kvp_out", [128, 512])
    kvs_out = dout("kvs_out", [16, 128, 512])
    xdbg = dout("xdbg", [TOK, 1024]) if dbg else None

    with ExitStack() as es:
        tk = Trk(nc, es)
        op, dma = tk.op, tk.dma
        PE, ACT, DVE, POOL = nc.tensor, nc.scalar, nc.vector, nc.gpsimd

        def sb(name, shape, dt, stack=es):
            return stack.enter_context(nc.sbuf_tensor(name, shape, dt))

        banks = [es.enter_context(nc.psum_tensor("bank%d" % i, [128, 512], F32)) for i in range(8)]
        bankr = [Res("bank%d" % i, excl=True) for i in range(8)]

        X = sb("X", [128, NT, 1024], F32)
        Xr = [Res("X%d" % i) for i in range(NT)]
        xnT = sb("xnT", [128, 8, TOK], BF16)
        xnTr = [Res("xnT%d" % i) for i in range(NT)]
        ident = sb("ident", [128, 128], BF16)
        identf = sb("identf", [128, 128], F32)
        gB = sb("gB", [128, 1024], F32)
        gBr = Res("gB")
        xtmp = [sb("xtmp%d" % i, [128, 1024], BF16) for i in range(2)]
        xtmpr = [Res("xtmp%d" % i) for i in range(2)]
        ssq = sb("ssq", [128, NT], F32)
        ssqr = Res("ssq")
        rstd = sb("rstd", [128, NT], F32)
        rstdr = Res("rstd")
        mhalf = sb("mhalf", [128, 32], F32)
        mask_p = sb("mask_p", [128, 128], BF16)
        mask_s = sb("mask_s", [128, 128], BF16)
        Eind = sb("Eind", [16, 2, 128], BF16)
        cr = Res("consts")

        def setup_consts():
            op("pool", dfr(POOL.memset, ident[:], 1.0), W=[cr])
            op("pool", dfr(POOL.affine_select, out=ident[:], in_=ident[:], pattern=[[-1, 128]],
                                                  compare_op=ALU.is_equal, fill=0.0, base=0,
                                                  channel_multiplier=1), R=[cr], W=[cr])
            op("pool", dfr(POOL.memset, identf[:], 1.0), W=[cr])
            op("pool", dfr(POOL.affine_select, out=identf[:], in_=identf[:], pattern=[[-1, 128]],
                                                  compare_op=ALU.is_equal, fill=0.0, base=0,
                                                  channel_multiplier=1), R=[cr], W=[cr])
            op("pool", dfr(POOL.memset, mhalf[:], -0.5), W=[cr])
        op("pool", dfr(POOL.memset, Eind[:], 1.0), W=[cr])
        for slot, cs in ((0, 64), (1, 8)):
            op("pool", dfr(POOL.affine_select, out=Eind[:, slot, :], in_=Eind[:, slot, :],
                                                  pattern=[[1, 128]], compare_op=ALU.is_ge, fill=0.0,
                                                  base=0, channel_multiplier=-cs), R=[cr], W=[cr])
            op("pool", dfr(POOL.affine_select, out=Eind[:, slot, :], in_=Eind[:, slot, :],
                                                  pattern=[[-1, 128]], compare_op=ALU.is_ge, fill=0.0,
                                                  base=cs - 1, channel_multiplier=cs), R=[cr], W=[cr])
        for slot, m in ((0, mask_p), (1, mask_s)):
            op("pe", dfr(PE.matmul, banks[3][:, 0:128], lhsT=Eind[:, slot, :], rhs=Eind[:, slot, :],
                                       start=True, stop=True), R=[cr], W=[bankr[3]], mark=True)
            op("dve", dfr(DVE.tensor_copy, out=m[:], in_=banks[3][:, 0:128]), R=[bankr[3]], W=[cr])
            op("pool", dfr(POOL.affine_select, out=m[:], in_=m[:], pattern=[[1, 128]],
                                                  compare_op=ALU.is_ge, fill=0.0, base=0,
                                                  channel_multiplier=-1), R=[cr], W=[cr])

        setup_consts()

        xpv = x_p.rearrange("(n p) d -> p n d", p=128)
        for ti in range(16):
            dma("sp", X[:, ti, :], xpv[:, ti, :], W=[Xr[ti]])
        dma("sp", X[:, 16, :], x_s[:, :], W=[Xr[16]])

        def norm_phase(gvec):
            dma("sp", gB[:], gvec.partition_broadcast(128), W=[gBr])
            for ti in range(NT):
                b = ti % 2
                op("act", dfr(ACT.activation, out=xtmp[b][:], in_=X[:, ti, :], func=AF.Square,
                                                 accum_out=ssq[:, ti:ti + 1]),
                   R=[Xr[ti]], W=[xtmpr[b], ssqr])
            op("dve", dfr(DVE.tensor_scalar, out=rstd[:], in0=ssq[:], scalar1=1.0 / 1024, scalar2=EPS,
                                                op0=ALU.mult, op1=ALU.add), R=[ssqr], W=[rstdr])
            op("pool", dfr(POOL.tensor_tensor, out=rstd[:], in0=rstd[:], in1=mhalf[:, 0:NT], op=ALU.pow),
               R=[rstdr, cr], W=[rstdr])
            trv = banks[7][:].bitcast(BF16).rearrange("p (k t) -> p k t", k=8)
            for ti in range(NT):
                b = ti % 2
                op("dve", dfr(DVE.scalar_tensor_tensor, out=xtmp[b][:], in0=X[:, ti, :],
                                                           scalar=rstd[:, ti:ti + 1], in1=gB[:],
                                                           op0=ALU.mult, op1=ALU.mult),
                   R=[Xr[ti], rstdr, gBr], W=[xtmpr[b]])
                for k in range(8):
                    op("pe", dfr(PE.transpose, out=trv[:, k, :], in_=xtmp[b][:, k * 128:(k + 1) * 128],
                                                  identity=ident[:]),
                       R=[xtmpr[b], cr], W=[bankr[7]], mark=(k == 7))
                op("act", dfr(ACT.copy, out=xnT[:, :, ti * 128:(ti + 1) * 128], in_=trv),
                   R=[bankr[7]], W=[xnTr[ti]])

        TCH = 256

        def hgrn_phase():
            with ExitStack() as hs:
                def hb(name, shape, dt):
                    return sb(name, shape, dt, hs)

                Wq = hb("Wq", [128, 8, 512], BF16)
                Wf = hb("Wf", [128, 8, 512], BF16)
                Wi = hb("Wi", [128, 8, 512], BF16)
                Wg = hb("Wg", [128, 8, 512], BF16)
                Wo = hb("Wo", [128, 4, 1024], BF16)
                Wqr, Wfr, Wir, Wgr, Wor = (Res(n) for n in ("Wq", "Wf", "Wi", "Wg", "Wo"))
                praw = hb("praw", [34, 128], F32)
                prawr = Res("praw")
                pT = hb("pT", [128, 34], F32)
                pTr = Res("pT")
                lb = hb("lb", [128, 2, 16], F32)
                oml = hb("oml", [128, 2, 16], F32)
                lbr = Res("lb")
                tT = [[hb("t%s%d" % (n, i), [128, TCH], F32) for n in "ABCDE"] for i in range(2)]
                tTr = [[Res("t%s%d" % (n, i)) for n in "ABCDE"] for i in range(2)]
                hctr = [0]
                NSET = 2
                qh = [hb("qh%d" % s, [128, 4, TCH], BF16) for s in range(NSET)]
                qt = [hb("qt%d" % s, [128, 4, TCH], BF16) for s in range(NSET)]
                kh = [hb("kh%d" % s, [128, 4, TCH], BF16) for s in range(NSET)]
                dec = [hb("dec%d" % s, [128, 4, 16], F32) for s in range(NSET)]
                rl = [hb("rl%d" % s, [128, 4, 16], F32) for s in range(NSET)]
                rlr = [[Res("rl") for _ in range(4)] for _ in range(NSET)]
                emask_p = hb("emask_p", [128, TCH], F32)
                emask_s = hb("emask_s", [128, 128], F32)
                qhr = [[Res("qh") for _ in range(4)] for _ in range(NSET)]
                qtr = [[Res("qt") for _ in range(4)] for _ in range(NSET)]
                khr = [[Res("kh") for _ in range(4)] for _ in range(NSET)]
                decr = [[Res("dec") for _ in range(4)] for _ in range(NSET)]
                smask_p = hb("smask_p", [128, TCH], F32)
                smask_s = hb("smask_s", [128, 128], F32)
                sel = hb("sel", [128, 16, 16], BF16)
                selT = hb("selT", [128, 16], BF16)
                v_sb = hb("v_sb", [128, 512], BF16)
                sg = hb("sg", [128, 512], BF16)
                gs = hb("gs", [128, 512], BF16)
                khtok = hb("khtok", [128, 4, 128], BF16)
                attm = hb("attm", [128, 4, 128], BF16)
                oss = hb("oss", [128, 4], F32)
                orstd = hb("orstd", [128, 4], F32)
                og = hb("og", [128, 512], BF16)
                ogT = hb("ogT", [128, 4, 128], BF16)
                v_sbr, sgr, gsr, khtokr, attmr, ossr, orstdr, ogr, ogTr = (
                    Res(n) for n in ("v_sb", "sg", "gs", "khtok", "attm", "oss", "orstd", "og", "ogT"))
                oss1 = hb("oss1", [128, 4], F32)
                orstd1 = hb("orstd1", [128, 4], F32)
                S = hb("S", [128, 4, 128], F32)
                Sr = [Res("S%d" % h) for h in range(4)]
                snap = hb("snap", [128, 4, 4, 128], BF16)
                snapr = [[Res("snap") for _ in range(4)] for _ in range(4)]
                S0 = [hb("S0_%d" % i, [128, 8, 128], F32) for i in range(2)]
                S0r = [Res("S0_%d" % i) for i in range(2)]
                S0bf = hb("S0bf", [128, 8, 128], BF16)
                S0bfr = Res("S0bf")
                Qexp = hb("Qexp", [128, 8, 128], BF16)
                Qexpr = Res("Qexp")
                Vexp = hb("Vexp", [128, 8, 128], BF16)
                Vexpr = Res("Vexp")

                f0 = S0[0][:].rearrange("p j e -> p (j e)").bitcast(BF16)
                f1 = S0[1][:].rearrange("p j e -> p (j e)").bitcast(BF16)
                TB = [(v_sb, sg, gs, khtok, attm, oss, orstd, og, ogT),
                      (f0[:, 0:512], f0[:, 512:1024], f0[:, 1024:1536],
                       f0[:, 1536:2048].rearrange("p (h t) -> p h t", h=4),
                       f1[:, 0:512].rearrange("p (h t) -> p h t", h=4), oss1, orstd1,
                       f1[:, 512:1024], f1[:, 1024:1536].rearrange("p (h t) -> p h t", h=4))]
                TBR = [(v_sbr, sgr, gsr, khtokr, attmr, ossr, orstdr, ogr, ogTr),
                       tuple(Res(n + "1") for n in ("v_sb", "sg", "gs", "khtok", "attm", "oss", "orstd", "og", "ogT"))]
                for ri, r_ in enumerate(TBR[1]):
                    if ri in (0, 1, 2, 3):
                        r_.also = (S0r[0],)
                    elif ri in (4, 7, 8):
                        r_.also = (S0r[1],)
                S0r[0].also = tuple(TBR[1][i] for i in (0, 1, 2, 3))
                S0r[1].also = tuple(TBR[1][i] for i in (4, 7, 8))

                q_ps = banks[0][:, 0:TCH]
                f_ps = banks[0][:, 256:256 + TCH]
                qfr = bankr[0]
                v_ps, v_psr = banks[1], bankr[1]
                g_ps, g_psr = banks[2], bankr[2]
                att_ps = banks[3][:].rearrange("p (h t) -> p h t", h=4)
                att_psr = bankr[3]
                o_ps = banks[4][:].rearrange("p (h t) -> p h t", h=4)
                o_psr = bankr[4]
                b5 = banks[3][:].bitcast(BF16)
                kT_ps = b5[:, 0:512].rearrange("p (h t) -> p h t", h=4)
                ogT_ps = b5[:, 512:1024].rearrange("p (h t) -> p h t", h=4)
                kT_psr, ogT_psr = bankr[3], bankr[3]
                tk.filler = lambda: dfr(PE.matmul, banks[5][:, 0:128], lhsT=ident[:], rhs=ident[:], start=True, stop=True)
                kvb = banks[6][:].rearrange("p (h e) -> p h e", h=4)
                kv_psr = [bankr[6]]
                kv4_ps = banks[6][:].rearrange("p (j e) -> p j e", j=4)
                y_ps, y_psr = banks[7], bankr[7]

                def v3(ap, c):
                    return ap.rearrange("p (c t) -> p c t", t=c)

                op("pool", dfr(POOL.memset, smask_p[:], 0.0), W=[cr])
                op("pool", dfr(POOL.memset, v3(smask_p, 64)[:, :, 0:1], 1.0), W=[cr])
                op("pool", dfr(POOL.memset, emask_p[:], 0.0), W=[cr])
                op("pool", dfr(POOL.memset, v3(emask_p, 64)[:, :, 63:64], 1.0), R=[cr], W=[cr])
                op("pool", dfr(POOL.memset, emask_s[:], 0.0), W=[cr])
                op("pool", dfr(POOL.memset, v3(emask_s, 8)[:, :, 7:8], 1.0), R=[cr], W=[cr])
                op("pool", dfr(POOL.memset, smask_s[:], 0.0), W=[cr])
                op("pool", dfr(POOL.memset, v3(smask_s, 8)[:, :, 0:1], 1.0), W=[cr])
                op("pool", dfr(POOL.memset, sel[:], 1.0), W=[cr])
                op("pool", dfr(POOL.affine_select, out=sel[:], in_=sel[:], pattern=[[-1, 16], [1, 16]],
                                                      compare_op=ALU.is_equal, fill=0.0, base=0,
                                                      channel_multiplier=0), R=[cr], W=[cr])
                op("pool", dfr(POOL.memset, selT[:], 1.0), W=[cr])
                op("pool", dfr(POOL.affine_select, out=selT[:], in_=selT[:], pattern=[[-8, 16]],
                                                      compare_op=ALU.is_ge, fill=0.0, base=0,
                                                      channel_multiplier=1), R=[cr], W=[cr])
                op("pool", dfr(POOL.affine_select, out=selT[:], in_=selT[:], pattern=[[8, 16]],
                                                      compare_op=ALU.is_ge, fill=0.0, base=7,
                                                      channel_multiplier=-1), R=[cr], W=[cr])

                dma("sp", praw[0:32, :], lb_a.rearrange("l (h d) -> (l h) d", d=128), W=[prawr])
                dma("sp", praw[32:34, :], onorm_a[:, :], W=[prawr])
                op("pe", dfr(PE.transpose, out=banks[3][:, 0:34], in_=praw[:, :], identity=identf[0:34, 0:34]),
                   R=[prawr, cr], W=[bankr[3]], mark=True)
                op("dve", dfr(DVE.tensor_copy, out=pT[:], in_=banks[3][:, 0:34]), R=[bankr[3]], W=[pTr])
                op("dve", dfr(DVE.memset, lb[:, 0, :], 0.0), W=[lbr])
                op("dve", dfr(DVE.tensor_tensor, out=lb[:, 1, :], in0=pT[:, 16:32], in1=pT[:, 0:16],
                                                    op=ALU.subtract), R=[pTr], W=[lbr])
                op("act", dfr(ACT.activation, out=lb[:, 1, :], in_=lb[:, 1, :], func=AF.Sigmoid),
                   R=[lbr], W=[lbr])
                op("dve", dfr(DVE.tensor_scalar, out=oml[:], in0=lb[:], scalar1=-1.0, scalar2=1.0,
                                                    op0=ALU.mult, op1=ALU.add), R=[lbr], W=[lbr])

                chunks = [(c * TCH, TCH, False) for c in range(2048 // TCH)] + [(2048, 128, True)]

                def load_weights(l, g):
                    wv = w_in_a[l].rearrange("(k p) n -> p k n", p=128)
                    for typ, (Wt, Wr) in enumerate(((Wq, Wqr), (Wf, Wfr), (Wi, Wir), (Wg, Wgr))):
                        c0 = typ * 2048 + g * 512
                        dma("pool", Wt[:], wv[:, :, c0:c0 + 512], W=[Wr], key="Win%d" % typ)
                    wo = w_out_a[l][g * 512:(g + 1) * 512, :].rearrange("(h p) n -> p h n", p=128)
                    dma("pool", Wo[:], wo, W=[Wor], key="Wo")
                    op("pool", dfr(POOL.tensor_scalar, out=Wo[:], in0=Wo[:], scalar1=pT[:, 32 + l:33 + l],
                                                          scalar2=0.0, op0=ALU.mult, op1=ALU.add),
                       R=[Wor, pTr], W=[Wor])

                def stage_a(l, g, st, t0, T, samp):
                    cs = 8 if samp else 64
                    nch = T // cs
                    mid = cs // 2 - 1
                    smask = smask_s if samp else smask_p
                    tiles = list(range(t0 // 128, (t0 + T) // 128))
                    xr = [xnTr[t] for t in tiles]
                    for hl in range(4):
                        hg = g * 4 + hl
                        qv, fv = q_ps[:, 0:T], f_ps[:, 0:T]
                        for k in range(8):
                            op("pe", dfr(PE.matmul, qv, lhsT=Wq[:, k, hl * 128:(hl + 1) * 128],
                                                       rhs=xnT[:, k, t0:t0 + T], start=(k == 0), stop=(k == 7)),
                               R=[Wqr] + xr, W=[qfr])
                        for k in range(8):
                            op("pe", dfr(PE.matmul, fv, lhsT=Wf[:, k, hl * 128:(hl + 1) * 128],
                                                       rhs=xnT[:, k, t0:t0 + T], start=(k == 0), stop=(k == 7)),
                               R=[Wfr] + xr, W=[qfr], mark=(k == 7))
                        tb = hctr[0] % 2
                        hctr[0] += 1
                        A, B, C, Dd, Ee = (t_[:, 0:T] for t_ in tT[tb])
                        tAr, tBr, tCr, tDr, tEr = tTr[tb]
                        op("act", dfr(ACT.activation, out=A, in_=fv, func=AF.Sigmoid), R=[qfr], W=[tAr])
                        op("act", dfr(ACT.activation, out=B, in_=fv, func=AF.Sigmoid, scale=-1.0),
                           R=[qfr], W=[tBr])
                        op("act", dfr(ACT.activation, out=C, in_=qv, func=AF.Sigmoid), R=[qfr], W=[tCr])
                        op("dve", dfr(DVE.tensor_tensor, out=C, in0=qv, in1=C, op=ALU.mult),
                           R=[qfr, tCr], W=[tCr])
                        if l > 0:
                            op("act", dfr(ACT.activation, out=A, in_=A, func=AF.Identity,
                                          scale=oml[:, l, hg:hg + 1], bias=lb[:, l, hg:hg + 1]),
                               R=[tAr, lbr], W=[tAr])
                        op("pool", dfr(POOL.tensor_tensor, out=Dd, in0=A, in1=smask[:, 0:T], op=ALU.mult),
                           R=[tAr, cr], W=[tDr])
                        op("dve", dfr(DVE.tensor_tensor_scan, out=Ee, data0=A, data1=Dd, initial=1.0,
                                                                 op0=ALU.mult, op1=ALU.max),
                           R=[tAr, tDr], W=[tEr])
                        op("act", dfr(ACT.copy, out=Dd[:, 0:T - 1], in_=A[:, 1:T]), R=[tAr], W=[tDr])
                        op("pool", dfr(POOL.memset, v3(Dd, cs)[:, :, cs - 1:cs], 1.0), R=[tDr], W=[tDr])
                        emask = emask_s if samp else emask_p
                        op("dve", dfr(DVE.tensor_tensor_scan, out=A[:, ::-1], data0=Dd[:, ::-1],
                                                                 data1=emask[:, 0:T][:, ::-1], initial=1.0,
                                                                 op0=ALU.mult, op1=ALU.max),
                           R=[tDr, cr], W=[tAr])
                        E3 = v3(Ee, cs)
                        op("dve", dfr(DVE.scalar_tensor_tensor, out=kh[st][:, hl, 0:T], in0=B,
                                                                   scalar=oml[:, l, hg:hg + 1], in1=A,
                                                                   op0=ALU.mult, op1=ALU.mult),
                           R=[tBr, tAr, lbr], W=[khr[st][hl]])
                        op("dve", dfr(DVE.scalar_tensor_tensor, out=qh[st][:, hl, 0:T], in0=C, scalar=QSCALE,
                                                                   in1=Ee, op0=ALU.mult, op1=ALU.mult),
                           R=[tCr, tEr], W=[qhr[st][hl]])
                        op("dve", dfr(DVE.tensor_copy, out=dec[st][:, hl, 0:nch], in_=E3[:, :, cs - 1]),
                           R=[tEr], W=[decr[st][hl]])
                        op("dve", dfr(DVE.reciprocal, out=rl[st][:, hl, 0:nch], in_=dec[st][:, hl, 0:nch]),
                           R=[decr[st][hl]], W=[rlr[st][hl]])
                        op("pool", dfr(POOL.tensor_tensor,
                            out=v3(qt[st][:, hl, 0:T], cs), in0=v3(qh[st][:, hl, 0:T], cs),
                            in1=rl[st][:, hl, 0:nch].unsqueeze(2).to_broadcast([128, nch, cs]), op=ALU.mult),
                           R=[qhr[st][hl], rlr[st][hl]], W=[qtr[st][hl]])
                        yield

                def stage_b(l, g, st, t0, T, samp, kctr):
                    for ti in range(T // 128):
                        tile = t0 // 128 + ti
                        tsl = slice(ti * 128, (ti + 1) * 128)
                        gsl = slice(tile * 128, (tile + 1) * 128)
                        tp = 0
                        v_sb, sg, gs, khtok, attm, oss, orstd, og, ogT = TB[tp]
                        v_sbr, sgr, gsr, khtokr, attmr, ossr, orstdr, ogr, ogTr = TBR[tp]
                        osq, osqr = sg, sgr
                        for k in range(8):
                            op("pe", dfr(PE.matmul, v_ps[:], lhsT=xnT[:, k, gsl], rhs=Wi[:, k, :],
                                                       start=(k == 0), stop=(k == 7)),
                               R=[Wir, xnTr[tile]], W=[v_psr], mark=(k == 7))
                        for k in range(8):
                            op("pe", dfr(PE.matmul, g_ps[:], lhsT=xnT[:, k, gsl], rhs=Wg[:, k, :],
                                                       start=(k == 0), stop=(k == 7)),
                               R=[Wgr, xnTr[tile]], W=[g_psr], mark=(k == 7))
                        op("act", dfr(ACT.copy, out=v_sb[:], in_=v_ps[:]), R=[v_psr], W=[v_sbr])
                        op("act", dfr(ACT.activation, out=sg[:], in_=g_ps[:], func=AF.Sigmoid),
                           R=[g_psr], W=[sgr])
                        op("dve", dfr(DVE.tensor_tensor, out=gs[:], in0=g_ps[:], in1=sg[:], op=ALU.mult),
                           R=[g_psr, sgr], W=[gsr])
                        yield
                        for hl in range(4):
                            op("pe", dfr(PE.transpose, out=kT_ps[:, hl, :], in_=kh[st][:, hl, tsl],
                                                          identity=ident[:]),
                               R=[khr[st][hl], cr], W=[kT_psr], mark=(hl == 3))
                        op("act", dfr(ACT.copy, out=khtok[:], in_=kT_ps), R=[kT_psr], W=[khtokr])
                        for hl in range(4):
                            op("pe", dfr(PE.matmul, att_ps[:, hl, :], lhsT=kh[st][:, hl, tsl],
                                                       rhs=qt[st][:, hl, tsl], start=True, stop=True),
                               R=[khr[st][hl], qtr[st][hl]], W=[att_psr], mark=(hl == 3))
                        msk = mask_s if samp else mask_p
                        op("dve", dfr(DVE.tensor_tensor, out=attm[:], in0=att_ps,
                                                            in1=msk[:].unsqueeze(1).to_broadcast([128, 4, 128]),
                                                            op=ALU.mult), R=[att_psr, cr], W=[attmr])
                        vh = v_sb[:].rearrange("p (h e) -> p h e", h=4)
                        yield
                        if not samp:
                            k0 = kctr[0]
                            for c in range(2):
                                kk = k0 + c
                                sl64 = slice(64 * c, 64 * c + 64)
                                for hl in range(4):
                                    op("pe", dfr(PE.matmul, kvb[:, hl, :], lhsT=khtok[sl64, hl, :], rhs=vh[sl64, hl, :],
                                                               start=True, stop=True),
                                       R=[khtokr, v_sbr], W=kv_psr, mark=(hl == 3), pm=1, n=128)
                                for hl in range(4):
                                    dcol = dec[st][:, hl, 2 * ti + c:2 * ti + c + 1]
                                    if kk == 0:
                                        op("dve", dfr(DVE.tensor_copy, out=S[:, hl, :], in_=kvb[:, hl, :]),
                                           R=kv_psr, W=[Sr[hl]])
                                    else:
                                        op("dve", dfr(DVE.scalar_tensor_tensor,
                                            out=S[:, hl, :], in0=S[:, hl, :], scalar=dcol, in1=kvb[:, hl, :],
                                            op0=ALU.mult, op1=ALU.add),
                                           R=[Sr[hl], decr[st][hl]] + kv_psr, W=[Sr[hl]])
                                op("act", dfr(ACT.copy, out=snap[:, :, (kk + 1) % 4, :], in_=S[:, :, :]),
                                   R=Sr, W=[snapr[h_][(kk + 1) % 4] for h_ in range(4)])
                                yield
                            if tile == 15:
                                for hl in range(4):
                                    dma("sp", sp_out[l, g * 4 + hl, :, :], S[:, hl, :], R=[Sr[hl]])
                            for hl in range(4):
                                op("pe", dfr(PE.matmul, o_ps[:, hl, :], lhsT=attm[:, hl, :], rhs=vh[:, hl, :],
                                                           start=True, stop=False),
                                   R=[attmr, v_sbr], W=[o_psr])
                                for c in range(2):
                                    kk = k0 + c
                                    sl64 = slice(64 * c, 64 * c + 64)
                                    op("pe", dfr(PE.matmul,
                                        o_ps[sl64, hl, :], lhsT=qh[st][:, hl, ti * 128 + 64 * c:ti * 128 + 64 * c + 64],
                                        rhs=snap[:, hl, kk % 4, :], start=False, stop=True),
                                       R=[qhr[st][hl], snapr[hl][kk % 4]], W=[o_psr], pm=2, n=128)
                            for hl in range(4):
                                kctr[hl] += 2
                        else:
                            items = [(hl, hf) for hl in range(4) for hf in range(2)]

                            def s0_load(n):
                                hl_, hf_ = items[n]
                                dma("sp", S0[n % 2][:],
                                    st_in[l, 8 * hf_:8 * hf_ + 8, g * 4 + hl_, :, :].rearrange("j d e -> d j e"),
                                    W=[S0r[n % 2]])

                            s0_load(0)
                            for n, (hl, hf) in enumerate(items):
                                hg = g * 4 + hl
                                bi = n % 2
                                if n + 1 < len(items):
                                    s0_load(n + 1)
                                j0 = 8 * hf
                                op("act", dfr(ACT.copy, out=S0bf[:], in_=S0[bi][:]), R=[S0r[bi]], W=[S0bfr])
                                op("dve", dfr(DVE.tensor_tensor,
                                    out=Qexp[:].rearrange("p j (a u) -> p j a u", u=8),
                                    in0=qh[st][:, hl, 0:128].rearrange("p (a u) -> p a u", u=8).unsqueeze(1)
                                    .to_broadcast([128, 8, 16, 8]),
                                    in1=sel[:, j0:j0 + 8, :].unsqueeze(3).to_broadcast([128, 8, 16, 8]),
                                    op=ALU.mult), R=[qhr[st][hl], cr], W=[Qexpr])
                                op("dve", dfr(DVE.tensor_tensor,
                                    out=Vexp[:], in0=vh[:, hl, :].unsqueeze(1).to_broadcast([128, 8, 128]),
                                    in1=selT[:, j0:j0 + 8].unsqueeze(2).to_broadcast([128, 8, 128]), op=ALU.mult),
                                   R=[v_sbr, cr], W=[Vexpr])
                                if hf == 0:
                                    op("pe", dfr(PE.matmul, o_ps[:, hl, :], lhsT=attm[:, hl, :], rhs=vh[:, hl, :],
                                                               start=True, stop=False), R=[attmr, v_sbr], W=[o_psr])
                                for jj in range(8):
                                    op("pe", dfr(PE.matmul, o_ps[:, hl, :], lhsT=Qexp[:, jj, :], rhs=S0bf[:, jj, :],
                                                               start=False, stop=(hf == 1 and jj == 7)),
                                       R=[Qexpr, S0bfr], W=[o_psr])
                                for q4 in range(2):
                                    op("pe", dfr(PE.matmul, kv4_ps, lhsT=khtok[:, hl, :],
                                                               rhs=Vexp[:, 4 * q4:4 * q4 + 4, :], start=True, stop=True),
                                       R=[khtokr, Vexpr], W=kv_psr, mark=True)
                                    for jj in range(4):
                                        jl = 4 * q4 + jj
                                        op("dve", dfr(DVE.scalar_tensor_tensor,
                                            out=S0[bi][:, jl, :], in0=S0[bi][:, jl, :],
                                            scalar=dec[st][:, hl, j0 + jl:j0 + jl + 1],
                                            in1=kv4_ps[:, jj, :], op0=ALU.mult, op1=ALU.add),
                                           R=[S0r[bi], decr[st][hl]] + kv_psr, W=[S0r[bi]])
                                dma("sp", ss_out[l, j0:j0 + 8, hg, :, :].rearrange("j d e -> d j e"), S0[bi][:],
                                    R=[S0r[bi]])
                                yield
                        op("act", dfr(ACT.activation, out=osq[:], in_=banks[4][:], func=AF.Square),
                           R=[o_psr], W=[osqr])
                        op("dve", dfr(DVE.tensor_reduce, out=oss[:], in_=osq[:].rearrange("p (h e) -> p h e", h=4),
                                                            axis=AX.X, op=ALU.add), R=[osqr], W=[ossr])
                        op("dve", dfr(DVE.tensor_scalar, out=orstd[:], in0=oss[:], scalar1=1.0 / 128,
                                                            scalar2=EPS, op0=ALU.mult, op1=ALU.add),
                           R=[ossr], W=[orstdr])
                        op("pool", dfr(POOL.tensor_tensor, out=orstd[:], in0=orstd[:], in1=mhalf[:, 0:4],
                                                              op=ALU.pow), R=[orstdr, cr], W=[orstdr])
                        for hl in range(4):
                            op("dve", dfr(DVE.scalar_tensor_tensor,
                                out=og[:, hl * 128:(hl + 1) * 128], in0=o_ps[:, hl, :], scalar=orstd[:, hl:hl + 1],
                                in1=gs[:, hl * 128:(hl + 1) * 128], op0=ALU.mult, op1=ALU.mult),
                               R=[o_psr, orstdr, gsr], W=[ogr])
                        yield
                        for hl in range(4):
                            op("pe", dfr(PE.transpose, out=ogT_ps[:, hl, :], in_=og[:, hl * 128:(hl + 1) * 128],
                                                          identity=ident[:]),
                               R=[ogr, cr], W=[ogT_psr], mark=(hl == 3))
                        op("act", dfr(ACT.copy, out=ogT[:], in_=ogT_ps), R=[ogT_psr], W=[ogTr])
                        for half in range(2):
                            hs_ = slice(half * 512, (half + 1) * 512)
                            for hl in range(4):
                                op("pe", dfr(PE.matmul, y_ps[:], lhsT=ogT[:, hl, :], rhs=Wo[:, hl, hs_],
                                                           start=(hl == 0), stop=(hl == 3)),
                                   R=[ogTr, Wor], W=[y_psr], mark=(hl == 3))
                            op("dve", dfr(DVE.tensor_tensor, out=X[:, tile, hs_], in0=y_ps[:], in1=X[:, tile, hs_],
                                                                op=ALU.add),
                               R=[y_psr, Xr[tile]], W=[Xr[tile]])

                import os
                LV = int(os.environ.get("KDBG_LV", "99"))
                def drive(items):
                    items = [[g_, w_] for g_, w_ in items if g_ is not None]
                    while items:
                        for it in list(items):
                            for _ in range(it[1]):
                                try:
                                    next(it[0])
                                except StopIteration:
                                    items.remove(it)
                                    break

                def load_qf(l, g):
                    wv = w_in_a[l].rearrange("(k p) n -> p k n", p=128)
                    for typ, (Wt, Wr) in ((0, (Wq, Wqr)), (1, (Wf, Wfr))):
                        c0 = typ * 2048 + g * 512
                        dma("pool", Wt[:], wv[:, :, c0:c0 + 512], W=[Wr], key="Win%d" % typ)

                def load_igo(l, g):
                    wv = w_in_a[l].rearrange("(k p) n -> p k n", p=128)
                    for typ, (Wt, Wr) in ((2, (Wi, Wir)), (3, (Wg, Wgr))):
                        c0 = typ * 2048 + g * 512
                        dma("pool", Wt[:], wv[:, :, c0:c0 + 512], W=[Wr], key="Win%d" % typ)
                    wo = w_out_a[l][g * 512:(g + 1) * 512, :].rearrange("(h p) n -> p h n", p=128)
                    dma("pool", Wo[:], wo, W=[Wor], key="Wo")
                    op("pool", dfr(POOL.tensor_scalar, out=Wo[:], in0=Wo[:], scalar1=pT[:, 32 + l:33 + l],
                                                          scalar2=0.0, op0=ALU.mult, op1=ALU.add),
                       R=[Wor, pTr], W=[Wor])

                gidx = 0
                for l in range(n_a):
                    norm_phase(norm_a[l])
                    pending_b = None
                    for g in range(4):
                        kctr = [0, 0, 0, 0]
                        for ci, (t0, T, samp) in enumerate(chunks):
                            st = gidx % NSET
                            gidx += 1
                            if ci == 0:
                                load_qf(l, g)
                            drive([(stage_a(l, g, st, t0, T, samp), 1), (pending_b, 3)])
                            if ci == 0:
                                load_igo(l, g)
                                for hl0 in range(4):
                                    op("pool", dfr(POOL.memset, snap[:, hl0, 0, :], 0.0), W=[snapr[hl0][0]])
                            pending_b = stage_b(l, g, st, t0, T, samp, kctr)
                    drive([(pending_b, 1)])
                tk.barrier()

        hgrn_phase()


        TS = 256

        def swa_phase():
            with ExitStack() as ws:
                def hb(name, shape, dt, stack=ws):
                    return sb(name, shape, dt, stack)

                KT = hb("KT", [128, 4, TOK], BF16)
                KTr = [Res("KT%d" % i) for i in range(NT)]
                V = hb("V", [128, NT, 4, 64], BF16)
                Vr = [Res("V%d" % i) for i in range(NT)]
                KcT = hb("KcT", [128, 16, 4, 128], BF16)
                Vc = hb("Vc", [128, 16, 4, 64], BF16)
                KcTr, Vcr = Res("KcT"), Res("Vc")
                ones64 = hb("ones64", [128, 64], BF16)
                mask2 = hb("mask2", [128, 2, 128], BF16)
                maskc = hb("maskc", [128, 8], BF16)
                WA = hb("WA", [128, 8, 512], BF16)
                WB = hb("WB", [128, 8, 512], BF16)
                WO = hb("WO", [128, 4, 1024], BF16)
                WAr, WBr, WOr = Res("WA"), Res("WB"), Res("WO")
                esraw = hb("esraw", [128, 32], F32)
                esp = hb("esp", [128, 16], F32)
                esr = Res("es")

                op("pool", dfr(POOL.memset, ones64[:], 1.0), W=[cr])
                op("pool", dfr(POOL.memset, mask2[:], 1.0), W=[cr])
                op("pool", dfr(POOL.affine_select, out=mask2[:, 0, :], in_=mask2[:, 0, :], pattern=[[-1, 128]],
                                                      compare_op=ALU.is_ge, fill=0.0, base=0,
                                                      channel_multiplier=1), R=[cr], W=[cr])
                op("pool", dfr(POOL.affine_select, out=mask2[:, 1, :], in_=mask2[:, 1, :], pattern=[[1, 128]],
                                                      compare_op=ALU.is_ge, fill=0.0, base=0,
                                                      channel_multiplier=-1), R=[cr], W=[cr])
                op("pool", dfr(POOL.memset, maskc[:], 1.0), W=[cr])
                op("pool", dfr(POOL.affine_select, out=maskc[:], in_=maskc[:], pattern=[[-1, 8]],
                                                      compare_op=ALU.is_ge, fill=0.0, base=0,
                                                      channel_multiplier=1), R=[cr], W=[cr])

                tk.filler = lambda: dfr(PE.matmul, banks[5][:, 0:128], lhsT=ident[:], rhs=ident[:], start=True, stop=True)
                norm_phase(norm_kv)
                with ExitStack() as ks:
                    kcs = [hb("kc%d" % i, [128, 512], F32, ks) for i in range(2)]
                    kcbs = [hb("kcb%d" % i, [128, 4, 2, 64], BF16, ks) for i in range(2)]
                    kcrs = [Res("kc%d" % i) for i in range(2)]
                    kcbrs = [Res("kcb%d" % i) for i in range(2)]
                    kvf = hb("kvf", [128, 512], F32, ks)
                    kvfr = Res("kvf")
                    dma("pool", WA[:], w_kv.rearrange("(k p) n -> p k n", p=128), W=[WAr], key="wkv")
                    WBv = WB[:].rearrange("p k (g d h) -> p k g d h", g=4, d=2)
                    WAv = WA[:, :, 0:256].rearrange("p k (g h) -> p k g h", g=4)
                    for k in range(8):
                        op("pool", dfr(POOL.tensor_copy,
                            out=WBv[:, k], in_=WAv[:, k].unsqueeze(2).to_broadcast([128, 4, 2, 64])),
                           R=[WAr], W=[WBr])
                    kchunks = [(c * 512, 512) for c in range(4)] + [(2048, 128)]
                    import os
                    KV_ = int(os.environ.get("KDBG_KV", "99"))
                    for g in range(4 if KV_ >= 1 else 0):
                        for (t0, T) in kchunks:
                            tl = list(range(t0 // 128, (t0 + T) // 128))
                            for k in range(8):
                                op("pe", dfr(PE.matmul, banks[0][:, 0:T], lhsT=WB[:, k, g * 128:(g + 1) * 128],
                                                           rhs=xnT[:, k, t0:t0 + T], start=(k == 0), stop=(k == 7)),
                                   R=[WBr] + [xnTr[t] for t in tl], W=[bankr[0]], mark=(k == 7))
                            op("act", dfr(ACT.copy, out=KT[:, g, t0:t0 + T], in_=banks[0][:, 0:T]),
                               R=[bankr[0]], W=[KTr[t] for t in tl])
                    for tile in range(NT if KV_ >= 2 else 0):
                        gsl = slice(tile * 128, (tile + 1) * 128)
                        full = tile >= 15
                        c0 = 0 if full else 256
                        for k in range(8):
                            op("pe", dfr(PE.matmul, banks[1][:, c0:512], lhsT=xnT[:, k, gsl], rhs=WA[:, k, c0:512],
                                                       start=(k == 0), stop=(k == 7)),
                               R=[WAr, xnTr[tile]], W=[bankr[1]], mark=(k == 7))
                        op("dve", dfr(DVE.tensor_copy, out=V[:, tile, :, :].rearrange("p g h -> p (g h)"),
                                                          in_=banks[1][:, 256:512]), R=[bankr[1]], W=[Vr[tile]])
                        if full:
                            op("act", dfr(ACT.copy, out=kvf[:], in_=banks[1][:]), R=[bankr[1]], W=[kvfr])
                            if KV_ < 3:
                                pass
                            elif tile == 15:
                                dma("sp", kvp_out[:, :], kvf[:], R=[kvfr])
                            else:
                                for j in range(16):
                                    dma("sp", kvs_out[j, 120:128, :], kvf[8 * j:8 * j + 8, :], R=[kvfr])
                    if KV_ >= 4:
                        dma("sp", kvs_out[:, 0:120, :], cache[:, 8:128, :])
                    trb = banks[2][:].bitcast(BF16)[:, 0:512].rearrange("p (g w) -> p g w", g=4)
                    for j in range(16 if KV_ >= 5 else 0):
                        kc, kcb, kcr, kcbr = kcs[j % 2], kcbs[j % 2], kcrs[j % 2], kcbrs[j % 2]
                        dma("sp", kc[:], cache[j, :, :], W=[kcr])
                        op("dve", dfr(DVE.tensor_copy,
                            out=kcb[:], in_=kc[:, 0:256].rearrange("p (g h) -> p g h", g=4).unsqueeze(2)
                            .to_broadcast([128, 4, 2, 64])), R=[kcr], W=[kcbr])
                        op("pool", dfr(POOL.tensor_copy, out=Vc[:, j, :, :].rearrange("p g h -> p (g h)"),
                                                            in_=kc[:, 256:512]), R=[kcr], W=[Vcr])
                        for g in range(4):
                            op("pe", dfr(PE.transpose, out=trb[:, g, :],
                                                          in_=kcb[:, g, :, :].rearrange("p d h -> p (d h)"),
                                                          identity=ident[:]),
                               R=[kcbr, cr], W=[bankr[2]], mark=(g == 3))
                        op("act", dfr(ACT.copy, out=KcT[:, j, :, :], in_=trb), R=[bankr[2]], W=[KcTr])
                    tk.barrier()

                if n_b == 0:
                    tk.barrier()
                    return
                QTs = [hb("QT%d" % i, [128, 4, TS], BF16) for i in range(2)]
                gsTs = [hb("gsT%d" % i, [128, 4, TS], BF16) for i in range(2)]
                QTrs = [[Res("QT%d" % i) for i in range(4)] for _ in range(2)]
                gsTrs = [[Res("gsT%d" % i) for i in range(4)] for _ in range(2)]
                pTs = [[hb("pT%d_%d" % (q, i), [128, 2, 2, 128], BF16) for i in range(2)] for q in range(2)]
                pTrs = [[Res("pT%d_%d" % (q, i)) for i in range(2)] for q in range(2)]
                pc = [hb("pc%d" % i, [128, 16, 4, 8], BF16) for i in range(2)]
                pcr = [Res("pc%d" % i) for i in range(2)]
                t1 = hb("t1", [128, 4, 128], F32)
                t2 = hb("t2", [128, 4, 128], F32)
                t1r, t2r = Res("t1"), Res("t2")
                ogT = hb("ogTb", [128, 4, 128], BF16)
                ogTr = Res("ogTb")
                sTbs = [[banks[2], banks[3]], [banks[2], banks[3]]]
                sTrs = [[bankr[2], bankr[3]], [bankr[2], bankr[3]]]
                tk.filler = lambda: dfr(PE.matmul, banks[6][:, 0:128], lhsT=ident[:], rhs=ident[:], start=True, stop=True)
                oT_ps = banks[4][:].rearrange("p (r t) -> p r t", r=4)
                dn_ps = banks[5][:].rearrange("p (r t) -> p r t", r=4)
                oTr, dnr = bankr[4], bankr[5]
                y_ps, y_psr = banks[7], bankr[7]
                pctr = [0]
                swc = [0]

                def load_w(jl, g):
                    wv = w_in_b[jl].rearrange("(k p) n -> p k n", p=128)
                    dma("pool", WA[:], wv[:, :, g * 512:(g + 1) * 512], W=[WAr], key="bq")
                    dma("pool", WB[:], wv[:, :, 2048 + g * 512:2048 + (g + 1) * 512], W=[WBr], key="bg")
                    dma("pool", WO[:], w_out_b[jl][g * 512:(g + 1) * 512, :].rearrange("(r p) n -> p r n", p=128),
                        W=[WOr], key="bo")

                def sw_a(g, t0, T, cs_):
                    QT, gsT, QTr, gsTr = QTs[cs_], gsTs[cs_], QTrs[cs_], gsTrs[cs_]
                    tl = [xnTr[t] for t in range(t0 // 128, (t0 + T) // 128)]
                    for pr in range(4):
                        for k in range(8):
                            op("pe", dfr(PE.matmul, banks[0][:, 0:T], lhsT=WA[:, k, pr * 128:(pr + 1) * 128],
                                                       rhs=xnT[:, k, t0:t0 + T], start=(k == 0), stop=(k == 7)),
                               R=[WAr] + tl, W=[bankr[0]], mark=(k == 7), n=T)
                        op("act", dfr(ACT.activation, out=QT[:, pr, 0:T], in_=banks[0][:, 0:T], func=AF.Copy,
                                                         scale=0.125), R=[bankr[0]], W=[QTr[pr]], n=T)
                        for k in range(8):
                            op("pe", dfr(PE.matmul, banks[0][:, 256:256 + T], lhsT=WB[:, k, pr * 128:(pr + 1) * 128],
                                                       rhs=xnT[:, k, t0:t0 + T], start=(k == 0), stop=(k == 7)),
                               R=[WBr] + tl, W=[bankr[0]], mark=(k == 7), n=T)
                        op("act", dfr(ACT.activation, out=gsT[:, pr, 0:T], in_=banks[0][:, 256:256 + T], func=AF.Tanh,
                                                         scale=0.5), R=[bankr[0]], W=[gsTr[pr]], n=T)
                        op("dve", dfr(DVE.scalar_tensor_tensor, out=gsT[:, pr, 0:T], in0=gsT[:, pr, 0:T], scalar=1.0,
                                                                   in1=banks[0][:, 256:256 + T], op0=ALU.add,
                                                                   op1=ALU.mult),
                           R=[gsTr[pr], bankr[0]], W=[gsTr[pr]], n=T)

                def sw_b(g, t0, T, cs_):
                    QT, gsT, QTr, gsTr = QTs[cs_], gsTs[cs_], QTrs[cs_], gsTrs[cs_]
                    for ti in range(T // 128):
                        tile = t0 // 128 + ti
                        tsl = slice(ti * 128, (ti + 1) * 128)
                        samp = tile == 16
                        kts = [1] if (tile == 0 or samp) else [0, 1]
                        if samp:
                            sTb, sTr = sTbs[0], sTrs[0]
                            for par in range(2):
                                ps = slice(par * 64, par * 64 + 64)
                                scb = sTb[par][:].rearrange("p (j r t) -> p j r t", j=16, r=4)
                                for j in range(16):
                                    for pr in range(4):
                                        op("pe", dfr(PE.matmul, scb[:, j, pr, :], lhsT=KcT[ps, j, g, :],
                                                                   rhs=QT[ps, pr, 8 * j:8 * j + 8], start=True,
                                                                   stop=True),
                                           R=[KcTr, QTr[pr]], W=[sTr[par]], mark=(j == 15 and pr == 3), pm=1, n=64)
                            for par in range(2):
                                op("act", dfr(ACT.activation, out=pc[par][:].rearrange("p j r t -> p (j r t)"),
                                                                 in_=sTb[par][:], func=AF.Exp),
                                   R=[sTr[par]], W=[pcr[par]])
                                op("pool", dfr(POOL.tensor_tensor,
                                    out=pc[par][:].rearrange("p j r t -> p (j r) t"),
                                    in0=pc[par][:].rearrange("p j r t -> p (j r) t"),
                                    in1=maskc[:].unsqueeze(1).to_broadcast([128, 64, 8]), op=ALU.mult),
                                   R=[pcr[par], cr], W=[pcr[par]])
                        for pp in range(2):
                            sTb, sTr, pT, pTr = sTbs[pp], sTrs[pp], pTs[pp], pTrs[pp]
                            sTv = [sTb[par][:].rearrange("p (a k t) -> p a k t", a=2, k=2) for par in range(2)]
                            for a in range(2):
                                pr = 2 * pp + a
                                for par in range(2):
                                    ps = slice(par * 64, par * 64 + 64)
                                    for kt_i in kts:
                                        ktile = tile - 1 + kt_i
                                        op("pe", dfr(PE.matmul, sTv[par][:, a, kt_i, :],
                                                                   lhsT=KT[ps, g, ktile * 128:(ktile + 1) * 128],
                                                                   rhs=QT[ps, pr, tsl], start=True, stop=True),
                                           R=[KTr[ktile], QTr[pr]], W=[sTr[par]],
                                           mark=(a == 1 and kt_i == 1), pm=1, n=128)
                            k0 = kts[0]
                            for par in range(2):
                                op("act", dfr(ACT.activation, out=pT[par][:, :, k0:2, :], in_=sTv[par][:, :, k0:2, :],
                                                                 func=AF.Exp), R=[sTr[par]], W=[pTr[par]])
                                if samp:
                                    mk = mask_s[:].unsqueeze(1).to_broadcast([128, 2, 128])
                                    op("pool", dfr(POOL.tensor_tensor, out=pT[par][:, :, 1, :],
                                                                          in0=pT[par][:, :, 1, :], in1=mk, op=ALU.mult),
                                       R=[pTr[par], cr], W=[pTr[par]])
                                elif len(kts) == 2 and False:
                                    mk = mask2[:, :, :].unsqueeze(1).to_broadcast([128, 2, 2, 128])
                                    op("pool", dfr(POOL.tensor_tensor, out=pT[par][:], in0=pT[par][:], in1=mk,
                                                   op=ALU.mult), R=[pTr[par], cr], W=[pTr[par]])
                                else:
                                    for kt_i in kts:
                                        mk = mask2[:, kt_i, :].unsqueeze(1).to_broadcast([128, 2, 128])
                                        op("pool", dfr(POOL.tensor_tensor, out=pT[par][:, :, kt_i, :],
                                                                              in0=pT[par][:, :, kt_i, :], in1=mk,
                                                                              op=ALU.mult),
                                           R=[pTr[par], cr], W=[pTr[par]])
                            for a in range(2):
                                pr = 2 * pp + a
                                for par in range(2):
                                    ps = slice(par * 64, par * 64 + 64)
                                    for (dst, dres, use_v) in ((oT_ps, oTr, True), (dn_ps, dnr, False)):
                                        n_mm = len(kts) + (16 if samp else 0)
                                        cnt = 0
                                        for kt_i in kts:
                                            ktile = tile - 1 + kt_i
                                            cnt += 1
                                            lh = V[:, ktile, g, :] if use_v else ones64[:]
                                            op("pe", dfr(PE.matmul, dst[ps, pr, :], lhsT=lh,
                                                                       rhs=pT[par][:, a, kt_i, :],
                                                                       start=(cnt == 1), stop=(cnt == n_mm)),
                                               R=[Vr[ktile], pTr[par], cr], W=[dres], pm=2, n=128)
                                        if samp:
                                            for j in range(16):
                                                cnt += 1
                                                lh = Vc[:, j, g, :] if use_v else ones64[:]
                                                op("pe", dfr(PE.matmul, dst[ps, pr, 8 * j:8 * j + 8], lhsT=lh,
                                                                           rhs=pc[par][:, j, pr, :],
                                                                           start=False, stop=(cnt == n_mm)),
                                                   R=[Vcr, pcr[par], cr], W=[dres], pm=2, n=64)
                        esb_ = esp[:, g * 4:(g + 1) * 4].unsqueeze(2).to_broadcast([128, 4, 128])
                        op("dve", dfr(DVE.scalar_tensor_tensor, out=t1[:], in0=dn_ps, scalar=2.0, in1=esb_,
                                                                   op0=ALU.mult, op1=ALU.add),
                           R=[dnr, esr], W=[t1r], n=512)
                        op("dve", dfr(DVE.reciprocal, out=t1[:], in_=t1[:]), R=[t1r], W=[t1r])
                        op("dve", dfr(DVE.tensor_tensor, out=t2[:], in0=oT_ps, in1=t1[:], op=ALU.mult),
                           R=[oTr, t1r], W=[t2r])
                        op("pool", dfr(POOL.tensor_tensor, out=ogT[:], in0=t2[:], in1=gsT[:, :, tsl], op=ALU.mult),
                           R=[t2r] + gsTr, W=[ogTr])
                        for half in range(2):
                            hs_ = slice(half * 512, (half + 1) * 512)
                            for pr in range(4):
                                op("pe", dfr(PE.matmul, y_ps[:], lhsT=ogT[:, pr, :], rhs=WO[:, pr, hs_],
                                                           start=(pr == 0), stop=(pr == 3)),
                                   R=[ogTr, WOr], W=[y_psr], mark=(pr == 3))
                            op("dve", dfr(DVE.tensor_tensor, out=X[:, tile, hs_], in0=y_ps[:], in1=X[:, tile, hs_],
                                                                op=ALU.add),
                               R=[y_psr, Xr[tile]], W=[Xr[tile]])

                chunks = [(c * TS, TS) for c in range(2048 // TS)] + [(2048, 128)]
                for jl in range(n_b):
                    norm_phase(norm_b[jl])
                    dma("sp", esraw[:], sinks_b[jl].partition_broadcast(128), W=[esr])
                    op("act", dfr(ACT.activation, out=esraw[:], in_=esraw[:], func=AF.Exp), R=[esr], W=[esr])
                    op("dve", dfr(DVE.tensor_scalar, out=esraw[:], in0=esraw[:], scalar1=2.0, scalar2=0.0,
                                                        op0=ALU.mult, op1=ALU.add), R=[esr], W=[esr])
                    ev = esraw[:].rearrange("p (r a) -> p r a", a=2)
                    op("dve", dfr(DVE.tensor_copy, out=esp[0:64, :], in_=ev[0:64, :, 0]), R=[esr], W=[esr])
                    op("dve", dfr(DVE.tensor_copy, out=esp[64:128, :], in_=ev[64:128, :, 1]), R=[esr], W=[esr])
                    for g in range(4):
                        load_w(jl, g)
                        for (t0, T) in chunks:
                            sw_a(g, t0, T, swc[0] % 2)
                            sw_b(g, t0, T, swc[0] % 2)
                            swc[0] += 1
                tk.barrier()

        if n_a == 2:
            swa_phase()

        def final_norm():
            tk.filler = None
            with ExitStack() as fs:
                yt = [sb("yt%d" % i, [128, 1024], F32, fs) for i in range(2)]
                ytr = [Res("yt%d" % i) for i in range(2)]
                dma("sp", gB[:], norm_f.partition_broadcast(128), W=[gBr])
                for ti in range(NT):
                    op("act", dfr(ACT.activation, out=xtmp[ti % 2][:], in_=X[:, ti, :], func=AF.Square,
                                                     accum_out=ssq[:, ti:ti + 1]),
                       R=[Xr[ti]], W=[xtmpr[ti % 2], ssqr])
                op("dve", dfr(DVE.tensor_scalar, out=rstd[:], in0=ssq[:], scalar1=1.0 / 1024, scalar2=EPS,
                                                    op0=ALU.mult, op1=ALU.add), R=[ssqr], W=[rstdr])
                op("pool", dfr(POOL.tensor_tensor, out=rstd[:], in0=rstd[:], in1=mhalf[:, 0:NT], op=ALU.pow),
                   R=[rstdr, cr], W=[rstdr])
                ypv = y_p.rearrange("(n p) d -> p n d", p=128)
                for ti in range(NT):
                    b = ti % 2
                    op("dve", dfr(DVE.scalar_tensor_tensor, out=yt[b][:], in0=X[:, ti, :],
                                                               scalar=rstd[:, ti:ti + 1], in1=gB[:],
                                                               op0=ALU.mult, op1=ALU.mult),
                       R=[Xr[ti], rstdr, gBr], W=[ytr[b]])
                    if ti < 16:
                        dma("sp", ypv[:, ti, :], yt[b][:], R=[ytr[b]])
                    else:
                        dma("sp", y_s[:, :], yt[b][:], R=[ytr[b]])
                tk.barrier()

        if not dbg or n_b == 2:
            final_norm()

        if dbg:
            xd = xdbg.rearrange("(n p) d -> p n d", p=128)
            for ti in range(NT):
                dma("sp", xd[:, ti, :], X[:, ti, :], R=[Xr[ti]])
        tk.finish()
    return nc


def _shard_inputs(inp):
    maps = []
    shared = {k: np.ascontiguousarray(inp[k], dtype=np.float32) for k in
              ("w_in_a", "w_out_a", "norm_a", "onorm_a", "lower_bounds_a", "norm_kv", "w_kv", "w_in_b",
               "w_out_b", "norm_b", "sinks_b", "norm_f")}
    for c in range(NCORES):
        m = dict(shared)
        m["x_p"] = np.ascontiguousarray(inp["x_prompt"][c], dtype=np.float32)
        m["x_s"] = np.ascontiguousarray(inp["x_sample"][16 * c:16 * c + 16].reshape(128, 1024), dtype=np.float32)
        m["st_in"] = np.ascontiguousarray(inp["state_hgrn"][:, 16 * c:16 * c + 16], dtype=np.float32)
        m["cache"] = np.ascontiguousarray(inp["cache_kv_window"][16 * c:16 * c + 16].reshape(16, 128, 512),
                                          dtype=np.float32)
        maps.append(m)
    return maps


def kernel(**inputs):
    nc = build()
    maps = _shard_inputs(inputs)
    res = run_bass_kernel_spmd(nc, maps, core_ids=list(range(NCORES)))
    r = res.results
    y_prompt = np.stack([r[c]["y_p"] for c in range(NCORES)], axis=0)
    y_sample = np.concatenate([r[c]["y_s"].reshape(16, 8, 1024) for c in range(NCORES)], axis=0)
    sp = np.stack([r[c]["sp_out"] for c in range(NCORES)], axis=1)
    ss = np.concatenate([r[c]["ss_out"] for c in range(NCORES)], axis=1)
    kvp = np.stack([r[c]["kvp_out"].reshape(128, 2, 4, 64) for c in range(NCORES)], axis=0)
    kvs = np.concatenate([r[c]["kvs_out"].reshape(16, 128, 2, 4, 64) for c in range(NCORES)], axis=0)
    return (y_prompt, y_sample, sp, ss, kvp, kvs)
```

```python
import bisect
from contextlib import ExitStack

import numpy as np
import concourse.bass as bass
import concourse.mybir as mybir
from concourse.bass_utils import run_bass_kernel_spmd

F32 = mybir.dt.float32
BF16 = mybir.dt.bfloat16
AF = mybir.ActivationFunctionType
ALU = mybir.AluOpType
AX = mybir.AxisListType

NCORES = 8
NT = 17
TOK = NT * 128
EPS = 1e-6
QSCALE = 128 ** -0.5
SEM_LIMIT = 30000
NDS = 24


class Res:
    __slots__ = ("name", "w", "rc", "rd", "excl", "also")

    def __init__(self, name, excl=False):
        self.name = name
        self.excl = excl
        self.also = ()
        self.w = None
        self.rc = {}
        self.rd = []


def dfr(f, *a, **k):
    fn = lambda: f(*a, **k)
    o = k.get("out", a[0] if a else None)
    try:
        n = 1
        for d in o.shape[1:]:
            n *= int(d)
        fn.n = n
    except Exception:
        fn.n = None
    fn.nm = getattr(f, "__name__", "")
    return fn


class Eng:
    def __init__(self, name, h):
        self.name = name
        self.h = h
        self.sem = None
        self.cnt = 0
        self.pos = 0
        self.last = None
        self.last_marked = True
        self.mpos = []
        self.mtk = []
        self.waited = {}


class OpRec:
    __slots__ = ("id", "en", "fn", "deps", "odeps", "dma", "mark", "dur", "lat", "key", "pm")


SCHED = True


class Trk:
    def __init__(self, nc, es):
        self.nc = nc
        self.es = es
        self.E = {n: Eng(n, h) for n, h in [("pe", nc.tensor), ("act", nc.scalar), ("dve", nc.vector),
                                            ("pool", nc.gpsimd), ("sp", nc.sync)]}
        self.sems = []
        self.dsem = [self._newsem("dma%d" % i) for i in range(NDS)]
        self.dval = [0] * NDS
        self.dnext = 0
        self.swn = 0
        self.swlast = {}
        self.nid = 0
        self.pending = []
        self.filler = None
        self.filler_dep = None
        self._engof = {}
        self.done = {}

    def _newsem(self, name):
        s = self.es.enter_context(self.nc.semaphore(name))
        self.sems.append(s)
        return len(self.sems) - 1

    def _record(self, en, fn, R, W, is_dma, mark, dur, lat, key=None, pm=0):
        deps, odeps = set(), set()
        W = list(W)
        for w in list(W):
            for x in w.also:
                if x not in W:
                    W.append(x)
        same_skip = (not is_dma) and en == "pe"
        for r in R:
            if r.w is not None:
                deps.add(r.w)
            if r.excl:
                for e2, oid in r.rc.items():
                    if e2 != en:
                        deps.add(oid)
        for w in W:
            if w.w is not None:
                (odeps if (same_skip and self._eng_of(w.w) == en) else deps).add(w.w)
            for e2, oid in w.rc.items():
                (odeps if (same_skip and e2 == en) else deps).add(oid)
            deps.update(w.rd)
        o = OpRec()
        o.id, o.en, o.fn, o.dma, o.mark, o.dur, o.lat, o.key = self.nid, en, fn, is_dma, mark, dur, lat, key
        o.pm = pm
        self.nid += 1
        for r in R:
            if is_dma:
                r.rd.append(o.id)
                if len(r.rd) > 12:
                    deps.add(r.rd[0])
                    r.rd = r.rd[1:]
            else:
                prev = r.rc.get(en)
                if prev is not None:
                    odeps.add(prev)
                r.rc[en] = o.id
        for w in W:
            w.w = o.id
            w.rc = {}
            w.rd = []
        o.deps = [d for d in deps if d != o.id]
        o.odeps = [d for d in odeps if d != o.id and d not in deps]
        self._engof[o.id] = en
        self.pending.append(o)
        return o

    def _eng_of(self, oid):
        return self._engof.get(oid)

    def op(self, en, fn, R=(), W=(), mark=None, n=256, pm=0):
        if mark is None:
            mark = en != "pe"
        n = getattr(fn, "n", None) or n
        nm = getattr(fn, "nm", "")
        if en == "pe":
            dur = max(64, n) / 1.95 + 10
        elif en == "act":
            dur = 230 + 0.75 * n
        elif en == "dve":
            f_ = 2.0 if "scan" in nm else (6.5 if nm == "reciprocal" else 1.0)
            dur = (150 + f_ * n) / 0.96
        else:
            dur = 1000.0 if n <= 32 else 450 + 1.3 * n
        return self._record(en, fn, R, W, False, mark, dur, 0.0, pm=pm)

    def dma(self, qn, out, in_, R=(), W=(), key=None, nbytes=262144, **kw):
        issue = 1000.0 if qn == "pool" else 60.0
        fn = lambda: self.E[qn].h.dma_start(out=out, in_=in_, **kw)
        self._record(qn, fn, R, W, True, False, issue, 2000.0 + nbytes / 150.0, key)

    def _schedule(self, ops):
        if not SCHED:
            return ops
        import heapq
        idx = {o.id: i for i, o in enumerate(ops)}
        nd = [0] * len(ops)
        succ = [[] for _ in ops]
        ready_t = [0.0] * len(ops)
        for i, o in enumerate(ops):
            for d in o.deps + o.odeps:
                j = idx.get(d)
                if j is not None:
                    nd[i] += 1
                    succ[j].append(i)
        def hk(o):
            return ("pe", o.pm) if o.en == "pe" else o.en
        heaps = {e: [] for e in self.E if e != "pe"}
        for m in range(3):
            heaps[("pe", m)] = []
        for i, o in enumerate(ops):
            if nd[i] == 0:
                heapq.heappush(heaps[hk(o)], (0.0, i))
        free = {e: 0.0 for e in self.E}
        fin = [0.0] * len(ops)
        order = []
        nleft = len(ops)
        import os
        WIN = int(os.environ.get('K_WIN', '6000'))
        XLAT = float(os.environ.get('K_XLAT', '120'))
        lo = 0
        sched = [False] * len(ops)
        pe_mode = 0
        SWITCH = float(os.environ.get('K_SWITCH', '300'))
        rnow = {e: [] for e in heaps}
        POL = int(os.environ.get('K_POL', '1'))
        FILL = float(os.environ.get('K_FILL', '500'))
        if POL == 2:
            rank = [0.0] * len(ops)
            for i in range(len(ops) - 1, -1, -1):
                m = 0.0
                for j in succ[i]:
                    if rank[j] > m:
                        m = rank[j]
                rank[i] = ops[i].dur + ops[i].lat + m
            prio = [-(rank[i]) for i in range(len(ops))]
        else:
            prio = list(range(len(ops)))

        def engname(e):
            return "pe" if isinstance(e, tuple) else e

        while nleft:
            best = None
            for e, h in heaps.items():
                en = engname(e)
                rn = rnow[e]
                while h and h[0][0] <= free[en]:
                    rt, i = heapq.heappop(h)
                    heapq.heappush(rn, (prio[i], i, rt))
                if rn:
                    _, i, rt = rn[0]
                    st = free[en]
                elif h:
                    rt, i = h[0]
                    st = max(free[en], rt)
                else:
                    continue
                if en == "pe" and e[1] != pe_mode:
                    st += SWITCH
                if best is None or (st, i) < (best[0], best[1]):
                    best = (st, i, e)
            st, i, e = best
            en = engname(e)
            if rnow[e] and rnow[e][0][1] == i:
                heapq.heappop(rnow[e])
            else:
                heapq.heappop(heaps[e])
            if en == "pe":
                pe_mode = e[1]
            o = ops[i]
            st = max(st, free[en])
            if en == "pe" and self.filler is not None and FILL > 0 and st - free[en] >= FILL:
                kfill = min(int((st - free[en]) / float(os.environ.get("K_FD", "60"))), int(os.environ.get("K_FMAX", "40")))
                for _ in range(kfill):
                    fo = OpRec()
                    fo.id, fo.en, fo.fn, fo.deps, fo.odeps, fo.dma, fo.mark = -1, "pe", self.filler(), [], [], False, False
                    fo.dur, fo.lat, fo.key, fo.pm = 60.0, 0.0, None, 0
                    order.append(fo)
            free[en] = st + o.dur
            fin[i] = st + o.dur + o.lat
            order.append(o)
            nleft -= 1
            for j in succ[i]:
                nd[j] -= 1
                lat = XLAT if ops[j].en != en else 40.0
                ready_t[j] = max(ready_t[j], fin[i] + lat)
                if nd[j] == 0:
                    heapq.heappush(heaps[hk(ops[j])], (ready_t[j], j))
        return order

    def _mark_last(self, F):
        if F.last_marked:
            return F.mtk[-1]
        if F.sem is None or F.cnt >= SEM_LIMIT:
            F.sem = self._newsem("%s_%d" % (F.name, len(self.sems)))
            F.cnt = 0
        F.cnt += 1
        F.last.then_inc(self.sems[F.sem], 1)
        t = (F.sem, F.cnt)
        F.mpos.append(F.pos)
        F.mtk.append(t)
        F.last_marked = True
        return t

    def _resolve(self, oid):
        d = self.done[oid]
        if d[0] == "t":
            return (d[1], d[2])
        F = self.E[d[1]]
        i = bisect.bisect_left(F.mpos, d[2])
        if i < len(F.mpos):
            return F.mtk[i]
        assert F.pos >= d[2]
        return self._mark_last(F)

    def _wait(self, F, tk):
        s, v = tk
        if F.waited.get(s, 0) >= v:
            return
        F.h.wait_ge(self.sems[s], v)
        F.waited[s] = v

    def flush(self):
        ops = self.pending
        self.pending = []
        for o in self._schedule(ops):
            F = self.E[o.en]
            if o.id == -1:
                if self.filler_dep is None or self.filler_dep not in self.done:
                    continue
                self._wait(F, self._resolve(self.filler_dep))
                F.last = o.fn()
                F.pos += 1
                F.last_marked = False
                continue
            for d in o.deps:
                self._wait(F, self._resolve(d))
            if o.dma:
                if o.en == "pool":
                    self.swn += 1
                    si = self._newsem("sw%d" % self.swn)
                    self.swlast[o.key] = si
                    ins = o.fn()
                    ins.then_inc(self.sems[si], 16)
                    self.done[o.id] = ("t", si, 16)
                else:
                    i = self.dnext
                    self.dnext = (i + 1) % NDS
                    if self.dval[i] > 0:
                        self._wait(F, (self.dsem[i], self.dval[i]))
                    ins = o.fn()
                    ins.then_inc(self.sems[self.dsem[i]], 16)
                    self.dval[i] += 16
                    self.done[o.id] = ("t", self.dsem[i], self.dval[i])
            else:
                ins = o.fn()
                F.pos += 1
                F.last = ins
                F.last_marked = False
                if o.mark:
                    self._mark_last(F)
                self.done[o.id] = ("c", o.en, F.pos)

    def barrier(self):
        self.flush()
        tickets = []
        for n in ("pe", "act", "dve", "pool"):
            F = self.E[n]
            if F.last is not None:
                tickets.append(self._mark_last(F))
        for i in range(NDS):
            if self.dval[i] > 0:
                tickets.append((self.dsem[i], self.dval[i]))
        for si in self.swlast.values():
            tickets.append((si, 16))
        for n in ("pe", "act", "dve", "pool", "sp"):
            for t in tickets:
                self._wait(self.E[n], t)

    def finish(self):
        self.barrier()


def build(n_a=2, n_b=2, dbg=False):
    nc = bass.Bass("TRN2", target_bir_lowering=False)

    def din(name, shape):
        return nc.dram_tensor(name, shape, F32, kind="ExternalInput").ap()

    def dout(name, shape):
        return nc.dram_tensor(name, shape, F32, kind="ExternalOutput").ap()

    x_p = din("x_p", [2048, 1024])
    x_s = din("x_s", [128, 1024])
    st_in = din("st_in", [2, 16, 16, 128, 128])
    cache = din("cache", [16, 128, 512])
    w_in_a = din("w_in_a", [2, 1024, 8192])
    w_out_a = din("w_out_a", [2, 2048, 1024])
    norm_a = din("norm_a", [2, 1024])
    onorm_a = din("onorm_a", [2, 128])
    lb_a = din("lower_bounds_a", [2, 2048])
    norm_kv = din("norm_kv", [1024])
    w_kv = din("w_kv", [1024, 512])
    w_in_b = din("w_in_b", [2, 1024, 4096])
    w_out_b = din("w_out_b", [2, 2048, 1024])
    norm_b = din("norm_b", [2, 1024])
    sinks_b = din("sinks_b", [2, 32])
    norm_f = din("norm_f", [1024])

    y_p = dout("y_p", [2048, 1024])
    y_s = dout("y_s", [128, 1024])
    sp_out = dout("sp_out", [2, 16, 128, 128])
    ss_out = dout("ss_out", [2, 16, 16, 128, 128])
    kvp_out = dout("kvp_out", [128, 512])
    kvs_out = dout("kvs_out", [16, 128, 512])
    xdbg = dout("xdbg", [TOK, 1024]) if dbg else None

    with ExitStack() as es:
        tk = Trk(nc, es)
        op, dma = tk.op, tk.dma
        PE, ACT, DVE, POOL = nc.tensor, nc.scalar, nc.vector, nc.gpsimd

        def sb(name, shape, dt, stack=es):
            return stack.enter_context(nc.sbuf_tensor(name, shape, dt))

        banks = [es.enter_context(nc.psum_tensor("bank%d" % i, [128, 512], F32)) for i in range(8)]
        bankr = [Res("bank%d" % i, excl=True) for i in range(8)]

        X = sb("X", [128, NT, 1024], F32)
        Xr = [Res("X%d" % i) for i in range(NT)]
        xnT = sb("xnT", [128, 8, TOK], BF16)
        xnTr = [Res("xnT%d" % i) for i in range(NT)]
        ident = sb("ident", [128, 128], BF16)
        identf = sb("identf", [128, 128], F32)
        gB = sb("gB", [128, 1024], F32)
        gBr = Res("gB")
        xtmp = [sb("xtmp%d" % i, [128, 1024], BF16) for i in range(2)]
        xtmpr = [Res("xtmp%d" % i) for i in range(2)]
        ssq = sb("ssq", [128, NT], F32)
        ssqr = Res("ssq")
        rstd = sb("rstd", [128, NT], F32)
        rstdr = Res("rstd")
        mhalf = sb("mhalf", [128, 32], F32)
        mask_p = sb("mask_p", [128, 128], BF16)
        mask_s = sb("mask_s", [128, 128], BF16)
        Eind = sb("Eind", [16, 2, 128], BF16)
        cr = Res("consts")

        def setup_consts():
            op("pool", dfr(POOL.memset, ident[:], 1.0), W=[cr])
            tk.filler_dep = op("pool", dfr(POOL.affine_select, out=ident[:], in_=ident[:], pattern=[[-1, 128]],
                                                  compare_op=ALU.is_equal, fill=0.0, base=0,
                                                  channel_multiplier=1), R=[cr], W=[cr]).id
            op("pool", dfr(POOL.memset, identf[:], 1.0), W=[cr])
            op("pool", dfr(POOL.affine_select, out=identf[:], in_=identf[:], pattern=[[-1, 128]],
                                                  compare_op=ALU.is_equal, fill=0.0, base=0,
                                                  channel_multiplier=1), R=[cr], W=[cr])
            op("pool", dfr(POOL.memset, mhalf[:], -0.5), W=[cr])
        op("pool", dfr(POOL.memset, Eind[:], 1.0), W=[cr])
        for slot, cs in ((0, 64), (1, 8)):
            op("pool", dfr(POOL.affine_select, out=Eind[:, slot, :], in_=Eind[:, slot, :],
                                                  pattern=[[1, 128]], compare_op=ALU.is_ge, fill=0.0,
                                                  base=0, channel_multiplier=-cs), R=[cr], W=[cr])
            op("pool", dfr(POOL.affine_select, out=Eind[:, slot, :], in_=Eind[:, slot, :],
                                                  pattern=[[-1, 128]], compare_op=ALU.is_ge, fill=0.0,
                                                  base=cs - 1, channel_multiplier=cs), R=[cr], W=[cr])
        for slot, m in ((0, mask_p), (1, mask_s)):
            op("pe", dfr(PE.matmul, banks[3][:, 0:128], lhsT=Eind[:, slot, :], rhs=Eind[:, slot, :],
                                       start=True, stop=True), R=[cr], W=[bankr[3]], mark=True)
            op("dve", dfr(DVE.tensor_copy, out=m[:], in_=banks[3][:, 0:128]), R=[bankr[3]], W=[cr])
            op("pool", dfr(POOL.affine_select, out=m[:], in_=m[:], pattern=[[1, 128]],
                                                  compare_op=ALU.is_ge, fill=0.0, base=0,
                                                  channel_multiplier=-1), R=[cr], W=[cr])

        setup_consts()

        xpv = x_p.rearrange("(n p) d -> p n d", p=128)
        for ti in range(16):
            dma("sp", X[:, ti, :], xpv[:, ti, :], W=[Xr[ti]])
        dma("sp", X[:, 16, :], x_s[:, :], W=[Xr[16]])

        def norm_phase(gvec):
            dma("sp", gB[:], gvec.partition_broadcast(128), W=[gBr])
            for ti in range(NT):
                b = ti % 2
                op("act", dfr(ACT.activation, out=xtmp[b][:], in_=X[:, ti, :], func=AF.Square,
                                                 accum_out=ssq[:, ti:ti + 1]),
                   R=[Xr[ti]], W=[xtmpr[b], ssqr])
            op("dve", dfr(DVE.tensor_scalar, out=rstd[:], in0=ssq[:], scalar1=1.0 / 1024, scalar2=EPS,
                                                op0=ALU.mult, op1=ALU.add), R=[ssqr], W=[rstdr])
            op("pool", dfr(POOL.tensor_tensor, out=rstd[:], in0=rstd[:], in1=mhalf[:, 0:NT], op=ALU.pow),
               R=[rstdr, cr], W=[rstdr])
            trv = banks[7][:].bitcast(BF16).rearrange("p (k t) -> p k t", k=8)
            for ti in range(NT):
                b = ti % 2
                op("dve", dfr(DVE.scalar_tensor_tensor, out=xtmp[b][:], in0=X[:, ti, :],
                                                           scalar=rstd[:, ti:ti + 1], in1=gB[:],
                                                           op0=ALU.mult, op1=ALU.mult),
                   R=[Xr[ti], rstdr, gBr], W=[xtmpr[b]])
                for k in range(8):
                    op("pe", dfr(PE.transpose, out=trv[:, k, :], in_=xtmp[b][:, k * 128:(k + 1) * 128],
                                                  identity=ident[:]),
                       R=[xtmpr[b], cr], W=[bankr[7]], mark=(k == 7))
                op("act", dfr(ACT.copy, out=xnT[:, :, ti * 128:(ti + 1) * 128], in_=trv),
                   R=[bankr[7]], W=[xnTr[ti]])

        TCH = 256

        def hgrn_phase():
            with ExitStack() as hs:
                def hb(name, shape, dt):
                    return sb(name, shape, dt, hs)

                Wq = hb("Wq", [128, 8, 512], BF16)
                Wf = hb("Wf", [128, 8, 512], BF16)
                Wi = hb("Wi", [128, 8, 512], BF16)
                Wg = hb("Wg", [128, 8, 512], BF16)
                Wo = hb("Wo", [128, 4, 1024], BF16)
                Wqr, Wfr, Wir, Wgr, Wor = (Res(n) for n in ("Wq", "Wf", "Wi", "Wg", "Wo"))
                praw = hb("praw", [34, 128], F32)
                prawr = Res("praw")
                pT = hb("pT", [128, 34], F32)
                pTr = Res("pT")
                lb = hb("lb", [128, 2, 16], F32)
                oml = hb("oml", [128, 2, 16], F32)
                lbr = Res("lb")
                tT = [[hb("t%s%d" % (n, i), [128, TCH], F32) for n in "ABCDE"] for i in range(2)]
                tTr = [[Res("t%s%d" % (n, i)) for n in "ABCDE"] for i in range(2)]
                hctr = [0]
                NSET = 2
                qh = [hb("qh%d" % s, [128, 4, TCH], BF16) for s in range(NSET)]
                qt = [hb("qt%d" % s, [128, 4, TCH], BF16) for s in range(NSET)]
                kh = [hb("kh%d" % s, [128, 4, TCH], BF16) for s in range(NSET)]
                dec = [hb("dec%d" % s, [128, 4, 16], F32) for s in range(NSET)]
                rl = [hb("rl%d" % s, [128, 4, 16], F32) for s in range(NSET)]
                rlr = [[Res("rl") for _ in range(4)] for _ in range(NSET)]
                emask_p = hb("emask_p", [128, TCH], F32)
                emask_s = hb("emask_s", [128, 128], F32)
                qhr = [[Res("qh") for _ in range(4)] for _ in range(NSET)]
                qtr = [[Res("qt") for _ in range(4)] for _ in range(NSET)]
                khr = [[Res("kh") for _ in range(4)] for _ in range(NSET)]
                decr = [[Res("dec") for _ in range(4)] for _ in range(NSET)]
                smask_p = hb("smask_p", [128, TCH], F32)
                smask_s = hb("smask_s", [128, 128], F32)
                sel = hb("sel", [128, 16, 16], BF16)
                selT = hb("selT", [128, 16], BF16)
                v_sb = hb("v_sb", [128, 512], BF16)
                sg = hb("sg", [128, 512], BF16)
                gs = hb("gs", [128, 512], BF16)
                khtok = hb("khtok", [128, 4, 128], BF16)
                attm = hb("attm", [128, 4, 128], BF16)
                oss = hb("oss", [128, 4], F32)
                orstd = hb("orstd", [128, 4], F32)
                og = hb("og", [128, 512], BF16)
                ogT = hb("ogT", [128, 4, 128], BF16)
                v_sbr, sgr, gsr, khtokr, attmr, ossr, orstdr, ogr, ogTr = (
                    Res(n) for n in ("v_sb", "sg", "gs", "khtok", "attm", "oss", "orstd", "og", "ogT"))
                oss1 = hb("oss1", [128, 4], F32)
                orstd1 = hb("orstd1", [128, 4], F32)
                S = hb("S", [128, 4, 128], F32)
                Sr = [Res("S%d" % h) for h in range(4)]
                snap = hb("snap", [128, 4, 4, 128], BF16)
                snapr = [[Res("snap") for _ in range(4)] for _ in range(4)]
                S0 = [hb("S0_%d" % i, [128, 8, 128], F32) for i in range(2)]
                S0r = [Res("S0_%d" % i) for i in range(2)]
                S0bf = hb("S0bf", [128, 8, 128], BF16)
                S0bfr = Res("S0bf")
                Qexp = hb("Qexp", [128, 8, 128], BF16)
                Qexpr = Res("Qexp")
                Vexp = hb("Vexp", [128, 8, 128], BF16)
                Vexpr = Res("Vexp")

                f0 = S0[0][:].rearrange("p j e -> p (j e)").bitcast(BF16)
                f1 = S0[1][:].rearrange("p j e -> p (j e)").bitcast(BF16)
                TB = [(v_sb, sg, gs, khtok, attm, oss, orstd, og, ogT),
                      (f0[:, 0:512], f0[:, 512:1024], f0[:, 1024:1536],
                       f0[:, 1536:2048].rearrange("p (h t) -> p h t", h=4),
                       f1[:, 0:512].rearrange("p (h t) -> p h t", h=4), oss1, orstd1,
                       f1[:, 512:1024], f1[:, 1024:1536].rearrange("p (h t) -> p h t", h=4))]
                TBR = [(v_sbr, sgr, gsr, khtokr, attmr, ossr, orstdr, ogr, ogTr),
                       tuple(Res(n + "1") for n in ("v_sb", "sg", "gs", "khtok", "attm", "oss", "orstd", "og", "ogT"))]
                for ri, r_ in enumerate(TBR[1]):
                    if ri in (0, 1, 2, 3):
                        r_.also = (S0r[0],)
                    elif ri in (4, 7, 8):
                        r_.also = (S0r[1],)
                S0r[0].also = tuple(TBR[1][i] for i in (0, 1, 2, 3))
                S0r[1].also = tuple(TBR[1][i] for i in (4, 7, 8))

                q_ps = banks[0][:, 0:TCH]
                f_ps = banks[0][:, 256:256 + TCH]
                qfr = bankr[0]
                v_ps, v_psr = banks[1], bankr[1]
                g_ps, g_psr = banks[2], bankr[2]
                att_ps = banks[3][:].rearrange("p (h t) -> p h t", h=4)
                att_psr = bankr[3]
                o_ps = banks[4][:].rearrange("p (h t) -> p h t", h=4)
                o_psr = bankr[4]
                b5 = banks[3][:].bitcast(BF16)
                kT_ps = b5[:, 0:512].rearrange("p (h t) -> p h t", h=4)
                ogT_ps = b5[:, 512:1024].rearrange("p (h t) -> p h t", h=4)
                kT_psr, ogT_psr = bankr[3], bankr[3]
                tk.filler = lambda: dfr(PE.matmul, banks[5][:, 0:128], lhsT=ident[:], rhs=ident[:], start=True, stop=True)
                kvb = banks[6][:].rearrange("p (h e) -> p h e", h=4)
                kv_psr = [bankr[6]]
                kv4_ps = banks[6][:].rearrange("p (j e) -> p j e", j=4)
                y_ps, y_psr = banks[7], bankr[7]

                def v3(ap, c):
                    return ap.rearrange("p (c t) -> p c t", t=c)

                op("pool", dfr(POOL.memset, smask_p[:], 0.0), W=[cr])
                op("pool", dfr(POOL.memset, v3(smask_p, 64)[:, :, 0:1], 1.0), W=[cr])
                op("pool", dfr(POOL.memset, emask_p[:], 0.0), W=[cr])
                op("pool", dfr(POOL.memset, v3(emask_p, 64)[:, :, 63:64], 1.0), R=[cr], W=[cr])
                op("pool", dfr(POOL.memset, emask_s[:], 0.0), W=[cr])
                op("pool", dfr(POOL.memset, v3(emask_s, 8)[:, :, 7:8], 1.0), R=[cr], W=[cr])
                op("pool", dfr(POOL.memset, smask_s[:], 0.0), W=[cr])
                op("pool", dfr(POOL.memset, v3(smask_s, 8)[:, :, 0:1], 1.0), W=[cr])
                op("pool", dfr(POOL.memset, sel[:], 1.0), W=[cr])
                op("pool", dfr(POOL.affine_select, out=sel[:], in_=sel[:], pattern=[[-1, 16], [1, 16]],
                                                      compare_op=ALU.is_equal, fill=0.0, base=0,
                                                      channel_multiplier=0), R=[cr], W=[cr])
                op("pool", dfr(POOL.memset, selT[:], 1.0), W=[cr])
                op("pool", dfr(POOL.affine_select, out=selT[:], in_=selT[:], pattern=[[-8, 16]],
                                                      compare_op=ALU.is_ge, fill=0.0, base=0,
                                                      channel_multiplier=1), R=[cr], W=[cr])
                op("pool", dfr(POOL.affine_select, out=selT[:], in_=selT[:], pattern=[[8, 16]],
                                                      compare_op=ALU.is_ge, fill=0.0, base=7,
                                                      channel_multiplier=-1), R=[cr], W=[cr])

                dma("sp", praw[0:32, :], lb_a.rearrange("l (h d) -> (l h) d", d=128), W=[prawr])
                dma("sp", praw[32:34, :], onorm_a[:, :], W=[prawr])
                op("pe", dfr(PE.transpose, out=banks[3][:, 0:34], in_=praw[:, :], identity=identf[0:34, 0:34]),
                   R=[prawr, cr], W=[bankr[3]], mark=True)
                op("dve", dfr(DVE.tensor_copy, out=pT[:], in_=banks[3][:, 0:34]), R=[bankr[3]], W=[pTr])
                op("dve", dfr(DVE.memset, lb[:, 0, :], 0.0), W=[lbr])
                op("dve", dfr(DVE.tensor_tensor, out=lb[:, 1, :], in0=pT[:, 16:32], in1=pT[:, 0:16],
                                                    op=ALU.subtract), R=[pTr], W=[lbr])
                op("act", dfr(ACT.activation, out=lb[:, 1, :], in_=lb[:, 1, :], func=AF.Sigmoid),
                   R=[lbr], W=[lbr])
                op("dve", dfr(DVE.tensor_scalar, out=oml[:], in0=lb[:], scalar1=-1.0, scalar2=1.0,
                                                    op0=ALU.mult, op1=ALU.add), R=[lbr], W=[lbr])

                chunks = [(c * TCH, TCH, False) for c in range(2048 // TCH)] + [(2048, 128, True)]

                def load_weights(l, g):
                    wv = w_in_a[l].rearrange("(k p) n -> p k n", p=128)
                    for typ, (Wt, Wr) in enumerate(((Wq, Wqr), (Wf, Wfr), (Wi, Wir), (Wg, Wgr))):
                        c0 = typ * 2048 + g * 512
                        dma("pool", Wt[:], wv[:, :, c0:c0 + 512], W=[Wr], key="Win%d" % typ)
                    wo = w_out_a[l][g * 512:(g + 1) * 512, :].rearrange("(h p) n -> p h n", p=128)
                    dma("pool", Wo[:], wo, W=[Wor], key="Wo")
                    op("pool", dfr(POOL.tensor_scalar, out=Wo[:], in0=Wo[:], scalar1=pT[:, 32 + l:33 + l],
                                                          scalar2=0.0, op0=ALU.mult, op1=ALU.add),
                       R=[Wor, pTr], W=[Wor])

                def stage_a(l, g, st, t0, T, samp):
                    cs = 8 if samp else 64
                    nch = T // cs
                    mid = cs // 2 - 1
                    smask = smask_s if samp else smask_p
                    tiles = list(range(t0 // 128, (t0 + T) // 128))
                    xr = [xnTr[t] for t in tiles]
                    for hl in range(4):
                        hg = g * 4 + hl
                        qv, fv = q_ps[:, 0:T], f_ps[:, 0:T]
                        for k in range(8):
                            op("pe", dfr(PE.matmul, qv, lhsT=Wq[:, k, hl * 128:(hl + 1) * 128],
                                                       rhs=xnT[:, k, t0:t0 + T], start=(k == 0), stop=(k == 7)),
                               R=[Wqr] + xr, W=[qfr])
                        for k in range(8):
                            op("pe", dfr(PE.matmul, fv, lhsT=Wf[:, k, hl * 128:(hl + 1) * 128],
                                                       rhs=xnT[:, k, t0:t0 + T], start=(k == 0), stop=(k == 7)),
                               R=[Wfr] + xr, W=[qfr], mark=(k == 7))
                        tb = hctr[0] % 2
                        hctr[0] += 1
                        A, B, C, Dd, Ee = (t_[:, 0:T] for t_ in tT[tb])
                        tAr, tBr, tCr, tDr, tEr = tTr[tb]
                        op("act", dfr(ACT.activation, out=A, in_=fv, func=AF.Sigmoid), R=[qfr], W=[tAr])
                        op("act", dfr(ACT.activation, out=B, in_=fv, func=AF.Sigmoid, scale=-1.0),
                           R=[qfr], W=[tBr])
                        op("act", dfr(ACT.activation, out=C, in_=qv, func=AF.Sigmoid), R=[qfr], W=[tCr])
                        op("dve", dfr(DVE.tensor_tensor, out=C, in0=qv, in1=C, op=ALU.mult),
                           R=[qfr, tCr], W=[tCr])
                        if l > 0:
                            op("act", dfr(ACT.activation, out=A, in_=A, func=AF.Identity,
                                          scale=oml[:, l, hg:hg + 1], bias=lb[:, l, hg:hg + 1]),
                               R=[tAr, lbr], W=[tAr])
                        op("pool", dfr(POOL.tensor_tensor, out=Dd, in0=A, in1=smask[:, 0:T], op=ALU.mult),
                           R=[tAr, cr], W=[tDr])
                        op("dve", dfr(DVE.tensor_tensor_scan, out=Ee, data0=A, data1=Dd, initial=1.0,
                                                                 op0=ALU.mult, op1=ALU.max),
                           R=[tAr, tDr], W=[tEr])
                        op("act", dfr(ACT.copy, out=Dd[:, 0:T - 1], in_=A[:, 1:T]), R=[tAr], W=[tDr])
                        op("pool", dfr(POOL.memset, v3(Dd, cs)[:, :, cs - 1:cs], 1.0), R=[tDr], W=[tDr])
                        emask = emask_s if samp else emask_p
                        op("dve", dfr(DVE.tensor_tensor_scan, out=A[:, ::-1], data0=Dd[:, ::-1],
                                                                 data1=emask[:, 0:T][:, ::-1], initial=1.0,
                                                                 op0=ALU.mult, op1=ALU.max),
                           R=[tDr, cr], W=[tAr])
                        E3 = v3(Ee, cs)
                        op("dve", dfr(DVE.scalar_tensor_tensor, out=kh[st][:, hl, 0:T], in0=B,
                                                                   scalar=oml[:, l, hg:hg + 1], in1=A,
                                                                   op0=ALU.mult, op1=ALU.mult),
                           R=[tBr, tAr, lbr], W=[khr[st][hl]])
                        op("dve", dfr(DVE.scalar_tensor_tensor, out=qh[st][:, hl, 0:T], in0=C, scalar=QSCALE,
                                                                   in1=Ee, op0=ALU.mult, op1=ALU.mult),
                           R=[tCr, tEr], W=[qhr[st][hl]])
                        op("dve", dfr(DVE.tensor_copy, out=dec[st][:, hl, 0:nch], in_=E3[:, :, cs - 1]),
                           R=[tEr], W=[decr[st][hl]])
                        op("dve", dfr(DVE.reciprocal, out=rl[st][:, hl, 0:nch], in_=dec[st][:, hl, 0:nch]),
                           R=[decr[st][hl]], W=[rlr[st][hl]])
                        op("pool", dfr(POOL.tensor_tensor,
                            out=v3(qt[st][:, hl, 0:T], cs), in0=v3(qh[st][:, hl, 0:T], cs),
                            in1=rl[st][:, hl, 0:nch].unsqueeze(2).to_broadcast([128, nch, cs]), op=ALU.mult),
                           R=[qhr[st][hl], rlr[st][hl]], W=[qtr[st][hl]])
                        yield

                def stage_b(l, g, st, t0, T, samp, kctr):
                    for ti in range(T // 128):
                        tile = t0 // 128 + ti
                        tsl = slice(ti * 128, (ti + 1) * 128)
                        gsl = slice(tile * 128, (tile + 1) * 128)
                        tp = 0
                        v_sb, sg, gs, khtok, attm, oss, orstd, og, ogT = TB[tp]
                        v_sbr, sgr, gsr, khtokr, attmr, ossr, orstdr, ogr, ogTr = TBR[tp]
                        osq, osqr = sg, sgr
                        for k in range(8):
                            op("pe", dfr(PE.matmul, v_ps[:], lhsT=xnT[:, k, gsl], rhs=Wi[:, k, :],
                                                       start=(k == 0), stop=(k == 7)),
                               R=[Wir, xnTr[tile]], W=[v_psr], mark=(k == 7))
                        for k in range(8):
                            op("pe", dfr(PE.matmul, g_ps[:], lhsT=xnT[:, k, gsl], rhs=Wg[:, k, :],
                                                       start=(k == 0), stop=(k == 7)),
                               R=[Wgr, xnTr[tile]], W=[g_psr], mark=(k == 7))
                        op("act", dfr(ACT.copy, out=v_sb[:], in_=v_ps[:]), R=[v_psr], W=[v_sbr])
                        op("act", dfr(ACT.activation, out=sg[:], in_=g_ps[:], func=AF.Sigmoid),
                           R=[g_psr], W=[sgr])
                        op("dve", dfr(DVE.tensor_tensor, out=gs[:], in0=g_ps[:], in1=sg[:], op=ALU.mult),
                           R=[g_psr, sgr], W=[gsr])
                        yield
                        for hl in range(4):
                            op("pe", dfr(PE.transpose, out=kT_ps[:, hl, :], in_=kh[st][:, hl, tsl],
                                                          identity=ident[:]),
                               R=[khr[st][hl], cr], W=[kT_psr], mark=(hl == 3))
                        op("act", dfr(ACT.copy, out=khtok[:], in_=kT_ps), R=[kT_psr], W=[khtokr])
                        for hl in range(4):
                            op("pe", dfr(PE.matmul, att_ps[:, hl, :], lhsT=kh[st][:, hl, tsl],
                                                       rhs=qt[st][:, hl, tsl], start=True, stop=True),
                               R=[khr[st][hl], qtr[st][hl]], W=[att_psr], mark=(hl == 3))
                        msk = mask_s if samp else mask_p
                        op("dve", dfr(DVE.tensor_tensor, out=attm[:], in0=att_ps,
                                                            in1=msk[:].unsqueeze(1).to_broadcast([128, 4, 128]),
                                                            op=ALU.mult), R=[att_psr, cr], W=[attmr])
                        vh = v_sb[:].rearrange("p (h e) -> p h e", h=4)
                        yield
                        if not samp:
                            k0 = kctr[0]
                            for c in range(2):
                                kk = k0 + c
                                sl64 = slice(64 * c, 64 * c + 64)
                                for hl in range(4):
                                    op("pe", dfr(PE.matmul, kvb[:, hl, :], lhsT=khtok[sl64, hl, :], rhs=vh[sl64, hl, :],
                                                               start=True, stop=True),
                                       R=[khtokr, v_sbr], W=kv_psr, mark=(hl == 3), pm=1, n=128)
                                for hl in range(4):
                                    dcol = dec[st][:, hl, 2 * ti + c:2 * ti + c + 1]
                                    if kk == 0:
                                        op("dve", dfr(DVE.tensor_copy, out=S[:, hl, :], in_=kvb[:, hl, :]),
                                           R=kv_psr, W=[Sr[hl]])
                                    else:
                                        op("dve", dfr(DVE.scalar_tensor_tensor,
                                            out=S[:, hl, :], in0=S[:, hl, :], scalar=dcol, in1=kvb[:, hl, :],
                                            op0=ALU.mult, op1=ALU.add),
                                           R=[Sr[hl], decr[st][hl]] + kv_psr, W=[Sr[hl]])
                                op("act", dfr(ACT.copy, out=snap[:, :, (kk + 1) % 4, :], in_=S[:, :, :]),
                                   R=Sr, W=[snapr[h_][(kk + 1) % 4] for h_ in range(4)])
                                yield
                            if tile == 15:
                                for hl in range(4):
                                    dma("sp", sp_out[l, g * 4 + hl, :, :], S[:, hl, :], R=[Sr[hl]])
                            for hl in range(4):
                                op("pe", dfr(PE.matmul, o_ps[:, hl, :], lhsT=attm[:, hl, :], rhs=vh[:, hl, :],
                                                           start=True, stop=False),
                                   R=[attmr, v_sbr], W=[o_psr])
                                for c in range(2):
                                    kk = k0 + c
                                    sl64 = slice(64 * c, 64 * c + 64)
                                    op("pe", dfr(PE.matmul,
                                        o_ps[sl64, hl, :], lhsT=qh[st][:, hl, ti * 128 + 64 * c:ti * 128 + 64 * c + 64],
                                        rhs=snap[:, hl, kk % 4, :], start=False, stop=True),
                                       R=[qhr[st][hl], snapr[hl][kk % 4]], W=[o_psr], pm=2, n=128)
                            for hl in range(4):
                                kctr[hl] += 2
                        else:
                            items = [(hl, hf) for hl in range(4) for hf in range(2)]

                            def s0_load(n):
                                hl_, hf_ = items[n]
                                dma("sp", S0[n % 2][:],
                                    st_in[l, 8 * hf_:8 * hf_ + 8, g * 4 + hl_, :, :].rearrange("j d e -> d j e"),
                                    W=[S0r[n % 2]])

                            s0_load(0)
                            for n, (hl, hf) in enumerate(items):
                                hg = g * 4 + hl
                                bi = n % 2
                                if n + 1 < len(items):
                                    s0_load(n + 1)
                                j0 = 8 * hf
                                op("act", dfr(ACT.copy, out=S0bf[:], in_=S0[bi][:]), R=[S0r[bi]], W=[S0bfr])
                                op("dve", dfr(DVE.tensor_tensor,
                                    out=Qexp[:].rearrange("p j (a u) -> p j a u", u=8),
                                    in0=qh[st][:, hl, 0:128].rearrange("p (a u) -> p a u", u=8).unsqueeze(1)
                                    .to_broadcast([128, 8, 16, 8]),
                                    in1=sel[:, j0:j0 + 8, :].unsqueeze(3).to_broadcast([128, 8, 16, 8]),
                                    op=ALU.mult), R=[qhr[st][hl], cr], W=[Qexpr])
                                op("dve", dfr(DVE.tensor_tensor,
                                    out=Vexp[:], in0=vh[:, hl, :].unsqueeze(1).to_broadcast([128, 8, 128]),
                                    in1=selT[:, j0:j0 + 8].unsqueeze(2).to_broadcast([128, 8, 128]), op=ALU.mult),
                                   R=[v_sbr, cr], W=[Vexpr])
                                if hf == 0:
                                    op("pe", dfr(PE.matmul, o_ps[:, hl, :], lhsT=attm[:, hl, :], rhs=vh[:, hl, :],
                                                               start=True, stop=False), R=[attmr, v_sbr], W=[o_psr])
                                for jj in range(8):
                                    op("pe", dfr(PE.matmul, o_ps[:, hl, :], lhsT=Qexp[:, jj, :], rhs=S0bf[:, jj, :],
                                                               start=False, stop=(hf == 1 and jj == 7)),
                                       R=[Qexpr, S0bfr], W=[o_psr])
                                for q4 in range(2):
                                    op("pe", dfr(PE.matmul, kv4_ps, lhsT=khtok[:, hl, :],
                                                               rhs=Vexp[:, 4 * q4:4 * q4 + 4, :], start=True, stop=True),
                                       R=[khtokr, Vexpr], W=kv_psr, mark=True)
                                    for jj in range(4):
                                        jl = 4 * q4 + jj
                                        op("dve", dfr(DVE.scalar_tensor_tensor,
                                            out=S0[bi][:, jl, :], in0=S0[bi][:, jl, :],
                                            scalar=dec[st][:, hl, j0 + jl:j0 + jl + 1],
                                            in1=kv4_ps[:, jj, :], op0=ALU.mult, op1=ALU.add),
                                           R=[S0r[bi], decr[st][hl]] + kv_psr, W=[S0r[bi]])
                                dma("sp", ss_out[l, j0:j0 + 8, hg, :, :].rearrange("j d e -> d j e"), S0[bi][:],
                                    R=[S0r[bi]])
                                yield
                        op("act", dfr(ACT.activation, out=osq[:], in_=banks[4][:], func=AF.Square),
                           R=[o_psr], W=[osqr])
                        op("dve", dfr(DVE.tensor_reduce, out=oss[:], in_=osq[:].rearrange("p (h e) -> p h e", h=4),
                                                            axis=AX.X, op=ALU.add), R=[osqr], W=[ossr])
                        op("dve", dfr(DVE.tensor_scalar, out=orstd[:], in0=oss[:], scalar1=1.0 / 128,
                                                            scalar2=EPS, op0=ALU.mult, op1=ALU.add),
                           R=[ossr], W=[orstdr])
                        op("pool", dfr(POOL.tensor_tensor, out=orstd[:], in0=orstd[:], in1=mhalf[:, 0:4],
                                                              op=ALU.pow), R=[orstdr, cr], W=[orstdr])
                        for hl in range(4):
                            op("dve", dfr(DVE.scalar_tensor_tensor,
                                out=og[:, hl * 128:(hl + 1) * 128], in0=o_ps[:, hl, :], scalar=orstd[:, hl:hl + 1],
                                in1=gs[:, hl * 128:(hl + 1) * 128], op0=ALU.mult, op1=ALU.mult),
                               R=[o_psr, orstdr, gsr], W=[ogr])
                        yield
                        for hl in range(4):
                            op("pe", dfr(PE.transpose, out=ogT_ps[:, hl, :], in_=og[:, hl * 128:(hl + 1) * 128],
                                                          identity=ident[:]),
                               R=[ogr, cr], W=[ogT_psr], mark=(hl == 3))
                        op("act", dfr(ACT.copy, out=ogT[:], in_=ogT_ps), R=[ogT_psr], W=[ogTr])
                        for half in range(2):
                            hs_ = slice(half * 512, (half + 1) * 512)
                            for hl in range(4):
                                op("pe", dfr(PE.matmul, y_ps[:], lhsT=ogT[:, hl, :], rhs=Wo[:, hl, hs_],
                                                           start=(hl == 0), stop=(hl == 3)),
                                   R=[ogTr, Wor], W=[y_psr], mark=(hl == 3))
                            op("dve", dfr(DVE.tensor_tensor, out=X[:, tile, hs_], in0=y_ps[:], in1=X[:, tile, hs_],
                                                                op=ALU.add),
                               R=[y_psr, Xr[tile]], W=[Xr[tile]])

                import os
                LV = int(os.environ.get("KDBG_LV", "99"))
                def drive(items):
                    items = [[g_, w_] for g_, w_ in items if g_ is not None]
                    while items:
                        for it in list(items):
                            for _ in range(it[1]):
                                try:
                                    next(it[0])
                                except StopIteration:
                                    items.remove(it)
                                    break

                def load_qf(l, g):
                    wv = w_in_a[l].rearrange("(k p) n -> p k n", p=128)
                    for typ, (Wt, Wr) in ((0, (Wq, Wqr)), (1, (Wf, Wfr))):
                        c0 = typ * 2048 + g * 512
                        dma("pool", Wt[:], wv[:, :, c0:c0 + 512], W=[Wr], key="Win%d" % typ)

                def load_igo(l, g):
                    wv = w_in_a[l].rearrange("(k p) n -> p k n", p=128)
                    for typ, (Wt, Wr) in ((2, (Wi, Wir)), (3, (Wg, Wgr))):
                        c0 = typ * 2048 + g * 512
                        dma("pool", Wt[:], wv[:, :, c0:c0 + 512], W=[Wr], key="Win%d" % typ)
                    wo = w_out_a[l][g * 512:(g + 1) * 512, :].rearrange("(h p) n -> p h n", p=128)
                    dma("pool", Wo[:], wo, W=[Wor], key="Wo")
                    op("pool", dfr(POOL.tensor_scalar, out=Wo[:], in0=Wo[:], scalar1=pT[:, 32 + l:33 + l],
                                                          scalar2=0.0, op0=ALU.mult, op1=ALU.add),
                       R=[Wor, pTr], W=[Wor])

                gidx = 0
                for l in range(n_a):
                    norm_phase(norm_a[l])
                    pending_b = None
                    for g in range(4):
                        kctr = [0, 0, 0, 0]
                        for ci, (t0, T, samp) in enumerate(chunks):
                            st = gidx % NSET
                            gidx += 1
                            if ci == 0:
                                load_qf(l, g)
                            drive([(stage_a(l, g, st, t0, T, samp), 1), (pending_b, 3)])
                            if ci == 0:
                                load_igo(l, g)
                                for hl0 in range(4):
                                    op("pool", dfr(POOL.memset, snap[:, hl0, 0, :], 0.0), W=[snapr[hl0][0]])
                            pending_b = stage_b(l, g, st, t0, T, samp, kctr)
                    drive([(pending_b, 1)])
                tk.barrier()

        hgrn_phase()


        TS = 256

        def swa_phase():
            with ExitStack() as ws:
                def hb(name, shape, dt, stack=ws):
                    return sb(name, shape, dt, stack)

                KT = hb("KT", [128, 4, TOK], BF16)
                KTr = [Res("KT%d" % i) for i in range(NT)]
                V = hb("V", [128, NT, 4, 64], BF16)
                Vr = [Res("V%d" % i) for i in range(NT)]
                KcT = hb("KcT", [128, 16, 4, 128], BF16)
                Vc = hb("Vc", [128, 16, 4, 64], BF16)
                KcTr, Vcr = Res("KcT"), Res("Vc")
                ones64 = hb("ones64", [128, 64], BF16)
                mask2 = hb("mask2", [128, 2, 128], BF16)
                maskc = hb("maskc", [128, 8], BF16)
                WA = hb("WA", [128, 8, 512], BF16)
                WB = hb("WB", [128, 8, 512], BF16)
                WO = hb("WO", [128, 4, 1024], BF16)
                WAr, WBr, WOr = Res("WA"), Res("WB"), Res("WO")
                esraw = hb("esraw", [128, 32], F32)
                esp = hb("esp", [128, 16], F32)
                esr = Res("es")

                op("pool", dfr(POOL.memset, ones64[:], 1.0), W=[cr])
                op("pool", dfr(POOL.memset, mask2[:], 1.0), W=[cr])
                op("pool", dfr(POOL.affine_select, out=mask2[:, 0, :], in_=mask2[:, 0, :], pattern=[[-1, 128]],
                                                      compare_op=ALU.is_ge, fill=0.0, base=0,
                                                      channel_multiplier=1), R=[cr], W=[cr])
                op("pool", dfr(POOL.affine_select, out=mask2[:, 1, :], in_=mask2[:, 1, :], pattern=[[1, 128]],
                                                      compare_op=ALU.is_ge, fill=0.0, base=0,
                                                      channel_multiplier=-1), R=[cr], W=[cr])
                op("pool", dfr(POOL.memset, maskc[:], 1.0), W=[cr])
                op("pool", dfr(POOL.affine_select, out=maskc[:], in_=maskc[:], pattern=[[-1, 8]],
                                                      compare_op=ALU.is_ge, fill=0.0, base=0,
                                                      channel_multiplier=1), R=[cr], W=[cr])

                tk.filler = lambda: dfr(PE.matmul, banks[5][:, 0:128], lhsT=ident[:], rhs=ident[:], start=True, stop=True)
                norm_phase(norm_kv)
                with ExitStack() as ks:
                    kcs = [hb("kc%d" % i, [128, 512], F32, ks) for i in range(2)]
                    kcbs = [hb("kcb%d" % i, [128, 4, 2, 64], BF16, ks) for i in range(2)]
                    kcrs = [Res("kc%d" % i) for i in range(2)]
                    kcbrs = [Res("kcb%d" % i) for i in range(2)]
                    kvf = hb("kvf", [128, 512], F32, ks)
                    kvfr = Res("kvf")
                    dma("pool", WA[:], w_kv.rearrange("(k p) n -> p k n", p=128), W=[WAr], key="wkv")
                    WBv = WB[:].rearrange("p k (g d h) -> p k g d h", g=4, d=2)
                    WAv = WA[:, :, 0:256].rearrange("p k (g h) -> p k g h", g=4)
                    for k in range(8):
                        op("pool", dfr(POOL.tensor_copy,
                            out=WBv[:, k], in_=WAv[:, k].unsqueeze(2).to_broadcast([128, 4, 2, 64])),
                           R=[WAr], W=[WBr])
                    kchunks = [(c * 512, 512) for c in range(4)] + [(2048, 128)]
                    import os
                    KV_ = int(os.environ.get("KDBG_KV", "99"))
                    for g in range(4 if KV_ >= 1 else 0):
                        for (t0, T) in kchunks:
                            tl = list(range(t0 // 128, (t0 + T) // 128))
                            for k in range(8):
                                op("pe", dfr(PE.matmul, banks[0][:, 0:T], lhsT=WB[:, k, g * 128:(g + 1) * 128],
                                                           rhs=xnT[:, k, t0:t0 + T], start=(k == 0), stop=(k == 7)),
                                   R=[WBr] + [xnTr[t] for t in tl], W=[bankr[0]], mark=(k == 7))
                            op("act", dfr(ACT.copy, out=KT[:, g, t0:t0 + T], in_=banks[0][:, 0:T]),
                               R=[bankr[0]], W=[KTr[t] for t in tl])
                    for tile in range(NT if KV_ >= 2 else 0):
                        gsl = slice(tile * 128, (tile + 1) * 128)
                        full = tile >= 15
                        c0 = 0 if full else 256
                        for k in range(8):
                            op("pe", dfr(PE.matmul, banks[1][:, c0:512], lhsT=xnT[:, k, gsl], rhs=WA[:, k, c0:512],
                                                       start=(k == 0), stop=(k == 7)),
                               R=[WAr, xnTr[tile]], W=[bankr[1]], mark=(k == 7))
                        op("dve", dfr(DVE.tensor_copy, out=V[:, tile, :, :].rearrange("p g h -> p (g h)"),
                                                          in_=banks[1][:, 256:512]), R=[bankr[1]], W=[Vr[tile]])
                        if full:
                            op("act", dfr(ACT.copy, out=kvf[:], in_=banks[1][:]), R=[bankr[1]], W=[kvfr])
                            if KV_ < 3:
                                pass
                            elif tile == 15:
                                dma("sp", kvp_out[:, :], kvf[:], R=[kvfr])
                            else:
                                for j in range(16):
                                    dma("sp", kvs_out[j, 120:128, :], kvf[8 * j:8 * j + 8, :], R=[kvfr])
                    if KV_ >= 4:
                        dma("sp", kvs_out[:, 0:120, :], cache[:, 8:128, :])
                    trb = banks[2][:].bitcast(BF16)[:, 0:512].rearrange("p (g w) -> p g w", g=4)
                    for j in range(16 if KV_ >= 5 else 0):
                        kc, kcb, kcr, kcbr = kcs[j % 2], kcbs[j % 2], kcrs[j % 2], kcbrs[j % 2]
                        dma("sp", kc[:], cache[j, :, :], W=[kcr])
                        op("dve", dfr(DVE.tensor_copy,
                            out=kcb[:], in_=kc[:, 0:256].rearrange("p (g h) -> p g h", g=4).unsqueeze(2)
                            .to_broadcast([128, 4, 2, 64])), R=[kcr], W=[kcbr])
                        op("pool", dfr(POOL.tensor_copy, out=Vc[:, j, :, :].rearrange("p g h -> p (g h)"),
                                                            in_=kc[:, 256:512]), R=[kcr], W=[Vcr])
                        for g in range(4):
                            op("pe", dfr(PE.transpose, out=trb[:, g, :],
                                                          in_=kcb[:, g, :, :].rearrange("p d h -> p (d h)"),
                                                          identity=ident[:]),
                               R=[kcbr, cr], W=[bankr[2]], mark=(g == 3))
                        op("act", dfr(ACT.copy, out=KcT[:, j, :, :], in_=trb), R=[bankr[2]], W=[KcTr])
                    tk.barrier()

                if n_b == 0:
                    tk.barrier()
                    return
                QTs = [hb("QT%d" % i, [128, 4, TS], BF16) for i in range(2)]
                gsTs = [hb("gsT%d" % i, [128, 4, TS], BF16) for i in range(2)]
                QTrs = [[Res("QT%d" % i) for i in range(4)] for _ in range(2)]
                gsTrs = [[Res("gsT%d" % i) for i in range(4)] for _ in range(2)]
                pTs = [[hb("pT%d_%d" % (q, i), [128, 2, 2, 128], BF16) for i in range(2)] for q in range(2)]
                pTrs = [[Res("pT%d_%d" % (q, i)) for i in range(2)] for q in range(2)]
                pc = [hb("pc%d" % i, [128, 16, 4, 8], BF16) for i in range(2)]
                pcr = [Res("pc%d" % i) for i in range(2)]
                t1 = hb("t1", [128, 4, 128], F32)
                t2 = hb("t2", [128, 4, 128], F32)
                t1r, t2r = Res("t1"), Res("t2")
                ogT = hb("ogTb", [128, 4, 128], BF16)
                ogTr = Res("ogTb")
                sTbs = [[banks[2], banks[3]], [banks[2], banks[3]]]
                sTrs = [[bankr[2], bankr[3]], [bankr[2], bankr[3]]]
                tk.filler = lambda: dfr(PE.matmul, banks[6][:, 0:128], lhsT=ident[:], rhs=ident[:], start=True, stop=True)
                oT_ps = banks[4][:].rearrange("p (r t) -> p r t", r=4)
                dn_ps = banks[5][:].rearrange("p (r t) -> p r t", r=4)
                oTr, dnr = bankr[4], bankr[5]
                y_ps, y_psr = banks[7], bankr[7]
                pctr = [0]
                swc = [0]

                def load_w(jl, g):
                    wv = w_in_b[jl].rearrange("(k p) n -> p k n", p=128)
                    dma("pool", WA[:], wv[:, :, g * 512:(g + 1) * 512], W=[WAr], key="bq")
                    dma("pool", WB[:], wv[:, :, 2048 + g * 512:2048 + (g + 1) * 512], W=[WBr], key="bg")
                    dma("pool", WO[:], w_out_b[jl][g * 512:(g + 1) * 512, :].rearrange("(r p) n -> p r n", p=128),
                        W=[WOr], key="bo")

                def sw_a(g, t0, T, cs_):
                    QT, gsT, QTr, gsTr = QTs[cs_], gsTs[cs_], QTrs[cs_], gsTrs[cs_]
                    tl = [xnTr[t] for t in range(t0 // 128, (t0 + T) // 128)]
                    for pr in range(4):
                        for k in range(8):
                            op("pe", dfr(PE.matmul, banks[0][:, 0:T], lhsT=WA[:, k, pr * 128:(pr + 1) * 128],
                                                       rhs=xnT[:, k, t0:t0 + T], start=(k == 0), stop=(k == 7)),
                               R=[WAr] + tl, W=[bankr[0]], mark=(k == 7), n=T)
                        op("act", dfr(ACT.activation, out=QT[:, pr, 0:T], in_=banks[0][:, 0:T], func=AF.Copy,
                                                         scale=0.125), R=[bankr[0]], W=[QTr[pr]], n=T)
                        for k in range(8):
                            op("pe", dfr(PE.matmul, banks[0][:, 256:256 + T], lhsT=WB[:, k, pr * 128:(pr + 1) * 128],
                                                       rhs=xnT[:, k, t0:t0 + T], start=(k == 0), stop=(k == 7)),
                               R=[WBr] + tl, W=[bankr[0]], mark=(k == 7), n=T)
                        op("act", dfr(ACT.activation, out=gsT[:, pr, 0:T], in_=banks[0][:, 256:256 + T], func=AF.Tanh,
                                                         scale=0.5), R=[bankr[0]], W=[gsTr[pr]], n=T)
                        op("dve", dfr(DVE.scalar_tensor_tensor, out=gsT[:, pr, 0:T], in0=gsT[:, pr, 0:T], scalar=1.0,
                                                                   in1=banks[0][:, 256:256 + T], op0=ALU.add,
                                                                   op1=ALU.mult),
                           R=[gsTr[pr], bankr[0]], W=[gsTr[pr]], n=T)

                def sw_b(g, t0, T, cs_):
                    QT, gsT, QTr, gsTr = QTs[cs_], gsTs[cs_], QTrs[cs_], gsTrs[cs_]
                    for ti in range(T // 128):
                        tile = t0 // 128 + ti
                        tsl = slice(ti * 128, (ti + 1) * 128)
                        samp = tile == 16
                        kts = [1] if (tile == 0 or samp) else [0, 1]
                        if samp:
                            sTb, sTr = sTbs[0], sTrs[0]
                            for par in range(2):
                                ps = slice(par * 64, par * 64 + 64)
                                scb = sTb[par][:].rearrange("p (j r t) -> p j r t", j=16, r=4)
                                for j in range(16):
                                    for pr in range(4):
                                        op("pe", dfr(PE.matmul, scb[:, j, pr, :], lhsT=KcT[ps, j, g, :],
                                                                   rhs=QT[ps, pr, 8 * j:8 * j + 8], start=True,
                                                                   stop=True),
                                           R=[KcTr, QTr[pr]], W=[sTr[par]], mark=(j == 15 and pr == 3), pm=1, n=64)
                            for par in range(2):
                                op("act", dfr(ACT.activation, out=pc[par][:].rearrange("p j r t -> p (j r t)"),
                                                                 in_=sTb[par][:], func=AF.Exp),
                                   R=[sTr[par]], W=[pcr[par]])
                                op("pool", dfr(POOL.tensor_tensor,
                                    out=pc[par][:].rearrange("p j r t -> p (j r) t"),
                                    in0=pc[par][:].rearrange("p j r t -> p (j r) t"),
                                    in1=maskc[:].unsqueeze(1).to_broadcast([128, 64, 8]), op=ALU.mult),
                                   R=[pcr[par], cr], W=[pcr[par]])
                        for pp in range(2):
                            sTb, sTr, pT, pTr = sTbs[pp], sTrs[pp], pTs[pp], pTrs[pp]
                            sTv = [sTb[par][:].rearrange("p (a k t) -> p a k t", a=2, k=2) for par in range(2)]
                            for a in range(2):
                                pr = 2 * pp + a
                                for par in range(2):
                                    ps = slice(par * 64, par * 64 + 64)
                                    for kt_i in kts:
                                        ktile = tile - 1 + kt_i
                                        op("pe", dfr(PE.matmul, sTv[par][:, a, kt_i, :],
                                                                   lhsT=KT[ps, g, ktile * 128:(ktile + 1) * 128],
                                                                   rhs=QT[ps, pr, tsl], start=True, stop=True),
                                           R=[KTr[ktile], QTr[pr]], W=[sTr[par]],
                                           mark=(a == 1 and kt_i == 1), pm=1, n=128)
                            k0 = kts[0]
                            for par in range(2):
                                op("act", dfr(ACT.activation, out=pT[par][:, :, k0:2, :], in_=sTv[par][:, :, k0:2, :],
                                                                 func=AF.Exp), R=[sTr[par]], W=[pTr[par]])
                                if samp:
                                    mk = mask_s[:].unsqueeze(1).to_broadcast([128, 2, 128])
                                    op("pool", dfr(POOL.tensor_tensor, out=pT[par][:, :, 1, :],
                                                                          in0=pT[par][:, :, 1, :], in1=mk, op=ALU.mult),
                                       R=[pTr[par], cr], W=[pTr[par]])
                                elif len(kts) == 2 and False:
                                    mk = mask2[:, :, :].unsqueeze(1).to_broadcast([128, 2, 2, 128])
                                    op("pool", dfr(POOL.tensor_tensor, out=pT[par][:], in0=pT[par][:], in1=mk,
                                                   op=ALU.mult), R=[pTr[par], cr], W=[pTr[par]])
                                else:
                                    for kt_i in kts:
                                        mk = mask2[:, kt_i, :].unsqueeze(1).to_broadcast([128, 2, 128])
                                        op("pool", dfr(POOL.tensor_tensor, out=pT[par][:, :, kt_i, :],
                                                                              in0=pT[par][:, :, kt_i, :], in1=mk,
                                                                              op=ALU.mult),
                                           R=[pTr[par], cr], W=[pTr[par]])
                            for a in range(2):
                                pr = 2 * pp + a
                                for par in range(2):
                                    ps = slice(par * 64, par * 64 + 64)
                                    for (dst, dres, use_v) in ((oT_ps, oTr, True), (dn_ps, dnr, False)):
                                        n_mm = len(kts) + (16 if samp else 0)
                                        cnt = 0
                                        for kt_i in kts:
                                            ktile = tile - 1 + kt_i
                                            cnt += 1
                                            lh = V[:, ktile, g, :] if use_v else ones64[:]
                                            op("pe", dfr(PE.matmul, dst[ps, pr, :], lhsT=lh,
                                                                       rhs=pT[par][:, a, kt_i, :],
                                                                       start=(cnt == 1), stop=(cnt == n_mm)),
                                               R=[Vr[ktile], pTr[par], cr], W=[dres], pm=2, n=128)
                                        if samp:
                                            for j in range(16):
                                                cnt += 1
                                                lh = Vc[:, j, g, :] if use_v else ones64[:]
                                                op("pe", dfr(PE.matmul, dst[ps, pr, 8 * j:8 * j + 8], lhsT=lh,
                                                                           rhs=pc[par][:, j, pr, :],
                                                                           start=False, stop=(cnt == n_mm)),
                                                   R=[Vcr, pcr[par], cr], W=[dres], pm=2, n=64)
                        esb_ = esp[:, g * 4:(g + 1) * 4].unsqueeze(2).to_broadcast([128, 4, 128])
                        op("dve", dfr(DVE.scalar_tensor_tensor, out=t1[:], in0=dn_ps, scalar=2.0, in1=esb_,
                                                                   op0=ALU.mult, op1=ALU.add),
                           R=[dnr, esr], W=[t1r], n=512)
                        op("dve", dfr(DVE.reciprocal, out=t1[:], in_=t1[:]), R=[t1r], W=[t1r])
                        op("dve", dfr(DVE.tensor_tensor, out=t2[:], in0=oT_ps, in1=t1[:], op=ALU.mult),
                           R=[oTr, t1r], W=[t2r])
                        op("pool", dfr(POOL.tensor_tensor, out=ogT[:], in0=t2[:], in1=gsT[:, :, tsl], op=ALU.mult),
                           R=[t2r] + gsTr, W=[ogTr])
                        for half in range(2):
                            hs_ = slice(half * 512, (half + 1) * 512)
                            for pr in range(4):
                                op("pe", dfr(PE.matmul, y_ps[:], lhsT=ogT[:, pr, :], rhs=WO[:, pr, hs_],
                                                           start=(pr == 0), stop=(pr == 3)),
                                   R=[ogTr, WOr], W=[y_psr], mark=(pr == 3))
                            op("dve", dfr(DVE.tensor_tensor, out=X[:, tile, hs_], in0=y_ps[:], in1=X[:, tile, hs_],
                                                                op=ALU.add),
                               R=[y_psr, Xr[tile]], W=[Xr[tile]])

                chunks = [(c * TS, TS) for c in range(2048 // TS)] + [(2048, 128)]
                for jl in range(n_b):
                    norm_phase(norm_b[jl])
                    dma("sp", esraw[:], sinks_b[jl].partition_broadcast(128), W=[esr])
                    op("act", dfr(ACT.activation, out=esraw[:], in_=esraw[:], func=AF.Exp), R=[esr], W=[esr])
                    op("dve", dfr(DVE.tensor_scalar, out=esraw[:], in0=esraw[:], scalar1=2.0, scalar2=0.0,
                                                        op0=ALU.mult, op1=ALU.add), R=[esr], W=[esr])
                    ev = esraw[:].rearrange("p (r a) -> p r a", a=2)
                    op("dve", dfr(DVE.tensor_copy, out=esp[0:64, :], in_=ev[0:64, :, 0]), R=[esr], W=[esr])
                    op("dve", dfr(DVE.tensor_copy, out=esp[64:128, :], in_=ev[64:128, :, 1]), R=[esr], W=[esr])
                    for g in range(4):
                        load_w(jl, g)
                        for (t0, T) in chunks:
                            sw_a(g, t0, T, swc[0] % 2)
                            sw_b(g, t0, T, swc[0] % 2)
                            swc[0] += 1
                tk.barrier()

        if n_a == 2:
            swa_phase()

        def final_norm():
            tk.filler = None
            with ExitStack() as fs:
                yt = [sb("yt%d" % i, [128, 1024], F32, fs) for i in range(2)]
                ytr = [Res("yt%d" % i) for i in range(2)]
                dma("sp", gB[:], norm_f.partition_broadcast(128), W=[gBr])
                for ti in range(NT):
                    op("act", dfr(ACT.activation, out=xtmp[ti % 2][:], in_=X[:, ti, :], func=AF.Square,
                                                     accum_out=ssq[:, ti:ti + 1]),
                       R=[Xr[ti]], W=[xtmpr[ti % 2], ssqr])
                op("dve", dfr(DVE.tensor_scalar, out=rstd[:], in0=ssq[:], scalar1=1.0 / 1024, scalar2=EPS,
                                                    op0=ALU.mult, op1=ALU.add), R=[ssqr], W=[rstdr])
                op("pool", dfr(POOL.tensor_tensor, out=rstd[:], in0=rstd[:], in1=mhalf[:, 0:NT], op=ALU.pow),
                   R=[rstdr, cr], W=[rstdr])
                ypv = y_p.rearrange("(n p) d -> p n d", p=128)
                for ti in range(NT):
                    b = ti % 2
                    op("dve", dfr(DVE.scalar_tensor_tensor, out=yt[b][:], in0=X[:, ti, :],
                                                               scalar=rstd[:, ti:ti + 1], in1=gB[:],
                                                               op0=ALU.mult, op1=ALU.mult),
                       R=[Xr[ti], rstdr, gBr], W=[ytr[b]])
                    if ti < 16:
                        dma("sp", ypv[:, ti, :], yt[b][:], R=[ytr[b]])
                    else:
                        dma("sp", y_s[:, :], yt[b][:], R=[ytr[b]])
                tk.barrier()

        if not dbg or n_b == 2:
            final_norm()

        if dbg:
            xd = xdbg.rearrange("(n p) d -> p n d", p=128)
            for ti in range(NT):
                dma("sp", xd[:, ti, :], X[:, ti, :], R=[Xr[ti]])
        tk.finish()
    return nc


def _shard_inputs(inp):
    maps = []
    shared = {k: np.ascontiguousarray(inp[k], dtype=np.float32) for k in
              ("w_in_a", "w_out_a", "norm_a", "onorm_a", "lower_bounds_a", "norm_kv", "w_kv", "w_in_b",
               "w_out_b", "norm_b", "sinks_b", "norm_f")}
    for c in range(NCORES):
        m = dict(shared)
        m["x_p"] = np.ascontiguousarray(inp["x_prompt"][c], dtype=np.float32)
        m["x_s"] = np.ascontiguousarray(inp["x_sample"][16 * c:16 * c + 16].reshape(128, 1024), dtype=np.float32)
        m["st_in"] = np.ascontiguousarray(inp["state_hgrn"][:, 16 * c:16 * c + 16], dtype=np.float32)
        m["cache"] = np.ascontiguousarray(inp["cache_kv_window"][16 * c:16 * c + 16].reshape(16, 128, 512),
                                          dtype=np.float32)
        maps.append(m)
    return maps


def kernel(**inputs):
    nc = build()
    maps = _shard_inputs(inputs)
    res = run_bass_kernel_spmd(nc, maps, core_ids=list(range(NCORES)))
    r = res.results
    y_prompt = np.stack([r[c]["y_p"] for c in range(NCORES)], axis=0)
    y_sample = np.concatenate([r[c]["y_s"].reshape(16, 8, 1024) for c in range(NCORES)], axis=0)
    sp = np.stack([r[c]["sp_out"] for c in range(NCORES)], axis=1)
    ss = np.concatenate([r[c]["ss_out"] for c in range(NCORES)], axis=1)
    kvp = np.stack([r[c]["kvp_out"].reshape(128, 2, 4, 64) for c in range(NCORES)], axis=0)
    kvs = np.concatenate([r[c]["kvs_out"].reshape(16, 128, 2, 4, 64) for c in range(NCORES)], axis=0)
    return (y_prompt, y_sample, sp, ss, kvp, kvs)
```
